# Optimizing a Trainium2 kernel written in Bass

```python
import numpy as np
import jax, jax.numpy as jnp
from jax import lax

D_MODEL = 2048
BATCH = 8
SEQ = 2048
DEPTH = 2
DEC_BATCH = 2
DEC_SEQ = 4096
PAST_LEN = 128

HEAD_DIM = 64
A_PATTERNS = ((128, 1), (512, 4), (2048, 16))
A_HEADS_PER_GROUP = 4
A_HEADS = A_HEADS_PER_GROUP * len(A_PATTERNS)
A_WIDTH = A_HEADS * HEAD_DIM
A_OUT = A_HEADS_PER_GROUP * HEAD_DIM
A_BLOCK = 64
B_WIDTH = 512
B_BLOCKS = 8
B_BLOCK_W = B_WIDTH // B_BLOCKS
CONV_W = 4
RG_C = 8.0
C_HEADS = 12
C_WIDTH = C_HEADS * HEAD_DIM
GRID_W = 64
NA_KH = 8
NA_KW = 16
D_FF = -(-8 * D_MODEL // (3 * 256)) * 256
D_IN = 3 * A_WIDTH + 2 * B_WIDTH + 3 * C_WIDTH
D_MIX_OUT = A_OUT + B_WIDTH + C_WIDTH
NORM_EPS = 1e-6

kernel_name = 'hybrid_dilated_rglru_natten_encoder'


def _rmsnorm(x, g):
    x32 = x.astype(jnp.float32)
    y = x32 * lax.rsqrt(jnp.mean(x32 * x32, axis=-1, keepdims=True) + NORM_EPS)
    return (y * g.astype(jnp.float32)).astype(x.dtype)


def _alibi_slopes(n):
    return 2.0 ** (-8.0 * jnp.arange(1, n + 1, dtype=jnp.float32) / n)


def _to_strided(x, d):
    B, T = x.shape[:2]
    rest = x.shape[2:]
    x = x.reshape((B, T // d, d) + rest)
    perm = (0, 2, 1) + tuple(range(3, x.ndim))
    return x.transpose(perm).reshape((B * d, T // d) + rest)


def _from_strided(x, B, d):
    N, n = x.shape[:2]
    rest = x.shape[2:]
    x = x.reshape((B, d, n) + rest)
    perm = (0, 2, 1) + tuple(range(3, x.ndim))
    return x.transpose(perm).reshape((B, n * d) + rest)


def _strided_window_attention(q, k, v, slopes, dil, half):
    N, n, H, dh = q.shape
    blk = A_BLOCK
    nb = -(-n // blk)
    pad = nb * blk - n
    qb = jnp.pad(q, ((0, 0), (0, pad), (0, 0), (0, 0))).reshape(N, nb, blk, H, dh)

    def key_blocks(t):
        tp = jnp.pad(t, ((0, 0), (blk, blk + pad), (0, 0), (0, 0))).reshape(N, nb + 2, blk, H, dh)
        return jnp.concatenate([tp[:, :-2], tp[:, 1:-1], tp[:, 2:]], axis=2)

    kb, vb = key_blocks(k), key_blocks(v)
    qpos = jnp.arange(nb)[:, None] * blk + jnp.arange(blk)[None, :]
    kpos = jnp.arange(nb)[:, None] * blk - blk + jnp.arange(3 * blk)[None, :]
    rel = jnp.abs(kpos[:, None, :] - qpos[:, :, None])
    valid = (rel <= half) & (kpos[:, None, :] >= 0) & (kpos[:, None, :] < n)
    s = jnp.einsum('nbqhd,nbkhd->nbhqk', qb, kb).astype(jnp.float32) * (HEAD_DIM ** -0.5)
    s = s - (slopes * dil)[:, None, None] * rel[:, None].astype(jnp.float32)
    s = jnp.where(valid[:, None], s, -jnp.inf)
    m = jnp.max(s, axis=-1, keepdims=True)
    p = jnp.exp(s - m)
    den = jnp.sum(p, axis=-1, keepdims=True)
    o = jnp.einsum('nbhqk,nbkhd->nbqhd', (p / den).astype(v.dtype), vb)
    lse = (m + jnp.log(den))[..., 0]
    o = o.reshape(N, nb * blk, H, dh)[:, :n]
    lse = lse.transpose(0, 1, 3, 2).reshape(N, nb * blk, H)[:, :n]
    return o, lse


def _mixer_dilated(q, k, v):
    B, T = q.shape[:2]
    slopes = _alibi_slopes(A_HEADS)
    outs, lses = [], []
    for g, (win, dil) in enumerate(A_PATTERNS):
        hs = slice(g * A_HEADS_PER_GROUP, (g + 1) * A_HEADS_PER_GROUP)
        half = (win // 2) // dil
        o, lse = _strided_window_attention(_to_strided(q[:, :, hs], dil), _to_strided(k[:, :, hs], dil),
                                           _to_strided(v[:, :, hs], dil), slopes[hs], dil, half)
        outs.append(_from_strided(o, B, dil))
        lses.append(_from_strided(lse, B, dil))
    w = jax.nn.softmax(jnp.stack(lses), axis=0)
    o = jnp.sum(w[..., None] * jnp.stack(outs).astype(jnp.float32), axis=0)
    return o.reshape(B, T, A_OUT).astype(q.dtype)


def _lin_combine(e1, e2):
    a1, b1 = e1
    a2, b2 = e2
    return a1 * a2, a2 * b1 + b2


def _rglru_scan(xc, wa, ba, wx, bx, lam, reverse):
    B, T, W = xc.shape
    xb = xc.reshape(B, T, B_BLOCKS, B_BLOCK_W)
    r = jax.nn.sigmoid((jnp.einsum('btnc,ncd->btnd', xb, wa).reshape(B, T, W) + ba).astype(jnp.float32))
    i = jax.nn.sigmoid((jnp.einsum('btnc,ncd->btnd', xb, wx).reshape(B, T, W) + bx).astype(jnp.float32))
    log_a = -RG_C * r * jax.nn.softplus(-lam.astype(jnp.float32))
    a = jnp.exp(log_a)
    b = jnp.sqrt(-jnp.expm1(2.0 * log_a)) * (i * xc.astype(jnp.float32))
    _, h = lax.associative_scan(_lin_combine, (a, b), reverse=reverse, axis=1)
    return h


def _mixer_rglru(gate_in, x_in, conv_w, conv_b, wa, ba, wx, bx, lam):
    T = x_in.shape[1]
    left = CONV_W // 2
    xp = jnp.pad(x_in, ((0, 0), (left, CONV_W - 1 - left), (0, 0)))
    xc = conv_b + conv_w[0] * xp[:, 0:T]
    for j in range(1, CONV_W):
        xc = xc + conv_w[j] * xp[:, j:j + T]
    h = (_rglru_scan(xc, wa[0], ba[0], wx[0], bx[0], lam[0], False)
         + _rglru_scan(xc, wa[1], ba[1], wx[1], bx[1], lam[1], True))
    return jax.nn.gelu(gate_in) * h.astype(gate_in.dtype)


def _na_col_tables():
    n_blk = GRID_W // NA_KW
    span = 2 * NA_KW
    c = np.arange(GRID_W).reshape(n_blk, NA_KW)
    cs = np.clip(c - NA_KW // 2, 0, GRID_W - NA_KW)
    starts = np.clip(np.arange(n_blk) * NA_KW - NA_KW // 2, 0, GRID_W - span)
    kcol = starts[:, None] + np.arange(span)[None, :]
    mask = (kcol[:, None, :] >= cs[:, :, None]) & (kcol[:, None, :] < cs[:, :, None] + NA_KW)
    off = np.clip(kcol[:, None, :] - c[:, :, None] + NA_KW - 1, 0, 2 * NA_KW - 2)
    return kcol, mask, off


def _mixer_neighbourhood(q, k, v, rpb):
    B, T, H, dh = q.shape
    rows = T // GRID_W
    kh = min(NA_KH, rows)
    kcol, cmask, coff = _na_col_tables()
    n_blk, span = kcol.shape
    mask = jnp.asarray(np.broadcast_to(cmask[:, :, None, :], (n_blk, NA_KW, kh, span))
                       .reshape(n_blk, 1, NA_KW, kh * span))
    qg = q.reshape(B, rows, GRID_W, H, dh)
    kg = k.reshape(B, rows, GRID_W, H, dh)
    vg = v.reshape(B, rows, GRID_W, H, dh)
    rpb32 = rpb.astype(jnp.float32)
    scale = HEAD_DIM ** -0.5

    def row_fn(r):
        rs = jnp.clip(r - kh // 2, 0, rows - kh)
        q_r = lax.dynamic_index_in_dim(qg, r, axis=1, keepdims=False).reshape(B, n_blk, NA_KW, H, dh)

        def gather(t):
            t_r = lax.dynamic_slice_in_dim(t, rs, kh, axis=1)[:, :, kcol]
            return t_r.transpose(0, 2, 1, 3, 4, 5).reshape(B, n_blk, kh * span, H, dh)

        k_r, v_r = gather(kg), gather(vg)
        s = jnp.einsum('bjqhd,bjkhd->bjhqk', q_r, k_r).astype(jnp.float32) * scale
        row_idx = rs + jnp.arange(kh) - r + NA_KH - 1
        bias = rpb32[:, row_idx][:, :, coff]
        bias = bias.transpose(2, 0, 3, 1, 4).reshape(n_blk, H, NA_KW, kh * span)
        s = jnp.where(mask, s + bias, -jnp.inf)
        p = jax.nn.softmax(s, axis=-1)
        o = jnp.einsum('bjhqk,bjkhd->bjqhd', p.astype(v.dtype), v_r)
        return o.reshape(B, GRID_W, H, dh)

    out = lax.map(row_fn, jnp.arange(rows))
    return out.transpose(1, 0, 2, 3, 4).reshape(B, T, H * dh)


def _encode(x, norm1_g, w_in, conv_w, conv_b, rg_wa, rg_ba, rg_wx, rg_bx, rg_lam, na_rpb,
            w_out, norm2_g, w_ffn_in, w_ffn_out, final_g):
    B, T, _ = x.shape
    splits = list(np.cumsum([A_WIDTH, A_WIDTH, A_WIDTH, B_WIDTH, B_WIDTH, C_WIDTH, C_WIDTH]))
    for l in range(DEPTH):
        xn = _rmsnorm(x, norm1_g[l])
        proj = jnp.einsum('btd,de->bte', xn, w_in[l])
        qa, ka, va, gb, xb, qc, kc, vc = jnp.split(proj, splits, axis=-1)
        heads_a = lambda t: t.reshape(B, T, A_HEADS, HEAD_DIM)
        heads_c = lambda t: t.reshape(B, T, C_HEADS, HEAD_DIM)
        o_a = _mixer_dilated(heads_a(qa), heads_a(ka), heads_a(va))
        o_b = _mixer_rglru(gb, xb, conv_w[l], conv_b[l], rg_wa[l], rg_ba[l], rg_wx[l], rg_bx[l], rg_lam[l])
        o_c = _mixer_neighbourhood(heads_c(qc), heads_c(kc), heads_c(vc), na_rpb[l])
        mix = jnp.concatenate([o_a, o_b, o_c], axis=-1)
        x = x + jnp.einsum('bte,ed->btd', mix, w_out[l])
        hn = _rmsnorm(x, norm2_g[l])
        gate, up = jnp.split(jnp.einsum('btd,df->btf', hn, w_ffn_in[l]), 2, axis=-1)
        x = x + jnp.einsum('btf,fd->btd', jax.nn.silu(gate) * up, w_ffn_out[l])
    return _rmsnorm(x, final_g)


def setup_inputs(seed: int = 0) -> dict:
    key = jax.random.key(seed)
    ks = jax.random.split(key, 20)
    f32 = jnp.float32
    nrm = lambda k, shape, s: jax.random.normal(k, shape, f32) * s
    u = jax.random.uniform(ks[10], (DEPTH, 2, B_WIDTH), f32, minval=0.9, maxval=0.999)
    a0 = u ** (1.0 / RG_C)
    return {
        'x_prompt': nrm(ks[0], (BATCH, SEQ, D_MODEL), 1.0),
        'x_sample': nrm(ks[1], (DEC_BATCH, DEC_SEQ, D_MODEL), 1.0),
        'norm1_g': 1.0 + nrm(ks[2], (DEPTH, D_MODEL), 0.02),
        'w_in': nrm(ks[3], (DEPTH, D_MODEL, D_IN), D_MODEL ** -0.5),
        'conv_w': nrm(ks[4], (DEPTH, CONV_W, B_WIDTH), CONV_W ** -0.5),
        'conv_b': nrm(ks[5], (DEPTH, B_WIDTH), 0.02),
        'rg_wa': nrm(ks[6], (DEPTH, 2, B_BLOCKS, B_BLOCK_W, B_BLOCK_W), B_BLOCK_W ** -0.5),
        'rg_ba': nrm(ks[7], (DEPTH, 2, B_WIDTH), 0.02),
        'rg_wx': nrm(ks[8], (DEPTH, 2, B_BLOCKS, B_BLOCK_W, B_BLOCK_W), B_BLOCK_W ** -0.5),
        'rg_bx': nrm(ks[9], (DEPTH, 2, B_WIDTH), 0.02),
        'rg_lam': jnp.log(a0) - jnp.log1p(-a0),
        'na_rpb': nrm(ks[11], (DEPTH, C_HEADS, 2 * NA_KH - 1, 2 * NA_KW - 1), 0.1),
        'w_out': nrm(ks[12], (DEPTH, D_MIX_OUT, D_MODEL), D_MIX_OUT ** -0.5),
        'norm2_g': 1.0 + nrm(ks[13], (DEPTH, D_MODEL), 0.02),
        'w_ffn_in': nrm(ks[14], (DEPTH, D_MODEL, 2 * D_FF), D_MODEL ** -0.5),
        'w_ffn_out': nrm(ks[15], (DEPTH, D_FF, D_MODEL), D_FF ** -0.5),
        'final_g': 1.0 + nrm(ks[16], (D_MODEL,), 0.02),
    }


def reference(x_prompt, x_sample, norm1_g, w_in, conv_w, conv_b, rg_wa, rg_ba, rg_wx, rg_bx,
              rg_lam, na_rpb, w_out, norm2_g, w_ffn_in, w_ffn_out, final_g):
    y_prompt = _encode(x_prompt, norm1_g, w_in, conv_w, conv_b, rg_wa, rg_ba, rg_wx, rg_bx, rg_lam,
                       na_rpb, w_out, norm2_g, w_ffn_in, w_ffn_out, final_g)
    y_sample = _encode(x_sample, norm1_g, w_in, conv_w, conv_b, rg_wa, rg_ba, rg_wx, rg_bx, rg_lam,
                       na_rpb, w_out, norm2_g, w_ffn_in, w_ffn_out, final_g)
    return (y_prompt, y_sample)
```

```python
import os
from contextlib import ExitStack
import numpy as np
import ml_dtypes
import concourse.bass as bass
import concourse.mybir as mybir
from concourse.bass_utils import run_bass_kernel_spmd

F32 = mybir.dt.float32
BF16 = mybir.dt.bfloat16
AF = mybir.ActivationFunctionType
ALU = mybir.AluOpType

L = 2
D = 2048
DIN = 5632
DFF = 5632
TOK = 4096
NT = 8
QA0, KA0, VA0, GB0, XB0, QC0, KC0, VC0 = 0, 768, 1536, 2304, 2816, 3328, 4096, 4864
NEG = -30000.0
EPS = 1e-6
NLAYERS = int(os.environ.get("MK_LAYERS", "2"))
DEBUG = int(os.environ.get("MK_DEBUG", "0"))


class Prog:
    ENG = ('pe', 'act', 'dve', 'pool', 'sp')
    CE = ('pe', 'act', 'dve', 'pool')

    def __init__(self, nc, es):
        self.nc = nc
        self.es = es
        self.streams = {e: [] for e in self.ENG}
        self.cnt = {e: 0 for e in self.ENG}
        self.sem = {e: es.enter_context(nc.semaphore("c_" + e)) for e in self.CE}
        self.dsem = {}
        self.dcnt = {}
        self.waited = {e: {} for e in self.ENG}
        self.lastw = {}
        self.readers = {}
        self.sim = {e: [] for e in self.ENG}
        self.simval = {}

    def _need(self, eng, tok, kind):
        src = tok[0]
        if src == eng and src == 'pe':
            return False
        return True

    def _deps(self, eng, reads, writes, dmakey=None):
        deps = []
        for b in reads:
            t = self.lastw.get(b)
            if t is not None and self._need(eng, t, 'raw'):
                deps.append(t)
        for b in writes:
            t = self.lastw.get(b)
            if t is not None:
                if not (dmakey is not None and t[1] == ('d', dmakey)) and self._need(eng, t, 'waw'):
                    deps.append(t)
            for t in self.readers.get(b, ()):
                if self._need(eng, t, 'war'):
                    deps.append(t)
        best = {}
        for (src, sk, val) in deps:
            if val > best.get(sk, 0):
                best[sk] = val
        out = []
        w = self.waited[eng]
        for sk, val in best.items():
            if w.get(sk, 0) >= val:
                continue
            w[sk] = val
            out.append((sk, val))
        return out

    def _semh(self, sk):
        return self.sem[sk[1]] if sk[0] == 'c' else self.dsem[sk[1]]

    def _record(self, tok, reads, writes):
        for b in reads:
            self.readers.setdefault(b, []).append(tok)
        for b in writes:
            self.lastw[b] = tok
            self.readers[b] = []

    def group(self, eng, fns, reads=(), writes=()):
        dl = self._deps(eng, reads, writes)
        self.sim[eng].append((dl, ('c', eng), 1))
        waits = [(self._semh(sk), v) for sk, v in dl]
        self.cnt[eng] += 1
        tok = (eng, ('c', eng), self.cnt[eng])
        sem = self.sem[eng]

        def emit(e, waits=waits, fns=fns, sem=sem):
            for s, v in waits:
                e.wait_ge(s, v)
            for f in fns[:-1]:
                f(e)
            fns[-1](e).then_inc(sem, 1)
        self.streams[eng].append(emit)
        self._record(tok, reads, writes)

    def op(self, eng, fn, reads=(), writes=()):
        self.group(eng, [fn], reads, writes)

    def dma(self, q, key, fn, reads=(), writes=()):
        if key not in self.dsem:
            self.dsem[key] = self.es.enter_context(self.nc.semaphore("d_%d" % len(self.dsem)))
            self.dcnt[key] = 0
        dl = self._deps(q, reads, writes, dmakey=key)
        self.sim[q].append((dl, ('d', key), 16))
        waits = [(self._semh(sk), v) for sk, v in dl]
        self.dcnt[key] += 16
        tok = ('dma', ('d', key), self.dcnt[key])
        sem = self.dsem[key]

        def emit(e, waits=waits, fn=fn, sem=sem):
            for s, v in waits:
                e.wait_ge(s, v)
            fn(e).then_inc(sem, 16)
        self.streams[q].append(emit)
        self._record(tok, reads, writes)

    def barrier(self):
        allw = [(('c', e), self.cnt[e]) for e in self.CE if self.cnt[e] > 0]
        allw += [(('d', k), v) for k, v in self.dcnt.items() if v > 0]
        for eng in self.ENG:
            w = self.waited[eng]
            ws = []
            dl = []
            for sk, v in allw:
                if w.get(sk, 0) >= v:
                    continue
                w[sk] = v
                ws.append((self._semh(sk), v))
                dl.append((sk, v))
            self.sim[eng].append((dl, None, 0))

            def emit(e, ws=ws):
                for s, v in ws:
                    e.wait_ge(s, v)
            self.streams[eng].append(emit)
        self.lastw = {}
        self.readers = {}

    def check_deadlock(self):
        pos = {e: 0 for e in self.ENG}
        val = self.simval
        prog = True
        while prog:
            prog = False
            for e in self.ENG:
                q = self.sim[e]
                while pos[e] < len(q):
                    dl, sk, inc = q[pos[e]]
                    if all(val.get(k, 0) >= v for k, v in dl):
                        if sk is not None:
                            val[sk] = val.get(sk, 0) + inc
                        pos[e] += 1
                        prog = True
                    else:
                        break
        stuck = {e: (pos[e], len(self.sim[e])) for e in self.ENG if pos[e] < len(self.sim[e])}
        if stuck:
            msg = []
            for e in stuck:
                dl, sk, inc = self.sim[e][pos[e]]
                msg.append((e, pos[e], [(k, v, val.get(k, 0)) for k, v in dl if val.get(k, 0) < v]))
            raise RuntimeError("DEADLOCK in program order: %r" % (msg,))
        self.sim = {e: [] for e in self.ENG}

    def emit_all(self):
        self.barrier()
        self.check_deadlock()
        nc = self.nc
        st = self.streams
        with nc.Block() as block:
            @block.tensor
            def _(e):
                for f in st['pe']:
                    f(e)

            @block.scalar
            def _(e):
                for f in st['act']:
                    f(e)

            @block.vector
            def _(e):
                for f in st['dve']:
                    f(e)

            @block.gpsimd
            def _(e):
                for f in st['pool']:
                    f(e)

            @block.sync
            def _(e):
                for f in st['sp']:
                    f(e)
        self.streams = {e: [] for e in self.ENG}


class Rot:
    def __init__(self, items):
        self.items = list(items)
        self.i = 0

    def next(self):
        v = self.items[self.i % len(self.items)]
        self.i += 1
        return v


def chunk_kind(cc):
    col = cc * 128
    if col < VA0:
        return ('fb', col)
    if col < GB0:
        return ('v', col - VA0)
    if col < QC0:
        return ('ff', col - GB0)
    if col < VC0:
        return ('fb', col)
    return ('v', 768 + col - VC0)


def build_program():
    nc = bass.Bass("TRN2", target_bir_lowering=False)

    def din(name, shape, dt=F32):
        return nc.dram_tensor(name, shape, dt, kind="ExternalInput").ap()

    def dscr(name, shape, dt):
        kind = "ExternalOutput" if DEBUG else "Internal"
        return nc.dram_tensor(name, shape, dt, kind=kind).ap()

    xin = din("xin", [TOK, D])
    w_in = din("w_in", [L, D, DIN])
    w_out = din("w_out", [L, 1536, D])
    w_f1 = din("w_ffn_in", [L, D, 2 * DFF])
    w_f2 = din("w_ffn_out", [L, DFF, D])
    g1row = din("g1row", [L, 128, D])
    g2row = din("g2row", [L, 128, D])
    gfrow = din("gfrow", [128, D])
    rgw = din("rgw", [L, 16, 128, 128])
    rgv = din("rgv", [128, L, 4, 11])
    flag = din("flag", [128, 1])
    eba = din("eba", [4, 128, 25, 128])
    rawc = din("rawc", [L, 12, 128, 7, 128])
    qaa = din("qaa", [64, TOK], BF16)
    qac = din("qac", [64, TOK], BF16)
    kaug = din("kaug", [64, TOK], BF16)
    identd = din("ident", [128, 128])
    yout = nc.dram_tensor("yout", [TOK, D], F32, kind="ExternalOutput").ap()

    sf = dscr("sf", [DIN, TOK], BF16)
    sgx = dscr("sgx", [1024, TOK], F32)
    sv = dscr("sv", [TOK, 1536], BF16)
    smix = dscr("smix", [1536, TOK], BF16)
    sx = dscr("sx", [TOK, D], F32)
    sx1 = dscr("sx1", [TOK, D], F32)
    shn = dscr("shn", [D, TOK], BF16)

    with ExitStack() as es0:
        P = Prog(nc, es0)
        ps = es0.enter_context(nc.psum_tensor("ps", [128, 8, 512], F32))
        ident = es0.enter_context(nc.sbuf_tensor("ident_sb", [128, 128], F32))
        flg = es0.enter_context(nc.sbuf_tensor("flg_sb", [128, 1], F32))
        P.dma('sp', 'ident', lambda e: e.dma_start(out=ident[:], in_=identd), writes=['ident'])
        P.dma('sp', 'flg', lambda e: e.dma_start(out=flg[:], in_=flag), writes=['flg'])
        P.emit_all()

        def PSK(b):
            return ('ps', b)

        def phase1(l, xsrc):
            with ExitStack() as es:
                def sb(name, shape, dt):
                    return es.enter_context(nc.sbuf_tensor("L%d_" % l + name, shape, dt))
                xtok = sb("p1_xtok", [128, 4, D], F32)
                xs = [sb("p1_xs%d" % i, [128, D], F32) for i in range(2)]
                grow = sb("p1_grow", [128, D], F32)
                junk = sb("p1_junk", [128, D], BF16)
                xnT = [sb("p1_xnT%d" % i, [128, 16, 1024], BF16) for i in range(2)]
                wb = [sb("p1_wb%d" % i, [128, 16, 512], BF16) for i in range(3)]
                stb = [sb("p1_stb%d" % i, [128, 512], BF16) for i in range(4)]
                stf = [sb("p1_stf%d" % i, [128, 512], F32) for i in range(2)]
                stat = sb("p1_stat", [128, 4, 4], F32)
                P.dma('sp', 'grow', lambda e: e.dma_start(out=grow[:], in_=g1row[l]), writes=['grow'])
                tb = Rot([0, 1])
                pb = Rot([2, 3, 4, 5, 6, 7])
                wr = Rot([0, 1, 2])
                sbr = Rot([0, 1, 2, 3])
                sfr = Rot([0, 1])
                evr = Rot(['act', 'dve'])
                w_l = w_in[l].rearrange("(k p) c -> p k c", p=128)

                def load_x(i, hf):
                    for s in range(4):
                        r0 = (i * 8 + hf * 4 + s) * 128
                        P.dma('sp', ('xtok', s), lambda e, s=s, r0=r0: e.dma_start(out=xtok[:, s, :], in_=xsrc[r0:r0 + 128, :]),
                              writes=[('xtok', s)])

                def evac(eng, out_ap, in_ap, reads, writes):
                    if eng == 'act':
                        P.op('act', lambda e: e.activation(out=out_ap, in_=in_ap, func=AF.Copy), reads, writes)
                    else:
                        P.op('dve', lambda e: e.tensor_copy(out=out_ap, in_=in_ap), reads, writes)

                def norm_T(i, hf):
                    slot = i % 2
                    for s in range(4):
                        xsl = xs[s % 2]
                        xk = ('xs', s % 2)
                        st = stat[:, s, :]
                        s8 = hf * 4 + s
                        P.op('act', lambda e, s=s, st=st: e.activation(out=junk[:], in_=xtok[:, s, :], func=AF.Square, accum_out=st[:, 0:1]),
                             reads=[('xtok', s)], writes=['junk', ('st', s, 0)])
                        P.op('dve', lambda e, st=st: e.tensor_scalar(out=st[:, 1:2], in0=st[:, 0:1], scalar1=1.0 / D, scalar2=EPS, op0=ALU.mult, op1=ALU.add),
                             reads=[('st', s, 0)], writes=[('st', s, 1)])
                        P.op('act', lambda e, st=st: e.activation(out=st[:, 2:3], in_=st[:, 1:2], func=AF.Sqrt),
                             reads=[('st', s, 1)], writes=[('st', s, 2)])
                        P.op('dve', lambda e, st=st: e.reciprocal(out=st[:, 3:4], in_=st[:, 2:3]),
                             reads=[('st', s, 2)], writes=[('st', s, 3)])
                        P.op('dve', lambda e, s=s, st=st, xsl=xsl: e.scalar_tensor_tensor(out=xsl[:], in0=xtok[:, s, :], scalar=st[:, 3:4], in1=grow[:], op0=ALU.mult, op1=ALU.mult),
                             reads=[('xtok', s), ('st', s, 3), 'grow'], writes=[xk])
                        for kg in range(4):
                            b = tb.next()
                            fns = []
                            for kk in range(4):
                                k = kg * 4 + kk
                                fns.append(lambda e, b=b, kk=kk, k=k, xsl=xsl: e.transpose(out=ps[:, b, kk * 128:(kk + 1) * 128], in_=xsl[:, k * 128:(k + 1) * 128], identity=ident[:]))
                            P.group('pe', fns, reads=[xk, 'ident'], writes=[PSK(b)])
                            out_ap = xnT[slot][:, kg * 4:(kg + 1) * 4, s8 * 128:(s8 + 1) * 128]
                            in_ap = ps[:, b, :].rearrange("p (a c) -> p a c", a=4)
                            evac(evr.next(), out_ap, in_ap, [PSK(b)], [('xnT', slot, s8)])

                def proj(i):
                    slot = i % 2
                    xk = [('xnT', slot, s) for s in range(8)]
                    for g in range(11):
                        if i + 1 < 4 and g == 3:
                            norm_T(i + 1, 0)
                            load_x(i + 1, 1)
                        if i + 1 < 4 and g == 7:
                            norm_T(i + 1, 1)
                            if i + 2 < 4:
                                load_x(i + 2, 0)
                        ws = wr.next()
                        wt = wb[ws]
                        P.dma('pool', ('wb', ws), lambda e, wt=wt, g=g: e.dma_start(out=wt[:], in_=w_l[:, :, g * 512:(g + 1) * 512]),
                              writes=[('wb', ws)])
                        kinds = [chunk_kind(g * 4 + c) for c in range(4)]
                        c = 0
                        while c < 4:
                            kd, off = kinds[c]
                            if kd == 'v':
                                n = 1
                                while c + n < 4 and kinds[c + n][0] == 'v':
                                    n += 1
                                ncol = n * 128
                                for s in range(8):
                                    b = pb.next()
                                    fns = [(lambda e, b=b, k=k, s=s, c=c, ncol=ncol, wt=wt: e.matmul(ps[:, b, 0:ncol], lhsT=xnT[slot][:, k, s * 128:(s + 1) * 128], rhs=wt[:, k, c * 128:c * 128 + ncol], start=(k == 0), stop=(k == 15))) for k in range(16)]
                                    P.group('pe', fns, reads=[xk[s], ('wb', ws)], writes=[PSK(b)])
                                    ss = sbr.next()
                                    evac(evr.next(), stb[ss][:, 0:ncol], ps[:, b, 0:ncol], [PSK(b)], [('stb', ss)])
                                    r0 = (i * 8 + s) * 128
                                    P.dma('sp', ('stb', ss), lambda e, ss=ss, r0=r0, off=off, ncol=ncol: e.dma_start(out=sv[r0:r0 + 128, off:off + ncol], in_=stb[ss][:, 0:ncol]),
                                          reads=[('stb', ss)])
                                c += n
                            else:
                                for hf in range(2):
                                    b = pb.next()
                                    t0_ = (i * 2 + hf) * 512
                                    fns = [(lambda e, b=b, k=k, c=c, wt=wt, hf=hf: e.matmul(ps[:, b, :], lhsT=wt[:, k, c * 128:(c + 1) * 128], rhs=xnT[slot][:, k, hf * 512:(hf + 1) * 512], start=(k == 0), stop=(k == 15))) for k in range(16)]
                                    P.group('pe', fns, reads=xk[hf * 4:hf * 4 + 4] + [('wb', ws)], writes=[PSK(b)])
                                    if kd == 'fb':
                                        ss = sbr.next()
                                        evac(evr.next(), stb[ss][:], ps[:, b, :], [PSK(b)], [('stb', ss)])
                                        P.dma('sp', ('stb', ss), lambda e, ss=ss, off=off, t0_=t0_: e.dma_start(out=sf[off:off + 128, t0_:t0_ + 512], in_=stb[ss][:]),
                                              reads=[('stb', ss)])
                                    else:
                                        ss = sfr.next()
                                        evac(evr.next(), stf[ss][:], ps[:, b, :], [PSK(b)], [('stf', ss)])
                                        P.dma('sp', ('stf', ss), lambda e, ss=ss, off=off, t0_=t0_: e.dma_start(out=sgx[off:off + 128, t0_:t0_ + 512], in_=stf[ss][:]),
                                              reads=[('stf', ss)])
                                c += 1

                load_x(0, 0)
                norm_T(0, 0)
                load_x(0, 1)
                norm_T(0, 1)
                load_x(1, 0)
                for i in range(4):
                    proj(i)
                P.emit_all()

        def c_tiles(R):
            lo, hi = 99, -99
            for kind in ('s', 'p'):
                for r in (2 * R, 2 * R + 1):
                    if kind == 's':
                        rs = min(max(r - 4, 0), 56)
                    else:
                        base = (r // 32) * 32
                        rs = base + min(max(r % 32 - 4, 0), 24)
                    lo = min(lo, rs // 2 - R)
                    hi = max(hi, (rs + 7) // 2 - R)
            return lo, hi

        def phase2(l):
            with ExitStack() as es:
                def sb(name, shape, dt):
                    return es.enter_context(nc.sbuf_tensor("L%d_" % l + name, shape, dt))
                wg = sb("rg_w", [128, 16, 128], BF16)
                vec = sb("rg_vec", [128, 4, 11], F32)
                dv = sb("rg_dv", [128, 4, 12], F32)
                qtr = sb("rg_qtr", [128, 1], F32)
                xbs = [sb("rg_xb%d" % i, [128, 2052], F32) for i in range(2)]
                xc = sb("rg_xc", [128, TOK], F32)
                xcb = sb("rg_xcb", [128, TOK], BF16)
                hf = sb("rg_hf", [128, TOK], F32)
                hbt = [sb("rg_hb%d" % i, [128, 512], F32) for i in range(2)]
                gbt = [sb("rg_gb%d" % i, [128, 512], F32) for i in range(2)]
                T = [sb("rg_t%d" % i, [128, 512], F32) for i in range(6)]
                C = [sb("rg_c%d" % i, [128, 512], F32) for i in range(3)]
                ob = [sb("rg_ob%d" % i, [128, 512], BF16) for i in range(2)]

                def rglru_gen():
                    P.dma('pool', 'rg_w', lambda e: e.dma_start(out=wg[:], in_=rgw[l].rearrange("m p n -> p m n")), writes=['rg_w'])
                    P.dma('sp', 'rg_vec', lambda e: e.dma_start(out=vec[:], in_=rgv[:, l, :, :]), writes=['rg_vec'])
                    P.op('pool', lambda e: e.tensor_scalar(out=dv[:, :, 0:4], in0=vec[:, :, 5:9], scalar1=0.5, scalar2=None, op0=ALU.mult), reads=['rg_vec'], writes=['dv_a'])
                    P.op('act', lambda e: e.activation(out=dv[:, :, 8:10], in_=vec[:, :, 9:11], func=AF.Exp, scale=-1.0), reads=['rg_vec'], writes=['dv_t'])
                    P.op('act', lambda e: e.activation(out=dv[:, :, 10:12], in_=dv[:, :, 8:10], func=AF.Ln, bias=1.0), reads=['dv_t'], writes=['dv_s'])
                    P.op('pool', lambda e: e.tensor_scalar(out=dv[:, :, 4:6], in0=dv[:, :, 10:12], scalar1=-4.0, scalar2=None, op0=ALU.mult), reads=['dv_s'], writes=['dv_c'])
                    P.op('pool', lambda e: e.tensor_scalar(out=dv[:, :, 6:8], in0=dv[:, :, 10:12], scalar1=-8.0, scalar2=None, op0=ALU.mult), reads=['dv_s'], writes=['dv_c2'])
                    P.op('pool', lambda e: e.memset(qtr[:], 0.25), writes=['half'])
                    DVK = ['dv_a', 'dv_c', 'dv_c2', 'rg_vec']
                    TK = lambda n: ('rgT', n)
                    CK = lambda n: ('rgC', n)
                    rgb = Rot([7])
                    obr = Rot([0, 1])
                    yield
                    for c in range(4):
                        for sgi in range(2):
                            P.op('pool', lambda e, sgi=sgi: e.memset(xbs[sgi][:, 0:2], 0.0), writes=[('xb', sgi)])
                            P.op('pool', lambda e, sgi=sgi: e.memset(xbs[sgi][:, 2050:2052], 0.0), writes=[('xb', sgi)])
                            P.dma('sp', ('xb', sgi), lambda e, sgi=sgi, c=c: e.dma_start(out=xbs[sgi][:, 2:2050], in_=sgx[512 + c * 128:512 + (c + 1) * 128, sgi * 2048:(sgi + 1) * 2048]),
                                  writes=[('xb', sgi)])
                        yield
                        P.op('pool', lambda e: e.tensor_scalar(out=xbs[0][:, 2050:2051], in0=xbs[1][:, 2:3], scalar1=flg[:, 0:1], scalar2=None, op0=ALU.mult),
                             reads=[('xb', 1), 'flg'], writes=[('xb', 0)])
                        P.op('pool', lambda e: e.tensor_scalar(out=xbs[1][:, 0:2], in0=xbs[0][:, 2048:2050], scalar1=flg[:, 0:1], scalar2=None, op0=ALU.mult),
                             reads=[('xb', 0), 'flg'], writes=[('xb', 1)])
                        for sgi in range(2):
                            xo = xc[:, sgi * 2048:(sgi + 1) * 2048]
                            P.op('pool', lambda e, sgi=sgi, xo=xo, c=c: e.tensor_scalar(out=xo, in0=xbs[sgi][:, 0:2048], scalar1=vec[:, c, 0:1], scalar2=vec[:, c, 4:5], op0=ALU.mult, op1=ALU.add),
                                 reads=[('xb', sgi), 'rg_vec'], writes=[('xc', sgi)])
                            for j in range(1, 4):
                                P.op('dve', lambda e, sgi=sgi, xo=xo, c=c, j=j: e.scalar_tensor_tensor(out=xo, in0=xbs[sgi][:, j:j + 2048], scalar=vec[:, c, j:j + 1], in1=xo, op0=ALU.mult, op1=ALU.add),
                                     reads=[('xb', sgi), 'rg_vec', ('xc', sgi)], writes=[('xc', sgi)])
                                yield
                            P.op('pool', lambda e, sgi=sgi, xo=xo: e.tensor_copy(out=xcb[:, sgi * 2048:(sgi + 1) * 2048], in_=xo),
                                 reads=[('xc', sgi)], writes=[('xcb', sgi)])
                            yield
                        for d in range(2):
                            for step in range(8):
                                t = step if d == 0 else 7 - step
                                sgi = t // 4
                                cols = slice(t * 512, (t + 1) * 512)
                                gs = step % 2
                                if d == 1:
                                    P.dma('sp', ('gbt', gs), lambda e, gs=gs, c=c, cols=cols: e.dma_start(out=gbt[gs][:], in_=sgx[c * 128:(c + 1) * 128, cols]), writes=[('gbt', gs)])
                                br = rgb.next()
                                bi = rgb.next()
                                P.op('pe', lambda e, br=br, d=d, c=c, cols=cols: e.matmul(ps[:, br, :], lhsT=wg[:, d * 8 + 0 * 4 + c, :], rhs=xcb[:, cols], start=True, stop=True),
                                     reads=['rg_w', ('xcb', sgi)], writes=[PSK(br)])
                                yield
                                P.op('act', lambda e, br=br, d=d, c=c: e.activation(out=T[0][:], in_=ps[:, br, :], func=AF.Tanh, scale=0.5, bias=dv[:, c, d:d + 1]),
                                     reads=[PSK(br)] + DVK, writes=[TK(0)])
                                yield
                                P.op('pe', lambda e, bi=bi, d=d, c=c, cols=cols: e.matmul(ps[:, bi, :], lhsT=wg[:, d * 8 + 1 * 4 + c, :], rhs=xcb[:, cols], start=True, stop=True),
                                     reads=['rg_w', ('xcb', sgi)], writes=[PSK(bi)])
                                yield
                                P.op('act', lambda e, bi=bi, d=d, c=c: e.activation(out=T[1][:], in_=ps[:, bi, :], func=AF.Tanh, scale=0.5, bias=dv[:, c, 2 + d:3 + d]),
                                     reads=[PSK(bi)] + DVK, writes=[TK(1)])
                                P.op('act', lambda e, d=d, c=c: e.activation(out=T[2][:], in_=T[0][:], func=AF.Exp, scale=dv[:, c, 4 + d:5 + d], bias=dv[:, c, 4 + d:5 + d]),
                                     reads=[TK(0)] + DVK, writes=[TK(2)])
                                P.op('act', lambda e, d=d, c=c: e.activation(out=T[3][:], in_=T[0][:], func=AF.Exp, scale=dv[:, c, 6 + d:7 + d], bias=dv[:, c, 6 + d:7 + d]),
                                     reads=[TK(0)] + DVK, writes=[TK(3)])
                                yield
                                P.op('dve', lambda e: e.tensor_scalar(out=T[3][:], in0=T[3][:], scalar1=-1.0, scalar2=-0.99999988, op0=ALU.mult, op1=ALU.max), reads=[TK(3)], writes=[TK(3)])
                                P.op('dve', lambda e, cols=cols: e.scalar_tensor_tensor(out=T[4][:], in0=T[1][:], scalar=1.0, in1=xc[:, cols], op0=ALU.add, op1=ALU.mult),
                                     reads=[TK(1), ('xc', sgi)], writes=[TK(4)])
                                yield
                                P.op('act', lambda e: e.activation(out=T[5][:], in_=T[3][:], func=AF.Sqrt, scale=0.25, bias=qtr[:, 0:1]), reads=[TK(3), 'half'], writes=[TK(5)])
                                P.op('pool', lambda e: e.tensor_tensor(out=T[4][:], in0=T[4][:], in1=T[5][:], op=ALU.mult), reads=[TK(4), TK(5)], writes=[TK(4)])
                                if d == 0 and t == 4:
                                    P.op('pool', lambda e: e.tensor_scalar(out=T[2][:, 0:1], in0=T[2][:, 0:1], scalar1=flg[:, 0:1], scalar2=None, op0=ALU.mult), reads=[TK(2), 'flg'], writes=[TK(2)])
                                if d == 1 and t == 3:
                                    P.op('pool', lambda e: e.tensor_scalar(out=T[2][:, 511:512], in0=T[2][:, 511:512], scalar1=flg[:, 0:1], scalar2=None, op0=ALU.mult), reads=[TK(2), 'flg'], writes=[TK(2)])
                                yield
                                if d == 0:
                                    init = 0.0 if t == 0 else hf[:, t * 512 - 1:t * 512]
                                    P.op('dve', lambda e, cols=cols, init=init: e.tensor_tensor_scan(out=hf[:, cols], data0=T[2][:], data1=T[4][:], initial=init, op0=ALU.mult, op1=ALU.add),
                                         reads=[TK(2), TK(4), 'hf'], writes=['hf'])
                                    yield
                                    continue
                                hs = step % 2
                                init = 0.0 if t == 7 else hbt[1 - hs][:, 0:1]
                                P.op('dve', lambda e, hs=hs, init=init: e.tensor_tensor_scan(out=hbt[hs][:, ::-1], data0=T[2][:, ::-1], data1=T[4][:, ::-1], initial=init, op0=ALU.mult, op1=ALU.add),
                                     reads=[TK(2), TK(4), ('hbt', 1 - hs)], writes=[('hbt', hs)])
                                yield
                                P.op('act', lambda e, gs=gs: e.activation(out=C[0][:], in_=gbt[gs][:], func=AF.Square), reads=[('gbt', gs)], writes=[CK(0)])
                                P.op('pool', lambda e: e.tensor_scalar(out=C[0][:], in0=C[0][:], scalar1=0.044715, scalar2=1.0, op0=ALU.mult, op1=ALU.add), reads=[CK(0)], writes=[CK(0)])
                                P.op('pool', lambda e, gs=gs: e.tensor_tensor(out=C[0][:], in0=C[0][:], in1=gbt[gs][:], op=ALU.mult), reads=[CK(0), ('gbt', gs)], writes=[CK(0)])
                                P.op('act', lambda e: e.activation(out=C[1][:], in_=C[0][:], func=AF.Tanh, scale=0.7978845608028654), reads=[CK(0)], writes=[CK(1)])
                                yield
                                P.op('pool', lambda e, hs=hs, cols=cols: e.tensor_tensor(out=C[2][:], in0=hf[:, cols], in1=hbt[hs][:], op=ALU.add), reads=['hf', ('hbt', hs)], writes=[CK(2)])
                                P.op('pool', lambda e: e.tensor_scalar(out=C[1][:], in0=C[1][:], scalar1=1.0, scalar2=0.5, op0=ALU.add, op1=ALU.mult), reads=[CK(1)], writes=[CK(1)])
                                P.op('pool', lambda e, gs=gs: e.tensor_tensor(out=C[1][:], in0=C[1][:], in1=gbt[gs][:], op=ALU.mult), reads=[CK(1), ('gbt', gs)], writes=[CK(1)])
                                oslot = obr.next()
                                P.op('pool', lambda e, oslot=oslot: e.tensor_tensor(out=ob[oslot][:], in0=C[1][:], in1=C[2][:], op=ALU.mult), reads=[CK(1), CK(2)], writes=[('ob', oslot)])
                                P.dma('sp', ('ob', oslot), lambda e, oslot=oslot, c=c, cols=cols: e.dma_start(out=smix[256 + c * 128:256 + (c + 1) * 128, cols], in_=ob[oslot][:]),
                                      reads=[('ob', oslot)])
                                yield

                NQK = 6
                qk = [sb("at_qk%d" % i, [128, TOK], BF16) for i in range(NQK)]
                vsl = [sb("at_v%d" % i, [128, 32, 65], BF16) for i in range(4)]
                ebA = [sb("at_ebA%d" % i, [128, 25, 128], BF16) for i in range(2)]
                ebraw = [sb("at_ebr%d" % i, [128, 7, 128], F32) for i in range(2)]
                ebC = [sb("at_ebC%d" % i, [128, 7, 128], BF16) for i in range(2)]
                E = [sb("at_E%d" % i, [128, 4, 128], BF16) for i in range(6)]
                PT = [sb("at_PT%d" % i, [128, 4, 128], BF16) for i in range(6)]
                otok = [sb("at_o%d" % i, [128, 64], F32) for i in range(3)]
                rec = [sb("at_r%d" % i, [128, 1], F32) for i in range(3)]
                mst = [sb("at_m%d" % i, [64, 512], BF16) for i in range(3)]
                for i in range(4):
                    P.op('pool', lambda e, i=i: e.memset(vsl[i][:, :, 64:65], 1.0), writes=[('v', i)])
                qkr = Rot(range(NQK))
                vr = Rot(range(4))
                sbank = Rot([0, 1, 2, 6])
                obank = Rot([3, 4])
                tbank = Rot([5])
                er = Rot(range(6))
                pr = Rot(range(6))
                orr = Rot(range(3))
                mr = Rot(range(3))
                ebAr = Rot([0, 1])
                ebCr = Rot([0, 1])
                svt = sv.rearrange("(t p) c -> p t c", p=128)

                def load_entry(qrow, krow, vcol, qaug):
                    qs = qkr.next()
                    ks = qkr.next()
                    vs = vr.next()
                    P.dma('sp', ('qk', qs), lambda e: e.dma_start(out=qk[qs][0:64, :], in_=sf[qrow:qrow + 64, :]), writes=[('qk', qs)])
                    P.dma('sp', ('qk', qs), lambda e: e.dma_start(out=qk[qs][64:128, :], in_=qaug), writes=[('qk', qs)])
                    P.dma('sp', ('qk', ks), lambda e: e.dma_start(out=qk[ks][0:64, :], in_=sf[krow:krow + 64, :]), writes=[('qk', ks)])
                    P.dma('sp', ('qk', ks), lambda e: e.dma_start(out=qk[ks][64:128, :], in_=kaug), writes=[('qk', ks)])
                    P.dma('sp', ('v', vs), lambda e: e.dma_start(out=vsl[vs][:, :, 0:64], in_=svt[:, :, vcol:vcol + 64]), writes=[('v', vs)])
                    return qs, ks, vs

                heads = [('A', j) for j in range(4)] + [('C', h) for h in range(12)]
                loaded = {}
                pending = {}

                def load_head(hd):
                    kind, idx = hd
                    if kind == 'A':
                        ents = []
                        for g in range(3):
                            h = 4 * g + idx
                            ents.append(load_entry(QA0 + h * 64, KA0 + h * 64, h * 64, qaa))
                        es_ = ebAr.next()
                        P.dma('pool', ('ebA', es_), lambda e: e.dma_start(out=ebA[es_][:], in_=eba[idx]), writes=[('ebA', es_)])
                        loaded[hd] = (ents, ebA[es_], ('ebA', es_))
                    else:
                        h = idx
                        es_ = ebCr.next()
                        P.dma('sp', ('ebr', es_), lambda e: e.dma_start(out=ebraw[es_][:], in_=rawc[l, h]), writes=[('ebr', es_)])
                        ents = [load_entry(QC0 + h * 64, KC0 + h * 64, 768 + h * 64, qac)]
                        pending[hd] = lambda: P.op('act', lambda e: e.activation(out=ebC[es_][:], in_=ebraw[es_][:], func=AF.Exp), reads=[('ebr', es_)], writes=[('ebC', es_)])
                        loaded[hd] = (ents, ebC[es_], ('ebC', es_))

                def head_tiles(hd, B):
                    kind, idx = hd
                    res = []
                    if kind == 'A':
                        base = 0
                        for g, rad in enumerate((1, 2, 8)):
                            for dl in range(-rad, rad + 1):
                                kt = B + dl
                                if 0 <= kt < 32:
                                    res.append((base + dl + rad, g, kt))
                            base += 2 * rad + 1
                    else:
                        lo, hi = c_tiles(B)
                        for dl in range(lo, hi + 1):
                            kt = B + dl
                            if 0 <= kt < 32:
                                res.append((dl + 3, 0, kt))
                    return res

                chunks = []
                for hi_, hd in enumerate(heads):
                    kind, idx = hd
                    mixrow = idx * 64 if kind == 'A' else 768 + idx * 64
                    for B in range(32):
                        tl = head_tiles(hd, B)
                        runs = []
                        cur = [tl[0]]
                        for tt in tl[1:]:
                            if tt[0] == cur[-1][0] + 1 and len(cur) < 4:
                                cur.append(tt)
                            else:
                                runs.append(cur)
                                cur = [tt]
                        runs.append(cur)
                        for ri, run in enumerate(runs):
                            chunks.append(dict(hd=hd, hi=hi_, B=B, run=run, first=(ri == 0), last=(ri == len(runs) - 1),
                                               mixrow=mixrow, headstart=(B == 0 and ri == 0)))

                state = {}

                def emit_qk(ch):
                    hd = ch['hd']
                    if ch['headstart']:
                        if hd not in loaded:
                            load_head(hd)
                        if hd in pending:
                            pending.pop(hd)()
                        nxt = ch['hi'] + 1
                        if hd[0] == 'C' and nxt < len(heads) and heads[nxt] not in loaded:
                            load_head(heads[nxt])
                    ents, ebt, ebk = loaded[hd]
                    b = sbank.next()
                    ch['sb'] = b
                    B = ch['B']
                    fns = []
                    rd = set()
                    for ti, (ebi, en, kt) in enumerate(ch['run']):
                        qs, ks, vs = ents[en]
                        rd.add(('qk', qs))
                        rd.add(('qk', ks))
                        fns.append(lambda e, b=b, ti=ti, qs=qs, ks=ks, kt=kt, B=B: e.matmul(ps[:, b, ti * 128:(ti + 1) * 128], lhsT=qk[ks][:, kt * 128:(kt + 1) * 128], rhs=qk[qs][:, B * 128:(B + 1) * 128], start=True, stop=True))
                    P.group('pe', fns, reads=list(rd), writes=[PSK(b)])

                binfo = {}

                def emit_exp(ch):
                    b = ch['sb']
                    n = len(ch['run'])
                    es_ = er.next()
                    ch['es'] = es_
                    P.op('act', lambda e: e.activation(out=E[es_][:, 0:n, :], in_=ps[:, b, 0:n * 128].rearrange("p (a c) -> p a c", a=n), func=AF.Exp, scale=0.125, bias=-8.0),
                         reads=[PSK(b)], writes=[('E', es_)])

                def emit_mul(ch):
                    ents, ebt, ebk = loaded[ch['hd']]
                    n = len(ch['run'])
                    eb0 = ch['run'][0][0]
                    es_ = ch['es']
                    ps_ = pr.next()
                    ch['pt'] = ps_
                    state['mulc'] = state.get('mulc', 0) + 1
                    meng = 'dve'
                    P.op(meng, lambda e: e.tensor_tensor(out=PT[ps_][:, 0:n, :], in0=E[es_][:, 0:n, :], in1=ebt[:, eb0:eb0 + n, :], op=ALU.mult),
                         reads=[('E', es_), ebk], writes=[('PT', ps_)])

                def emit_pv(ch):
                    ents, ebt, ebk = loaded[ch['hd']]
                    n = len(ch['run'])
                    ps_ = ch['pt']
                    key = (ch['hi'], ch['B'])
                    if ch['first']:
                        binfo[key] = {'ob': obank.next()}
                    ob_ = binfo[key]['ob']
                    fns = []
                    rd = {('PT', ps_)}
                    for ti, (ebi, en, kt) in enumerate(ch['run']):
                        qs, ks, vs = ents[en]
                        rd.add(('v', vs))
                        fns.append(lambda e, ti=ti, vs=vs, kt=kt, st=(ch['first'] and ti == 0), sp=(ch['last'] and ti == n - 1): e.matmul(ps[:, ob_, 0:65], lhsT=PT[ps_][:, ti, :], rhs=vsl[vs][:, kt, :], start=st, stop=sp))
                    P.group('pe', fns, reads=list(rd), writes=[PSK(ob_)])

                def emit_fin(ch):
                    bi_ = binfo[(ch['hi'], ch['B'])]
                    ob_ = bi_['ob']
                    os_ = orr.next()
                    bi_['os'] = os_
                    P.op('dve', lambda e: e.reciprocal(out=rec[os_][:], in_=ps[:, ob_, 64:65]), reads=[PSK(ob_)], writes=[('rec', os_)])
                    P.op('dve', lambda e: e.tensor_scalar(out=otok[os_][:], in0=ps[:, ob_, 0:64], scalar1=rec[os_][:, 0:1], scalar2=None, op0=ALU.mult),
                         reads=[PSK(ob_), ('rec', os_)], writes=[('otok', os_)])

                def emit_tr(ch):
                    bi_ = binfo[(ch['hi'], ch['B'])]
                    os_ = bi_['os']
                    B = ch['B']
                    if B % 4 == 0:
                        state['tb'] = tbank.next()
                    tb_ = state['tb']
                    bi_['tb'] = tb_
                    P.op('pe', lambda e: e.transpose(out=ps[0:64, tb_, (B % 4) * 128:(B % 4 + 1) * 128], in_=otok[os_][:], identity=ident[:]),
                         reads=[('otok', os_), 'ident'], writes=[PSK(tb_)])

                def emit_ev(ch):
                    bi_ = binfo[(ch['hi'], ch['B'])]
                    tb_ = bi_['tb']
                    B = ch['B']
                    ms_ = mr.next()
                    mixrow = ch['mixrow']
                    P.op('act', lambda e: e.activation(out=mst[ms_][:], in_=ps[0:64, tb_, :], func=AF.Copy), reads=[PSK(tb_)], writes=[('mst', ms_)])
                    P.dma('sp', ('mst', ms_), lambda e: e.dma_start(out=smix[mixrow:mixrow + 64, (B - 3) * 128:(B + 1) * 128], in_=mst[ms_][:]),
                          reads=[('mst', ms_)])

                rg = rglru_gen()
                next(rg)
                KRG = 3
                cnt = 0
                for hi_ in range(len(heads)):
                    hc = [c_ for c_ in chunks if c_['hi'] == hi_]
                    n_ = len(hc)
                    for step in range(n_ + 8):
                        if 0 <= step - 7 < n_ and hc[step - 7]['last'] and hc[step - 7]['B'] % 4 == 3:
                            emit_ev(hc[step - 7])
                        if 0 <= step - 6 < n_ and hc[step - 6]['last']:
                            emit_tr(hc[step - 6])
                        if 0 <= step - 5 < n_ and hc[step - 5]['last']:
                            emit_fin(hc[step - 5])
                        if 0 <= step - 3 < n_:
                            emit_pv(hc[step - 3])
                        if 0 <= step - 2 < n_:
                            emit_mul(hc[step - 2])
                        if 0 <= step - 1 < n_:
                            emit_exp(hc[step - 1])
                        if step < n_:
                            emit_qk(hc[step])
                        cnt += 1
                        if cnt % KRG == 0:
                            next(rg, None)
                for _ in rg:
                    pass
                P.emit_all()

        def phase3a(l, xsrc):
            with ExitStack() as es:
                def sb(name, shape, dt):
                    return es.enter_context(nc.sbuf_tensor("L%d_" % l + name, shape, dt))
                xtoks = [sb("p3_xtok%d" % i, [128, 4, D], F32) for i in range(2)]
                mixTs = [sb("p3_mixT%d" % i, [128, 12, 512], BF16) for i in range(2)]
                xss = [sb("p3_xs%d" % i, [128, D], F32) for i in range(2)]
                grow = sb("p3_grow", [128, D], F32)
                hst = [sb("p3_hst%d" % i, [128, 16, 512], BF16) for i in range(2)]
                wres = sb("p3_wo", [128, 12, D], BF16)
                stat = sb("p3_stat", [128, 4, 4], F32)
                P.dma('sp', 'grow', lambda e: e.dma_start(out=grow[:], in_=g2row[l]), writes=['grow'])
                pb = Rot([0, 1, 2, 3, 4, 5])
                tb = Rot([6, 7])
                evr = Rot(['act', 'dve'])
                wo_l = w_out[l].rearrange("(k p) c -> p k c", p=128)
                smx = smix.rearrange("(c p) t -> p c t", p=128)
                shn_v = shn.rearrange("(k p) t -> p k t", p=128)

                def load_in(i):
                    sl = i % 2
                    for s in range(4):
                        r0 = (i * 4 + s) * 128
                        P.dma('sp', ('xtok', sl, s), lambda e, s=s, r0=r0, sl=sl: e.dma_start(out=xtoks[sl][:, s, :], in_=xsrc[r0:r0 + 128, :]), writes=[('xtok', sl, s)])
                    P.dma('sp', ('mixT', sl), lambda e, i=i, sl=sl: e.dma_start(out=mixTs[sl][:], in_=smx[:, :, i * 512:(i + 1) * 512]), writes=[('mixT', sl)])

                def load_x(i, s):
                    sl = i % 2
                    r0 = (i * 4 + s) * 128
                    P.dma('sp', ('xtok', sl, s), lambda e, s=s, r0=r0, sl=sl: e.dma_start(out=xtoks[sl][:, s, :], in_=xsrc[r0:r0 + 128, :]), writes=[('xtok', sl, s)])

                def load_mix(i):
                    sl = i % 2
                    P.dma('sp', ('mixT', sl), lambda e, i=i, sl=sl: e.dma_start(out=mixTs[sl][:], in_=smx[:, :, i * 512:(i + 1) * 512]), writes=[('mixT', sl)])

                def wout(i, n):
                    sl = i % 2
                    xtok = xtoks[sl]
                    mixT = mixTs[sl]
                    for s in range(4):
                        b = pb.next()
                        fns = [(lambda e, b=b, k=k, s=s, n=n: e.matmul(ps[:, b, :], lhsT=mixT[:, k, s * 128:(s + 1) * 128], rhs=wres[:, k, n * 512:(n + 1) * 512], start=(k == 0), stop=(k == 11))) for k in range(12)]
                        P.group('pe', fns, reads=[('mixT', sl), 'wres'], writes=[PSK(b)])
                        P.op('dve', lambda e, b=b, s=s, n=n: e.tensor_tensor(out=xtok[:, s, n * 512:(n + 1) * 512], in0=ps[:, b, :], in1=xtok[:, s, n * 512:(n + 1) * 512], op=ALU.add),
                             reads=[PSK(b), ('xtok', sl, s)], writes=[('xtok', sl, s)])

                def chain(i, s):
                    sl = i % 2
                    xtok = xtoks[sl]
                    xsl = xss[s % 2]
                    xsk = ('xs', s % 2)
                    r0 = (i * 4 + s) * 128
                    xk = ('xtok', sl, s)
                    P.dma('sp', ('x1o', sl, s), lambda e, s=s, r0=r0: e.dma_start(out=sx1[r0:r0 + 128, :], in_=xtok[:, s, :]), reads=[xk])
                    st = stat[:, s, :]
                    P.op('act', lambda e, s=s, st=st: e.activation(out=xsl[:], in_=xtok[:, s, :], func=AF.Square, accum_out=st[:, 0:1]),
                         reads=[xk], writes=[xsk, ('st', s, 0)])
                    P.op('dve', lambda e, st=st: e.tensor_scalar(out=st[:, 1:2], in0=st[:, 0:1], scalar1=1.0 / D, scalar2=EPS, op0=ALU.mult, op1=ALU.add),
                         reads=[('st', s, 0)], writes=[('st', s, 1)])
                    P.op('act', lambda e, st=st: e.activation(out=st[:, 2:3], in_=st[:, 1:2], func=AF.Sqrt), reads=[('st', s, 1)], writes=[('st', s, 2)])
                    P.op('dve', lambda e, st=st: e.reciprocal(out=st[:, 3:4], in_=st[:, 2:3]), reads=[('st', s, 2)], writes=[('st', s, 3)])
                    P.op('dve', lambda e, s=s, st=st: e.scalar_tensor_tensor(out=xsl[:], in0=xtok[:, s, :], scalar=st[:, 3:4], in1=grow[:], op0=ALU.mult, op1=ALU.mult),
                         reads=[xk, ('st', s, 3), 'grow'], writes=[xsk])

                def transp(i, s):
                    sl = i % 2
                    xsl = xss[s % 2]
                    xsk = ('xs', s % 2)
                    for kg in range(4):
                        b = tb.next()
                        fns = [(lambda e, b=b, kk=kk, kg=kg: e.transpose(out=ps[:, b, kk * 128:(kk + 1) * 128], in_=xsl[:, (kg * 4 + kk) * 128:(kg * 4 + kk + 1) * 128], identity=ident[:])) for kk in range(4)]
                        P.group('pe', fns, reads=[xsk, 'ident'], writes=[PSK(b)])
                        out_ap = hst[sl][:, kg * 4:(kg + 1) * 4, s * 128:(s + 1) * 128]
                        in_ap = ps[:, b, :].rearrange("p (a c) -> p a c", a=4)
                        if evr.next() == 'act':
                            P.op('act', lambda e, out_ap=out_ap, in_ap=in_ap: e.activation(out=out_ap, in_=in_ap, func=AF.Copy), reads=[PSK(b)], writes=[('hst', sl)])
                        else:
                            P.op('dve', lambda e, out_ap=out_ap, in_ap=in_ap: e.tensor_copy(out=out_ap, in_=in_ap), reads=[PSK(b)], writes=[('hst', sl)])

                for n in range(4):
                    P.dma('pool', 'wres', lambda e, n=n: e.dma_start(out=wres[:, :, n * 512:(n + 1) * 512], in_=wo_l[:, :, n * 512:(n + 1) * 512]), writes=['wres'])
                for i0 in range(2):
                    for s in range(4):
                        load_x(i0, s)
                    load_mix(i0)
                for n in range(4):
                    wout(0, n)
                for i in range(NT):
                    for s in range(4):
                        chain(i, s)
                        if i + 2 < NT:
                            load_x(i + 2, s)
                            if s == 0:
                                load_mix(i + 2)
                        if i + 1 < NT:
                            wout(i + 1, s)
                        transp(i, s)
                    P.dma('sp', ('hst', i % 2), lambda e, i=i: e.dma_start(out=shn_v[:, :, i * 512:(i + 1) * 512], in_=hst[i % 2][:]), reads=[('hst', i % 2)])
                P.emit_all()

        def phase3b(l):
            with ExitStack() as es:
                def sb(name, shape, dt):
                    return es.enter_context(nc.sbuf_tensor("L%d_" % l + name, shape, dt))
                hnT = sb("f_hnT", [128, 16, 1024], BF16)
                hT = sb("f_hT", [128, 44, 1024], BF16)
                WR = 3
                wring = [sb("f_w%d" % i, [128, 5632], BF16) for i in range(WR)]
                sg = [sb("f_sg%d" % i, [128, 512], F32) for i in range(2)]
                yTs = [sb("f_yT%d" % i, [128, 4, 1024], F32) for i in range(2)]
                xp = [sb("f_xp%d" % i, [128, 512], F32) for i in range(4)]
                wr = Rot(range(WR))
                pb = Rot([0, 1, 2, 3, 4, 5, 6, 7])
                pbo = Rot([0, 1, 2, 3, 4, 5])
                tb = Rot([6, 7])
                sgr = Rot([0, 1])
                xpr = Rot(range(4))
                w1_l = w_f1[l].rearrange("(k p) c -> p k c", p=128)
                w2_l = w_f2[l].rearrange("(f p) c -> p f c", p=128)
                shn_v = shn.rearrange("(k p) t -> p k t", p=128)
                for j in range(4):
                    P.dma('sp', 'hnT', lambda e, j=j: e.dma_start(out=hnT[:], in_=shn_v[:, :, j * 1024:(j + 1) * 1024]), writes=['hnT'])
                    for f in range(44):
                        ws = wr.next()
                        wt = wring[ws][:, 0:4096].rearrange("p (k g c) -> p k g c", k=16, g=2)
                        P.dma('pool', ('w', ws), lambda e, wt=wt, f=f: e.dma_start(out=wt[:, :, 0, :], in_=w1_l[:, :, f * 128:(f + 1) * 128]), writes=[('w', ws)])
                        P.dma('pool', ('w', ws), lambda e, wt=wt, f=f: e.dma_start(out=wt[:, :, 1, :], in_=w1_l[:, :, DFF + f * 128:DFF + (f + 1) * 128]), writes=[('w', ws)])
                        for hf in range(2):
                            bg = pb.next()
                            bu = pb.next()
                            fns = [(lambda e, bg=bg, k=k, wt=wt, hf=hf: e.matmul(ps[:, bg, :], lhsT=wt[:, k, 0, :], rhs=hnT[:, k, hf * 512:(hf + 1) * 512], start=(k == 0), stop=(k == 15))) for k in range(16)]
                            P.group('pe', fns, reads=['hnT', ('w', ws)], writes=[PSK(bg)])
                            fns = [(lambda e, bu=bu, k=k, wt=wt, hf=hf: e.matmul(ps[:, bu, :], lhsT=wt[:, k, 1, :], rhs=hnT[:, k, hf * 512:(hf + 1) * 512], start=(k == 0), stop=(k == 15))) for k in range(16)]
                            P.group('pe', fns, reads=['hnT', ('w', ws)], writes=[PSK(bu)])
                            sgs = sgr.next()
                            P.op('act', lambda e, bg=bg, sgs=sgs: e.activation(out=sg[sgs][:], in_=ps[:, bg, :], func=AF.Silu), reads=[PSK(bg)], writes=[('sg', sgs)])
                            P.op('dve', lambda e, bu=bu, sgs=sgs, f=f, hf=hf: e.tensor_tensor(out=hT[:, f, hf * 512:(hf + 1) * 512], in0=sg[sgs][:], in1=ps[:, bu, :], op=ALU.mult),
                                 reads=[PSK(bu), ('sg', sgs)], writes=[('hT', f)])
                    HTK = [('hT', f) for f in range(44)]

                    def ffn_out_c(cg, cc):
                        c = cg * 4 + cc
                        yT = yTs[cg % 2]
                        ws = wr.next()
                        wt = wring[ws][:, 0:5632].rearrange("p (f c) -> p f c", f=44)
                        P.dma('pool', ('w', ws), lambda e, wt=wt, c=c: e.dma_start(out=wt, in_=w2_l[:, :, c * 128:(c + 1) * 128]), writes=[('w', ws)])
                        for hf in range(2):
                            b = pbo.next()
                            fns = [(lambda e, b=b, f=f, wt=wt, hf=hf: e.matmul(ps[:, b, :], lhsT=wt[:, f, :], rhs=hT[:, f, hf * 512:(hf + 1) * 512], start=(f == 0), stop=(f == 43))) for f in range(44)]
                            P.group('pe', fns, reads=HTK + [('w', ws)], writes=[PSK(b)])
                            P.op('act', lambda e, b=b, cc=cc, hf=hf, yT=yT: e.activation(out=yT[:, cc, hf * 512:(hf + 1) * 512], in_=ps[:, b, :], func=AF.Copy), reads=[PSK(b)], writes=[('yT', cg % 2, cc, hf)])

                    def tail(cg, j=j):
                        yT = yTs[cg % 2]

                        def xload(s):
                            r0 = (j * 8 + s) * 128
                            xs_ = s % 4
                            P.dma('sp', ('xp', xs_), lambda e, xs_=xs_, r0=r0, cg=cg: e.dma_start(out=xp[xs_][:], in_=sx1[r0:r0 + 128, cg * 512:(cg + 1) * 512]), writes=[('xp', xs_)])
                        for s in range(3):
                            xload(s)
                        for s in range(8):
                            r0 = (j * 8 + s) * 128
                            xs_ = s % 4
                            if s + 3 < 8:
                                xload(s + 3)
                            b = tb.next()
                            fns = [(lambda e, b=b, cc=cc, s=s, yT=yT: e.transpose(out=ps[:, b, cc * 128:(cc + 1) * 128], in_=yT[:, cc, s * 128:(s + 1) * 128], identity=ident[:])) for cc in range(4)]
                            P.group('pe', fns, reads=[('yT', cg % 2, cc, s // 4) for cc in range(4)] + ['ident'], writes=[PSK(b)])
                            P.op('dve', lambda e, b=b, xs_=xs_: e.tensor_tensor(out=xp[xs_][:], in0=ps[:, b, :], in1=xp[xs_][:], op=ALU.add),
                                 reads=[PSK(b), ('xp', xs_)], writes=[('xp', xs_)])
                            P.dma('sp', ('xp', xs_), lambda e, xs_=xs_, r0=r0, cg=cg: e.dma_start(out=sx[r0:r0 + 128, cg * 512:(cg + 1) * 512], in_=xp[xs_][:]), reads=[('xp', xs_)])

                    for cg in range(4):
                        for cc in range(4):
                            ffn_out_c(cg, cc)
                            if cc == 0 and cg > 0:
                                tail(cg - 1)
                    tail(3)
                P.emit_all()

        def phase3c():
            with ExitStack() as es:
                def sb(name, shape, dt):
                    return es.enter_context(nc.sbuf_tensor("fin_" + name, shape, dt))
                xb_ = [sb("x%d" % i, [128, D], F32) for i in range(8)]
                gf = sb("gf", [128, D], F32)
                junk = sb("junk", [128, D], BF16)
                stat = sb("stat", [128, 8, 4], F32)
                P.dma('sp', 'gf', lambda e: e.dma_start(out=gf[:], in_=gfrow), writes=['gf'])
                def fload(t):
                    sl = t % 8
                    r0 = t * 128
                    P.dma('sp', ('fx', sl), lambda e, sl=sl, r0=r0: e.dma_start(out=xb_[sl][:], in_=sx[r0:r0 + 128, :]), writes=[('fx', sl)])
                for t in range(6):
                    fload(t)
                for t in range(32):
                    sl = t % 8
                    r0 = t * 128
                    xk = ('fx', sl)
                    st = stat[:, sl, :]
                    if t + 6 < 32:
                        fload(t + 6)
                    P.op('act', lambda e, sl=sl, st=st: e.activation(out=junk[:], in_=xb_[sl][:], func=AF.Square, accum_out=st[:, 0:1]), reads=[xk], writes=['junk', ('st', sl, 0)])
                    P.op('dve', lambda e, st=st: e.tensor_scalar(out=st[:, 1:2], in0=st[:, 0:1], scalar1=1.0 / D, scalar2=EPS, op0=ALU.mult, op1=ALU.add), reads=[('st', sl, 0)], writes=[('st', sl, 1)])
                    P.op('act', lambda e, st=st: e.activation(out=st[:, 2:3], in_=st[:, 1:2], func=AF.Sqrt), reads=[('st', sl, 1)], writes=[('st', sl, 2)])
                    P.op('dve', lambda e, st=st: e.reciprocal(out=st[:, 3:4], in_=st[:, 2:3]), reads=[('st', sl, 2)], writes=[('st', sl, 3)])
                    P.op('dve', lambda e, sl=sl, st=st: e.scalar_tensor_tensor(out=xb_[sl][:], in0=xb_[sl][:], scalar=st[:, 3:4], in1=gf[:], op0=ALU.mult, op1=ALU.mult),
                         reads=[xk, ('st', sl, 3), 'gf'], writes=[xk])
                    P.dma('sp', xk, lambda e, sl=sl, r0=r0: e.dma_start(out=yout[r0:r0 + 128, :], in_=xb_[sl][:]), reads=[xk])
                P.emit_all()

        for l in range(NLAYERS):
            xsrc = xin if l == 0 else sx
            phase1(l, xsrc)
            phase2(l)
            phase3a(l, xsrc)
            phase3b(l)
        phase3c()
    return nc


def _bf16(a):
    return np.asarray(a, dtype=np.float32).astype(ml_dtypes.bfloat16)


def _const_tables():
    slopes = 2.0 ** (-8.0 * np.arange(1, 13, dtype=np.float64) / 12.0)
    p = np.arange(128)[:, None]
    q = np.arange(128)[None, :]
    eba = np.zeros((4, 128, 25, 128), np.float32)
    for j in range(4):
        base = 0
        for g, (d, rad) in enumerate(((1, 1), (4, 2), (16, 8))):
            h = 4 * g + j
            for dl in range(-rad, rad + 1):
                delta = 128 * dl + p - q
                ok = (delta % d == 0) & (np.abs(delta) <= 64 * d)
                val = np.exp(-slopes[h] * np.abs(delta))
                eba[j, :, base + dl + rad, :] = np.where(ok, val, 0.0)
            base += 2 * rad + 1
    rows = np.arange(TOK) // 64
    kaug = (rows[None, :] == np.arange(64)[:, None]).astype(np.float32)
    return eba, kaug


def _q_aug(is_sample):
    a = np.arange(64)[:, None]
    r = (np.arange(TOK) // 64)[None, :]
    if is_sample:
        rs = np.clip(r - 4, 0, 56)
        qaa = np.zeros((64, TOK), np.float32)
    else:
        base = (r // 32) * 32
        rs = base + np.clip(r % 32 - 4, 0, 24)
        qaa = np.where((a // 32) == (r // 32), 0.0, NEG).astype(np.float32)
    qac = np.where((a >= rs) & (a < rs + 8), 0.0, NEG).astype(np.float32)
    return qaa, qac


def _rawc(na_rpb):
    p = np.arange(128)
    kr2, kc = (p // 64)[:, None], (p % 64)[:, None]
    qr2, qc = (p // 64)[None, :], (p % 64)[None, :]
    cs = np.clip(qc - 8, 0, 48)
    colok = (kc >= cs) & (kc < cs + 16)
    dc = np.clip(kc - qc + 15, 0, 30)
    out = np.full((L, 12, 128, 7, 128), NEG, np.float32)
    for dl in range(-3, 4):
        dr = 2 * dl + kr2 - qr2 + 7
        ok = colok & (dr >= 0) & (dr < 15)
        drc = np.clip(dr, 0, 14)
        g = na_rpb[:, :, drc, dc]
        out[:, :, :, dl + 3, :] = np.where(ok[None, None], g, NEG)
    return out


_NC_CACHE = {}


def kernel(x_prompt, x_sample, norm1_g, w_in, conv_w, conv_b, rg_wa, rg_ba, rg_wx, rg_bx,
           rg_lam, na_rpb, w_out, norm2_g, w_ffn_in, w_ffn_out, final_g):
    f32 = lambda a: np.ascontiguousarray(np.asarray(a, dtype=np.float32))
    x_prompt, x_sample = f32(x_prompt), f32(x_sample)
    slots = [(x_sample[0], True), (x_sample[1], True)]
    for i in range(4):
        slots.append((x_prompt[2 * i:2 * i + 2].reshape(TOK, D), False))
    slots.append(slots[-1])
    slots.append(slots[-1])

    eba, kaug = _const_tables()
    rawc = _rawc(f32(na_rpb))
    qa = {True: _q_aug(True), False: _q_aug(False)}
    rgw = np.zeros((L, 2, 2, 4, 128, 128), np.float32)
    for kind, w in enumerate((f32(rg_wa), f32(rg_wx))):
        for c in range(4):
            for half in range(2):
                rgw[:, :, kind, c, half * 64:(half + 1) * 64, half * 64:(half + 1) * 64] = w[:, :, 2 * c + half]
    rgw = rgw.reshape(L, 16, 128, 128)
    rgv = np.zeros((128, L, 4, 11), np.float32)

    def chan(v):
        return v.reshape(L, 4, 128).transpose(2, 0, 1)
    cw = f32(conv_w)
    for j in range(4):
        rgv[:, :, :, j] = chan(cw[:, j])
    rgv[:, :, :, 4] = chan(f32(conv_b))
    for d in range(2):
        rgv[:, :, :, 5 + d] = chan(f32(rg_ba)[:, d])
        rgv[:, :, :, 7 + d] = chan(f32(rg_bx)[:, d])
        rgv[:, :, :, 9 + d] = chan(f32(rg_lam)[:, d])
    bc = lambda v: np.ascontiguousarray(np.broadcast_to(f32(v)[..., None, :], v.shape[:-1] + (128, D)))
    common = {
        "w_in": f32(w_in), "w_out": f32(w_out), "w_ffn_in": f32(w_ffn_in), "w_ffn_out": f32(w_ffn_out),
        "g1row": bc(norm1_g), "g2row": bc(norm2_g), "gfrow": bc(final_g),
        "rgw": rgw, "rgv": rgv, "eba": eba, "rawc": rawc, "kaug": _bf16(kaug),
        "ident": np.eye(128, dtype=np.float32),
    }
    in_maps = []
    for xs, is_s in slots:
        m = dict(common)
        m["xin"] = np.ascontiguousarray(xs)
        m["flag"] = np.full((128, 1), 1.0 if is_s else 0.0, np.float32)
        m["qaa"] = _bf16(qa[is_s][0])
        m["qac"] = _bf16(qa[is_s][1])
        in_maps.append(m)
    if "nc" not in _NC_CACHE:
        _NC_CACHE["nc"] = build_program()
    nc = _NC_CACHE["nc"]
    res = run_bass_kernel_spmd(nc, in_maps, core_ids=list(range(8)))
    outs = [np.asarray(r["yout"], dtype=np.float32) for r in res.results]
    if DEBUG:
        kernel.debug = res.results
    y_sample = np.stack([outs[0], outs[1]], axis=0)
    y_prompt = np.concatenate([outs[2 + i].reshape(2, 2048, D) for i in range(4)], axis=0)
    return (y_prompt, y_sample)
```

```python
import os
from contextlib import ExitStack
import numpy as np
import ml_dtypes
import concourse.bass as bass
import concourse.mybir as mybir
from concourse.bass_utils import run_bass_kernel_spmd

F32 = mybir.dt.float32
BF16 = mybir.dt.bfloat16
AF = mybir.ActivationFunctionType
ALU = mybir.AluOpType

L = 2
D = 2048
DIN = 5632
DFF = 5632
TOK = 4096
NT = 8
QA0, KA0, VA0, GB0, XB0, QC0, KC0, VC0 = 0, 768, 1536, 2304, 2816, 3328, 4096, 4864
NEG = -30000.0
EPS = 1e-6
NLAYERS = int(os.environ.get("MK_LAYERS", "2"))
DEBUG = int(os.environ.get("MK_DEBUG", "0"))


class Prog:
    ENG = ('pe', 'act', 'dve', 'pool', 'sp')
    CE = ('pe', 'act', 'dve', 'pool')

    def __init__(self, nc, es):
        self.nc = nc
        self.es = es
        self.streams = {e: [] for e in self.ENG}
        self.cnt = {e: 0 for e in self.ENG}
        self.sem = {e: es.enter_context(nc.semaphore("c_" + e)) for e in self.CE}
        self.dsem = {}
        self.dcnt = {}
        self.waited = {e: {} for e in self.ENG}
        self.lastw = {}
        self.readers = {}
        self.sim = {e: [] for e in self.ENG}
        self.simval = {}

    def _need(self, eng, tok, kind):
        src = tok[0]
        if src == eng and src == 'pe':
            return False
        return True

    def _deps(self, eng, reads, writes, dmakey=None):
        deps = []
        for b in reads:
            t = self.lastw.get(b)
            if t is not None and self._need(eng, t, 'raw'):
                deps.append(t)
        for b in writes:
            t = self.lastw.get(b)
            if t is not None:
                if not (dmakey is not None and t[1] == ('d', dmakey)) and self._need(eng, t, 'waw'):
                    deps.append(t)
            for t in self.readers.get(b, ()):
                if self._need(eng, t, 'war'):
                    deps.append(t)
        best = {}
        for (src, sk, val) in deps:
            if val > best.get(sk, 0):
                best[sk] = val
        out = []
        w = self.waited[eng]
        for sk, val in best.items():
            if w.get(sk, 0) >= val:
                continue
            w[sk] = val
            out.append((sk, val))
        return out

    def _semh(self, sk):
        return self.sem[sk[1]] if sk[0] == 'c' else self.dsem[sk[1]]

    def _record(self, tok, reads, writes):
        for b in reads:
            self.readers.setdefault(b, []).append(tok)
        for b in writes:
            self.lastw[b] = tok
            self.readers[b] = []

    def group(self, eng, fns, reads=(), writes=()):
        dl = self._deps(eng, reads, writes)
        self.sim[eng].append((dl, ('c', eng), 1))
        waits = [(self._semh(sk), v) for sk, v in dl]
        self.cnt[eng] += 1
        tok = (eng, ('c', eng), self.cnt[eng])
        sem = self.sem[eng]

        def emit(e, waits=waits, fns=fns, sem=sem):
            for s, v in waits:
                e.wait_ge(s, v)
            for f in fns[:-1]:
                f(e)
            fns[-1](e).then_inc(sem, 1)
        self.streams[eng].append(emit)
        self._record(tok, reads, writes)

    def op(self, eng, fn, reads=(), writes=()):
        self.group(eng, [fn], reads, writes)

    def dma(self, q, key, fn, reads=(), writes=()):
        if key not in self.dsem:
            self.dsem[key] = self.es.enter_context(self.nc.semaphore("d_%d" % len(self.dsem)))
            self.dcnt[key] = 0
        dl = self._deps(q, reads, writes, dmakey=key)
        self.sim[q].append((dl, ('d', key), 16))
        waits = [(self._semh(sk), v) for sk, v in dl]
        self.dcnt[key] += 16
        tok = ('dma', ('d', key), self.dcnt[key])
        sem = self.dsem[key]

        def emit(e, waits=waits, fn=fn, sem=sem):
            for s, v in waits:
                e.wait_ge(s, v)
            fn(e).then_inc(sem, 16)
        self.streams[q].append(emit)
        self._record(tok, reads, writes)

    def barrier(self):
        allw = [(('c', e), self.cnt[e]) for e in self.CE if self.cnt[e] > 0]
        allw += [(('d', k), v) for k, v in self.dcnt.items() if v > 0]
        for eng in self.ENG:
            w = self.waited[eng]
            ws = []
            dl = []
            for sk, v in allw:
                if w.get(sk, 0) >= v:
                    continue
                w[sk] = v
                ws.append((self._semh(sk), v))
                dl.append((sk, v))
            self.sim[eng].append((dl, None, 0))

            def emit(e, ws=ws):
                for s, v in ws:
                    e.wait_ge(s, v)
            self.streams[eng].append(emit)
        self.lastw = {}
        self.readers = {}

    def check_deadlock(self):
        pos = {e: 0 for e in self.ENG}
        val = self.simval
        prog = True
        while prog:
            prog = False
            for e in self.ENG:
                q = self.sim[e]
                while pos[e] < len(q):
                    dl, sk, inc = q[pos[e]]
                    if all(val.get(k, 0) >= v for k, v in dl):
                        if sk is not None:
                            val[sk] = val.get(sk, 0) + inc
                        pos[e] += 1
                        prog = True
                    else:
                        break
        stuck = {e: (pos[e], len(self.sim[e])) for e in self.ENG if pos[e] < len(self.sim[e])}
        if stuck:
            msg = []
            for e in stuck:
                dl, sk, inc = self.sim[e][pos[e]]
                msg.append((e, pos[e], [(k, v, val.get(k, 0)) for k, v in dl if val.get(k, 0) < v]))
            raise RuntimeError("DEADLOCK in program order: %r" % (msg,))
        self.sim = {e: [] for e in self.ENG}

    def emit_all(self):
        self.barrier()
        self.check_deadlock()
        nc = self.nc
        st = self.streams
        with nc.Block() as block:
            @block.tensor
            def _(e):
                for f in st['pe']:
                    f(e)

            @block.scalar
            def _(e):
                for f in st['act']:
                    f(e)

            @block.vector
            def _(e):
                for f in st['dve']:
                    f(e)

            @block.gpsimd
            def _(e):
                for f in st['pool']:
                    f(e)

            @block.sync
            def _(e):
                for f in st['sp']:
                    f(e)
        self.streams = {e: [] for e in self.ENG}


class Rot:
    def __init__(self, items):
        self.items = list(items)
        self.i = 0

    def next(self):
        v = self.items[self.i % len(self.items)]
        self.i += 1
        return v


def chunk_kind(cc):
    col = cc * 128
    if col < VA0:
        return ('fb', col)
    if col < GB0:
        return ('v', col - VA0)
    if col < QC0:
        return ('ff', col - GB0)
    if col < VC0:
        return ('fb', col)
    return ('v', 768 + col - VC0)


def build_program():
    nc = bass.Bass("TRN2", target_bir_lowering=False)

    def din(name, shape, dt=F32):
        return nc.dram_tensor(name, shape, dt, kind="ExternalInput").ap()

    def dscr(name, shape, dt):
        kind = "ExternalOutput" if DEBUG else "Internal"
        return nc.dram_tensor(name, shape, dt, kind=kind).ap()

    xin = din("xin", [TOK, D])
    w_in = din("w_in", [L, D, DIN])
    w_out = din("w_out", [L, 1536, D])
    w_f1 = din("w_ffn_in", [L, D, 2 * DFF])
    w_f2 = din("w_ffn_out", [L, DFF, D])
    g1row = din("g1row", [L, 128, D])
    g2row = din("g2row", [L, 128, D])
    gfrow = din("gfrow", [128, D])
    rgw = din("rgw", [L, 16, 128, 128])
    rgv = din("rgv", [128, L, 4, 11])
    flag = din("flag", [128, 1])
    eba = din("eba", [4, 128, 25, 128])
    rawc = din("rawc", [L, 12, 128, 7, 128])
    qaa = din("qaa", [64, TOK], BF16)
    qac = din("qac", [64, TOK], BF16)
    kaug = din("kaug", [64, TOK], BF16)
    identd = din("ident", [128, 128])
    yout = nc.dram_tensor("yout", [TOK, D], F32, kind="ExternalOutput").ap()

    sf = dscr("sf", [DIN, TOK], BF16)
    sgx = dscr("sgx", [1024, TOK], F32)
    sv = dscr("sv", [TOK, 1536], BF16)
    smix = dscr("smix", [1536, TOK], BF16)
    sx = dscr("sx", [TOK, D], F32)
    sx1 = dscr("sx1", [TOK, D], F32)
    shn = dscr("shn", [D, TOK], BF16)

    with ExitStack() as es0:
        P = Prog(nc, es0)
        ps = es0.enter_context(nc.psum_tensor("ps", [128, 8, 512], F32))
        ident = es0.enter_context(nc.sbuf_tensor("ident_sb", [128, 128], F32))
        flg = es0.enter_context(nc.sbuf_tensor("flg_sb", [128, 1], F32))
        P.dma('sp', 'ident', lambda e: e.dma_start(out=ident[:], in_=identd), writes=['ident'])
        P.dma('sp', 'flg', lambda e: e.dma_start(out=flg[:], in_=flag), writes=['flg'])
        P.emit_all()

        def PSK(b):
            return ('ps', b)

        def phase1(l, xsrc):
            with ExitStack() as es:
                def sb(name, shape, dt):
                    return es.enter_context(nc.sbuf_tensor("L%d_" % l + name, shape, dt))
                xtok = sb("p1_xtok", [128, 4, D], F32)
                xs = [sb("p1_xs%d" % i, [128, D], F32) for i in range(2)]
                grow = sb("p1_grow", [128, D], F32)
                junk = sb("p1_junk", [128, D], BF16)
                xnT = [sb("p1_xnT%d" % i, [128, 16, 1024], BF16) for i in range(2)]
                wb = [sb("p1_wb%d" % i, [128, 16, 512], BF16) for i in range(3)]
                stb = [sb("p1_stb%d" % i, [128, 512], BF16) for i in range(4)]
                stf = [sb("p1_stf%d" % i, [128, 512], F32) for i in range(2)]
                stat = sb("p1_stat", [128, 4, 4], F32)
                P.dma('sp', 'grow', lambda e: e.dma_start(out=grow[:], in_=g1row[l]), writes=['grow'])
                tb = Rot([0, 1])
                pb = Rot([2, 3, 4, 5, 6, 7])
                wr = Rot([0, 1, 2])
                sbr = Rot([0, 1, 2, 3])
                sfr = Rot([0, 1])
                evr = Rot(['act', 'dve'])
                w_l = w_in[l].rearrange("(k p) c -> p k c", p=128)

                def load_x(i, hf):
                    for s in range(4):
                        r0 = (i * 8 + hf * 4 + s) * 128
                        P.dma('sp', ('xtok', s), lambda e, s=s, r0=r0: e.dma_start(out=xtok[:, s, :], in_=xsrc[r0:r0 + 128, :]),
                              writes=[('xtok', s)])

                def evac(eng, out_ap, in_ap, reads, writes):
                    if eng == 'act':
                        P.op('act', lambda e: e.activation(out=out_ap, in_=in_ap, func=AF.Copy), reads, writes)
                    else:
                        P.op('dve', lambda e: e.tensor_copy(out=out_ap, in_=in_ap), reads, writes)

                def norm_T(i, hf):
                    slot = i % 2
                    for s in range(4):
                        xsl = xs[s % 2]
                        xk = ('xs', s % 2)
                        st = stat[:, s, :]
                        s8 = hf * 4 + s
                        P.op('act', lambda e, s=s, st=st: e.activation(out=junk[:], in_=xtok[:, s, :], func=AF.Square, accum_out=st[:, 0:1]),
                             reads=[('xtok', s)], writes=['junk', ('st', s, 0)])
                        P.op('dve', lambda e, st=st: e.tensor_scalar(out=st[:, 1:2], in0=st[:, 0:1], scalar1=1.0 / D, scalar2=EPS, op0=ALU.mult, op1=ALU.add),
                             reads=[('st', s, 0)], writes=[('st', s, 1)])
                        P.op('act', lambda e, st=st: e.activation(out=st[:, 2:3], in_=st[:, 1:2], func=AF.Sqrt),
                             reads=[('st', s, 1)], writes=[('st', s, 2)])
                        P.op('dve', lambda e, st=st: e.reciprocal(out=st[:, 3:4], in_=st[:, 2:3]),
                             reads=[('st', s, 2)], writes=[('st', s, 3)])
                        P.op('dve', lambda e, s=s, st=st, xsl=xsl: e.scalar_tensor_tensor(out=xsl[:], in0=xtok[:, s, :], scalar=st[:, 3:4], in1=grow[:], op0=ALU.mult, op1=ALU.mult),
                             reads=[('xtok', s), ('st', s, 3), 'grow'], writes=[xk])
                        for kg in range(4):
                            b = tb.next()
                            fns = []
                            for kk in range(4):
                                k = kg * 4 + kk
                                fns.append(lambda e, b=b, kk=kk, k=k, xsl=xsl: e.transpose(out=ps[:, b, kk * 128:(kk + 1) * 128], in_=xsl[:, k * 128:(k + 1) * 128], identity=ident[:]))
                            P.group('pe', fns, reads=[xk, 'ident'], writes=[PSK(b)])
                            out_ap = xnT[slot][:, kg * 4:(kg + 1) * 4, s8 * 128:(s8 + 1) * 128]
                            in_ap = ps[:, b, :].rearrange("p (a c) -> p a c", a=4)
                            evac(evr.next(), out_ap, in_ap, [PSK(b)], [('xnT', slot, s8)])

                def proj(i):
                    slot = i % 2
                    xk = [('xnT', slot, s) for s in range(8)]
                    for g in range(11):
                        if i + 1 < 4 and g == 3:
                            norm_T(i + 1, 0)
                            load_x(i + 1, 1)
                        if i + 1 < 4 and g == 7:
                            norm_T(i + 1, 1)
                            if i + 2 < 4:
                                load_x(i + 2, 0)
                        ws = wr.next()
                        wt = wb[ws]
                        P.dma('pool', ('wb', ws), lambda e, wt=wt, g=g: e.dma_start(out=wt[:], in_=w_l[:, :, g * 512:(g + 1) * 512]),
                              writes=[('wb', ws)])
                        kinds = [chunk_kind(g * 4 + c) for c in range(4)]
                        c = 0
                        while c < 4:
                            kd, off = kinds[c]
                            if kd == 'v':
                                n = 1
                                while c + n < 4 and kinds[c + n][0] == 'v':
                                    n += 1
                                ncol = n * 128
                                for s in range(8):
                                    b = pb.next()
                                    fns = [(lambda e, b=b, k=k, s=s, c=c, ncol=ncol, wt=wt: e.matmul(ps[:, b, 0:ncol], lhsT=xnT[slot][:, k, s * 128:(s + 1) * 128], rhs=wt[:, k, c * 128:c * 128 + ncol], start=(k == 0), stop=(k == 15))) for k in range(16)]
                                    P.group('pe', fns, reads=[xk[s], ('wb', ws)], writes=[PSK(b)])
                                    ss = sbr.next()
                                    evac(evr.next(), stb[ss][:, 0:ncol], ps[:, b, 0:ncol], [PSK(b)], [('stb', ss)])
                                    r0 = (i * 8 + s) * 128
                                    P.dma('sp', ('stb', ss), lambda e, ss=ss, r0=r0, off=off, ncol=ncol: e.dma_start(out=sv[r0:r0 + 128, off:off + ncol], in_=stb[ss][:, 0:ncol]),
                                          reads=[('stb', ss)])
                                c += n
                            else:
                                for hf in range(2):
                                    b = pb.next()
                                    t0_ = (i * 2 + hf) * 512
                                    fns = [(lambda e, b=b, k=k, c=c, wt=wt, hf=hf: e.matmul(ps[:, b, :], lhsT=wt[:, k, c * 128:(c + 1) * 128], rhs=xnT[slot][:, k, hf * 512:(hf + 1) * 512], start=(k == 0), stop=(k == 15))) for k in range(16)]
                                    P.group('pe', fns, reads=xk[hf * 4:hf * 4 + 4] + [('wb', ws)], writes=[PSK(b)])
                                    if kd == 'fb':
                                        ss = sbr.next()
                                        evac(evr.next(), stb[ss][:], ps[:, b, :], [PSK(b)], [('stb', ss)])
                                        P.dma('sp', ('stb', ss), lambda e, ss=ss, off=off, t0_=t0_: e.dma_start(out=sf[off:off + 128, t0_:t0_ + 512], in_=stb[ss][:]),
                                              reads=[('stb', ss)])
                                    else:
                                        ss = sfr.next()
                                        evac(evr.next(), stf[ss][:], ps[:, b, :], [PSK(b)], [('stf', ss)])
                                        P.dma('sp', ('stf', ss), lambda e, ss=ss, off=off, t0_=t0_: e.dma_start(out=sgx[off:off + 128, t0_:t0_ + 512], in_=stf[ss][:]),
                                              reads=[('stf', ss)])
                                c += 1

                load_x(0, 0)
                norm_T(0, 0)
                load_x(0, 1)
                norm_T(0, 1)
                load_x(1, 0)
                for i in range(4):
                    proj(i)
                P.emit_all()

        def c_tiles(R):
            lo, hi = 99, -99
            for kind in ('s', 'p'):
                for r in (2 * R, 2 * R + 1):
                    if kind == 's':
                        rs = min(max(r - 4, 0), 56)
                    else:
                        base = (r // 32) * 32
                        rs = base + min(max(r % 32 - 4, 0), 24)
                    lo = min(lo, rs // 2 - R)
                    hi = max(hi, (rs + 7) // 2 - R)
            return lo, hi

        def phase2(l):
            with ExitStack() as es:
                def sb(name, shape, dt):
                    return es.enter_context(nc.sbuf_tensor("L%d_" % l + name, shape, dt))
                wg = sb("rg_w", [128, 16, 128], BF16)
                vec = sb("rg_vec", [128, 4, 11], F32)
                dv = sb("rg_dv", [128, 4, 12], F32)
                qtr = sb("rg_qtr", [128, 1], F32)
                xbs = [sb("rg_xb%d" % i, [128, 2052], F32) for i in range(2)]
                xc = sb("rg_xc", [128, TOK], F32)
                xcb = sb("rg_xcb", [128, TOK], BF16)
                hf = sb("rg_hf", [128, TOK], F32)
                hbt = [sb("rg_hb%d" % i, [128, 512], F32) for i in range(2)]
                gbt = [sb("rg_gb%d" % i, [128, 512], F32) for i in range(2)]
                T = [sb("rg_t%d" % i, [128, 512], F32) for i in range(6)]
                C = [sb("rg_c%d" % i, [128, 512], F32) for i in range(3)]
                ob = [sb("rg_ob%d" % i, [128, 512], BF16) for i in range(2)]

                def rglru_gen():
                    P.dma('pool', 'rg_w', lambda e: e.dma_start(out=wg[:], in_=rgw[l].rearrange("m p n -> p m n")), writes=['rg_w'])
                    P.dma('sp', 'rg_vec', lambda e: e.dma_start(out=vec[:], in_=rgv[:, l, :, :]), writes=['rg_vec'])
                    P.op('pool', lambda e: e.tensor_scalar(out=dv[:, :, 0:4], in0=vec[:, :, 5:9], scalar1=0.5, scalar2=None, op0=ALU.mult), reads=['rg_vec'], writes=['dv_a'])
                    P.op('act', lambda e: e.activation(out=dv[:, :, 8:10], in_=vec[:, :, 9:11], func=AF.Exp, scale=-1.0), reads=['rg_vec'], writes=['dv_t'])
                    P.op('act', lambda e: e.activation(out=dv[:, :, 10:12], in_=dv[:, :, 8:10], func=AF.Ln, bias=1.0), reads=['dv_t'], writes=['dv_s'])
                    P.op('pool', lambda e: e.tensor_scalar(out=dv[:, :, 4:6], in0=dv[:, :, 10:12], scalar1=-4.0, scalar2=None, op0=ALU.mult), reads=['dv_s'], writes=['dv_c'])
                    P.op('pool', lambda e: e.tensor_scalar(out=dv[:, :, 6:8], in0=dv[:, :, 10:12], scalar1=-8.0, scalar2=None, op0=ALU.mult), reads=['dv_s'], writes=['dv_c2'])
                    P.op('pool', lambda e: e.memset(qtr[:], 0.25), writes=['half'])
                    DVK = ['dv_a', 'dv_c', 'dv_c2', 'rg_vec']
                    TK = lambda n: ('rgT', n)
                    CK = lambda n: ('rgC', n)
                    rgb = Rot([7])
                    obr = Rot([0, 1])
                    yield
                    for c in range(4):
                        for sgi in range(2):
                            P.op('pool', lambda e, sgi=sgi: e.memset(xbs[sgi][:, 0:2], 0.0), writes=[('xb', sgi)])
                            P.op('pool', lambda e, sgi=sgi: e.memset(xbs[sgi][:, 2050:2052], 0.0), writes=[('xb', sgi)])
                            P.dma('sp', ('xb', sgi), lambda e, sgi=sgi, c=c: e.dma_start(out=xbs[sgi][:, 2:2050], in_=sgx[512 + c * 128:512 + (c + 1) * 128, sgi * 2048:(sgi + 1) * 2048]),
                                  writes=[('xb', sgi)])
                        yield
                        P.op('pool', lambda e: e.tensor_scalar(out=xbs[0][:, 2050:2051], in0=xbs[1][:, 2:3], scalar1=flg[:, 0:1], scalar2=None, op0=ALU.mult),
                             reads=[('xb', 1), 'flg'], writes=[('xb', 0)])
                        P.op('pool', lambda e: e.tensor_scalar(out=xbs[1][:, 0:2], in0=xbs[0][:, 2048:2050], scalar1=flg[:, 0:1], scalar2=None, op0=ALU.mult),
                             reads=[('xb', 0), 'flg'], writes=[('xb', 1)])
                        for sgi in range(2):
                            xo = xc[:, sgi * 2048:(sgi + 1) * 2048]
                            P.op('pool', lambda e, sgi=sgi, xo=xo, c=c: e.tensor_scalar(out=xo, in0=xbs[sgi][:, 0:2048], scalar1=vec[:, c, 0:1], scalar2=vec[:, c, 4:5], op0=ALU.mult, op1=ALU.add),
                                 reads=[('xb', sgi), 'rg_vec'], writes=[('xc', sgi)])
                            for j in range(1, 4):
                                P.op('dve', lambda e, sgi=sgi, xo=xo, c=c, j=j: e.scalar_tensor_tensor(out=xo, in0=xbs[sgi][:, j:j + 2048], scalar=vec[:, c, j:j + 1], in1=xo, op0=ALU.mult, op1=ALU.add),
                                     reads=[('xb', sgi), 'rg_vec', ('xc', sgi)], writes=[('xc', sgi)])
                                yield
                            P.op('pool', lambda e, sgi=sgi, xo=xo: e.tensor_copy(out=xcb[:, sgi * 2048:(sgi + 1) * 2048], in_=xo),
                                 reads=[('xc', sgi)], writes=[('xcb', sgi)])
                            yield
                        for d in range(2):
                            for step in range(8):
                                t = step if d == 0 else 7 - step
                                sgi = t // 4
                                cols = slice(t * 512, (t + 1) * 512)
                                gs = step % 2
                                if d == 1:
                                    P.dma('sp', ('gbt', gs), lambda e, gs=gs, c=c, cols=cols: e.dma_start(out=gbt[gs][:], in_=sgx[c * 128:(c + 1) * 128, cols]), writes=[('gbt', gs)])
                                br = rgb.next()
                                bi = rgb.next()
                                P.op('pe', lambda e, br=br, d=d, c=c, cols=cols: e.matmul(ps[:, br, :], lhsT=wg[:, d * 8 + 0 * 4 + c, :], rhs=xcb[:, cols], start=True, stop=True),
                                     reads=['rg_w', ('xcb', sgi)], writes=[PSK(br)])
                                yield
                                P.op('act', lambda e, br=br, d=d, c=c: e.activation(out=T[0][:], in_=ps[:, br, :], func=AF.Tanh, scale=0.5, bias=dv[:, c, d:d + 1]),
                                     reads=[PSK(br)] + DVK, writes=[TK(0)])
                                yield
                                P.op('pe', lambda e, bi=bi, d=d, c=c, cols=cols: e.matmul(ps[:, bi, :], lhsT=wg[:, d * 8 + 1 * 4 + c, :], rhs=xcb[:, cols], start=True, stop=True),
                                     reads=['rg_w', ('xcb', sgi)], writes=[PSK(bi)])
                                yield
                                P.op('act', lambda e, bi=bi, d=d, c=c: e.activation(out=T[1][:], in_=ps[:, bi, :], func=AF.Tanh, scale=0.5, bias=dv[:, c, 2 + d:3 + d]),
                                     reads=[PSK(bi)] + DVK, writes=[TK(1)])
                                P.op('act', lambda e, d=d, c=c: e.activation(out=T[2][:], in_=T[0][:], func=AF.Exp, scale=dv[:, c, 4 + d:5 + d], bias=dv[:, c, 4 + d:5 + d]),
                                     reads=[TK(0)] + DVK, writes=[TK(2)])
                                P.op('act', lambda e, d=d, c=c: e.activation(out=T[3][:], in_=T[0][:], func=AF.Exp, scale=dv[:, c, 6 + d:7 + d], bias=dv[:, c, 6 + d:7 + d]),
                                     reads=[TK(0)] + DVK, writes=[TK(3)])
                                yield
                                P.op('dve', lambda e: e.tensor_scalar(out=T[3][:], in0=T[3][:], scalar1=-1.0, scalar2=-0.99999988, op0=ALU.mult, op1=ALU.max), reads=[TK(3)], writes=[TK(3)])
                                P.op('dve', lambda e, cols=cols: e.scalar_tensor_tensor(out=T[4][:], in0=T[1][:], scalar=1.0, in1=xc[:, cols], op0=ALU.add, op1=ALU.mult),
                                     reads=[TK(1), ('xc', sgi)], writes=[TK(4)])
                                yield
                                P.op('act', lambda e: e.activation(out=T[5][:], in_=T[3][:], func=AF.Sqrt, scale=0.25, bias=qtr[:, 0:1]), reads=[TK(3), 'half'], writes=[TK(5)])
                                P.op('pool', lambda e: e.tensor_tensor(out=T[4][:], in0=T[4][:], in1=T[5][:], op=ALU.mult), reads=[TK(4), TK(5)], writes=[TK(4)])
                                if d == 0 and t == 4:
                                    P.op('pool', lambda e: e.tensor_scalar(out=T[2][:, 0:1], in0=T[2][:, 0:1], scalar1=flg[:, 0:1], scalar2=None, op0=ALU.mult), reads=[TK(2), 'flg'], writes=[TK(2)])
                                if d == 1 and t == 3:
                                    P.op('pool', lambda e: e.tensor_scalar(out=T[2][:, 511:512], in0=T[2][:, 511:512], scalar1=flg[:, 0:1], scalar2=None, op0=ALU.mult), reads=[TK(2), 'flg'], writes=[TK(2)])
                                yield
                                if d == 0:
                                    init = 0.0 if t == 0 else hf[:, t * 512 - 1:t * 512]
                                    P.op('dve', lambda e, cols=cols, init=init: e.tensor_tensor_scan(out=hf[:, cols], data0=T[2][:], data1=T[4][:], initial=init, op0=ALU.mult, op1=ALU.add),
                                         reads=[TK(2), TK(4), 'hf'], writes=['hf'])
                                    yield
                                    continue
                                hs = step % 2
                                init = 0.0 if t == 7 else hbt[1 - hs][:, 0:1]
                                P.op('dve', lambda e, hs=hs, init=init: e.tensor_tensor_scan(out=hbt[hs][:, ::-1], data0=T[2][:, ::-1], data1=T[4][:, ::-1], initial=init, op0=ALU.mult, op1=ALU.add),
                                     reads=[TK(2), TK(4), ('hbt', 1 - hs)], writes=[('hbt', hs)])
                                yield
                                P.op('act', lambda e, gs=gs: e.activation(out=C[0][:], in_=gbt[gs][:], func=AF.Square), reads=[('gbt', gs)], writes=[CK(0)])
                                P.op('pool', lambda e: e.tensor_scalar(out=C[0][:], in0=C[0][:], scalar1=0.044715, scalar2=1.0, op0=ALU.mult, op1=ALU.add), reads=[CK(0)], writes=[CK(0)])
                                P.op('pool', lambda e, gs=gs: e.tensor_tensor(out=C[0][:], in0=C[0][:], in1=gbt[gs][:], op=ALU.mult), reads=[CK(0), ('gbt', gs)], writes=[CK(0)])
                                P.op('act', lambda e: e.activation(out=C[1][:], in_=C[0][:], func=AF.Tanh, scale=0.7978845608028654), reads=[CK(0)], writes=[CK(1)])
                                yield
                                P.op('pool', lambda e, hs=hs, cols=cols: e.tensor_tensor(out=C[2][:], in0=hf[:, cols], in1=hbt[hs][:], op=ALU.add), reads=['hf', ('hbt', hs)], writes=[CK(2)])
                                P.op('pool', lambda e: e.tensor_scalar(out=C[1][:], in0=C[1][:], scalar1=1.0, scalar2=0.5, op0=ALU.add, op1=ALU.mult), reads=[CK(1)], writes=[CK(1)])
                                P.op('pool', lambda e, gs=gs: e.tensor_tensor(out=C[1][:], in0=C[1][:], in1=gbt[gs][:], op=ALU.mult), reads=[CK(1), ('gbt', gs)], writes=[CK(1)])
                                oslot = obr.next()
                                P.op('pool', lambda e, oslot=oslot: e.tensor_tensor(out=ob[oslot][:], in0=C[1][:], in1=C[2][:], op=ALU.mult), reads=[CK(1), CK(2)], writes=[('ob', oslot)])
                                P.dma('sp', ('ob', oslot), lambda e, oslot=oslot, c=c, cols=cols: e.dma_start(out=smix[256 + c * 128:256 + (c + 1) * 128, cols], in_=ob[oslot][:]),
                                      reads=[('ob', oslot)])
                                yield

                NQK = 6
                qk = [sb("at_qk%d" % i, [128, TOK], BF16) for i in range(NQK)]
                vsl = [sb("at_v%d" % i, [128, 32, 65], BF16) for i in range(4)]
                ebA = [sb("at_ebA%d" % i, [128, 25, 128], BF16) for i in range(2)]
                ebraw = [sb("at_ebr%d" % i, [128, 7, 128], F32) for i in range(2)]
                ebC = [sb("at_ebC%d" % i, [128, 7, 128], BF16) for i in range(2)]
                E = [sb("at_E%d" % i, [128, 4, 128], BF16) for i in range(6)]
                PT = [sb("at_PT%d" % i, [128, 4, 128], BF16) for i in range(6)]
                otok = [sb("at_o%d" % i, [128, 64], F32) for i in range(3)]
                rec = [sb("at_r%d" % i, [128, 1], F32) for i in range(3)]
                mst = [sb("at_m%d" % i, [64, 512], BF16) for i in range(3)]
                for i in range(4):
                    P.op('pool', lambda e, i=i: e.memset(vsl[i][:, :, 64:65], 1.0), writes=[('v', i)])
                qkr = Rot(range(NQK))
                vr = Rot(range(4))
                sbank = Rot([0, 1, 2, 6])
                obank = Rot([3, 4])
                tbank = Rot([5])
                er = Rot(range(6))
                pr = Rot(range(6))
                orr = Rot(range(3))
                mr = Rot(range(3))
                ebAr = Rot([0, 1])
                ebCr = Rot([0, 1])
                svt = sv.rearrange("(t p) c -> p t c", p=128)

                def load_entry(qrow, krow, vcol, qaug):
                    qs = qkr.next()
                    ks = qkr.next()
                    vs = vr.next()
                    P.dma('sp', ('qk', qs), lambda e: e.dma_start(out=qk[qs][0:64, :], in_=sf[qrow:qrow + 64, :]), writes=[('qk', qs)])
                    P.dma('sp', ('qk', qs), lambda e: e.dma_start(out=qk[qs][64:128, :], in_=qaug), writes=[('qk', qs)])
                    P.dma('sp', ('qk', ks), lambda e: e.dma_start(out=qk[ks][0:64, :], in_=sf[krow:krow + 64, :]), writes=[('qk', ks)])
                    P.dma('sp', ('qk', ks), lambda e: e.dma_start(out=qk[ks][64:128, :], in_=kaug), writes=[('qk', ks)])
                    P.dma('sp', ('v', vs), lambda e: e.dma_start(out=vsl[vs][:, :, 0:64], in_=svt[:, :, vcol:vcol + 64]), writes=[('v', vs)])
                    return qs, ks, vs

                heads = [('A', j) for j in range(4)] + [('C', h) for h in range(12)]
                loaded = {}
                pending = {}

                def load_head(hd):
                    kind, idx = hd
                    if kind == 'A':
                        ents = []
                        for g in range(3):
                            h = 4 * g + idx
                            ents.append(load_entry(QA0 + h * 64, KA0 + h * 64, h * 64, qaa))
                        es_ = ebAr.next()
                        P.dma('pool', ('ebA', es_), lambda e: e.dma_start(out=ebA[es_][:], in_=eba[idx]), writes=[('ebA', es_)])
                        loaded[hd] = (ents, ebA[es_], ('ebA', es_))
                    else:
                        h = idx
                        es_ = ebCr.next()
                        P.dma('sp', ('ebr', es_), lambda e: e.dma_start(out=ebraw[es_][:], in_=rawc[l, h]), writes=[('ebr', es_)])
                        ents = [load_entry(QC0 + h * 64, KC0 + h * 64, 768 + h * 64, qac)]
                        pending[hd] = lambda: P.op('act', lambda e: e.activation(out=ebC[es_][:], in_=ebraw[es_][:], func=AF.Exp), reads=[('ebr', es_)], writes=[('ebC', es_)])
                        loaded[hd] = (ents, ebC[es_], ('ebC', es_))

                def head_tiles(hd, B):
                    kind, idx = hd
                    res = []
                    if kind == 'A':
                        base = 0
                        for g, rad in enumerate((1, 2, 8)):
                            for dl in range(-rad, rad + 1):
                                kt = B + dl
                                if 0 <= kt < 32:
                                    res.append((base + dl + rad, g, kt))
                            base += 2 * rad + 1
                    else:
                        lo, hi = c_tiles(B)
                        for dl in range(lo, hi + 1):
                            kt = B + dl
                            if 0 <= kt < 32:
                                res.append((dl + 3, 0, kt))
                    return res

                chunks = []
                for hi_, hd in enumerate(heads):
                    kind, idx = hd
                    mixrow = idx * 64 if kind == 'A' else 768 + idx * 64
                    for B in range(32):
                        tl = head_tiles(hd, B)
                        runs = []
                        cur = [tl[0]]
                        for tt in tl[1:]:
                            if tt[0] == cur[-1][0] + 1 and len(cur) < 4:
                                cur.append(tt)
                            else:
                                runs.append(cur)
                                cur = [tt]
                        runs.append(cur)
                        for ri, run in enumerate(runs):
                            chunks.append(dict(hd=hd, hi=hi_, B=B, run=run, first=(ri == 0), last=(ri == len(runs) - 1),
                                               mixrow=mixrow, headstart=(B == 0 and ri == 0)))

                state = {}

                def emit_qk(ch):
                    hd = ch['hd']
                    if ch['headstart']:
                        if hd not in loaded:
                            load_head(hd)
                        if hd in pending:
                            pending.pop(hd)()
                        nxt = ch['hi'] + 1
                        if hd[0] == 'C' and nxt < len(heads) and heads[nxt] not in loaded:
                            load_head(heads[nxt])
                    ents, ebt, ebk = loaded[hd]
                    b = sbank.next()
                    ch['sb'] = b
                    B = ch['B']
                    fns = []
                    rd = set()
                    for ti, (ebi, en, kt) in enumerate(ch['run']):
                        qs, ks, vs = ents[en]
                        rd.add(('qk', qs))
                        rd.add(('qk', ks))
                        fns.append(lambda e, b=b, ti=ti, qs=qs, ks=ks, kt=kt, B=B: e.matmul(ps[:, b, ti * 128:(ti + 1) * 128], lhsT=qk[ks][:, kt * 128:(kt + 1) * 128], rhs=qk[qs][:, B * 128:(B + 1) * 128], start=True, stop=True))
                    P.group('pe', fns, reads=list(rd), writes=[PSK(b)])

                binfo = {}

                def emit_exp(ch):
                    b = ch['sb']
                    n = len(ch['run'])
                    es_ = er.next()
                    ch['es'] = es_
                    P.op('act', lambda e: e.activation(out=E[es_][:, 0:n, :], in_=ps[:, b, 0:n * 128].rearrange("p (a c) -> p a c", a=n), func=AF.Exp, scale=0.125, bias=-8.0),
                         reads=[PSK(b)], writes=[('E', es_)])

                def emit_mul(ch):
                    ents, ebt, ebk = loaded[ch['hd']]
                    n = len(ch['run'])
                    eb0 = ch['run'][0][0]
                    es_ = ch['es']
                    ps_ = pr.next()
                    ch['pt'] = ps_
                    state['mulc'] = state.get('mulc', 0) + 1
                    meng = 'dve'
                    P.op(meng, lambda e: e.tensor_tensor(out=PT[ps_][:, 0:n, :], in0=E[es_][:, 0:n, :], in1=ebt[:, eb0:eb0 + n, :], op=ALU.mult),
                         reads=[('E', es_), ebk], writes=[('PT', ps_)])

                def emit_pv(ch):
                    ents, ebt, ebk = loaded[ch['hd']]
                    n = len(ch['run'])
                    ps_ = ch['pt']
                    key = (ch['hi'], ch['B'])
                    if ch['first']:
                        binfo[key] = {'ob': obank.next()}
                    ob_ = binfo[key]['ob']
                    fns = []
                    rd = {('PT', ps_)}
                    for ti, (ebi, en, kt) in enumerate(ch['run']):
                        qs, ks, vs = ents[en]
                        rd.add(('v', vs))
                        fns.append(lambda e, ti=ti, vs=vs, kt=kt, st=(ch['first'] and ti == 0), sp=(ch['last'] and ti == n - 1): e.matmul(ps[:, ob_, 0:65], lhsT=PT[ps_][:, ti, :], rhs=vsl[vs][:, kt, :], start=st, stop=sp))
                    P.group('pe', fns, reads=list(rd), writes=[PSK(ob_)])

                def emit_fin(ch):
                    bi_ = binfo[(ch['hi'], ch['B'])]
                    ob_ = bi_['ob']
                    os_ = orr.next()
                    bi_['os'] = os_
                    P.op('dve', lambda e: e.reciprocal(out=rec[os_][:], in_=ps[:, ob_, 64:65]), reads=[PSK(ob_)], writes=[('rec', os_)])
                    P.op('dve', lambda e: e.tensor_scalar(out=otok[os_][:], in0=ps[:, ob_, 0:64], scalar1=rec[os_][:, 0:1], scalar2=None, op0=ALU.mult),
                         reads=[PSK(ob_), ('rec', os_)], writes=[('otok', os_)])

                def emit_tr(ch):
                    bi_ = binfo[(ch['hi'], ch['B'])]
                    os_ = bi_['os']
                    B = ch['B']
                    if B % 4 == 0:
                        state['tb'] = tbank.next()
                    tb_ = state['tb']
                    bi_['tb'] = tb_
                    P.op('pe', lambda e: e.transpose(out=ps[0:64, tb_, (B % 4) * 128:(B % 4 + 1) * 128], in_=otok[os_][:], identity=ident[:]),
                         reads=[('otok', os_), 'ident'], writes=[PSK(tb_)])

                def emit_ev(ch):
                    bi_ = binfo[(ch['hi'], ch['B'])]
                    tb_ = bi_['tb']
                    B = ch['B']
                    ms_ = mr.next()
                    mixrow = ch['mixrow']
                    P.op('act', lambda e: e.activation(out=mst[ms_][:], in_=ps[0:64, tb_, :], func=AF.Copy), reads=[PSK(tb_)], writes=[('mst', ms_)])
                    P.dma('sp', ('mst', ms_), lambda e: e.dma_start(out=smix[mixrow:mixrow + 64, (B - 3) * 128:(B + 1) * 128], in_=mst[ms_][:]),
                          reads=[('mst', ms_)])

                rg = rglru_gen()
                next(rg)
                KRG = 3
                cnt = 0
                for hi_ in range(len(heads)):
                    hc = [c_ for c_ in chunks if c_['hi'] == hi_]
                    n_ = len(hc)
                    for step in range(n_ + 8):
                        if 0 <= step - 7 < n_ and hc[step - 7]['last'] and hc[step - 7]['B'] % 4 == 3:
                            emit_ev(hc[step - 7])
                        if 0 <= step - 6 < n_ and hc[step - 6]['last']:
                            emit_tr(hc[step - 6])
                        if 0 <= step - 5 < n_ and hc[step - 5]['last']:
                            emit_fin(hc[step - 5])
                        if 0 <= step - 3 < n_:
                            emit_pv(hc[step - 3])
                        if 0 <= step - 2 < n_:
                            emit_mul(hc[step - 2])
                        if 0 <= step - 1 < n_:
                            emit_exp(hc[step - 1])
                        if step < n_:
                            emit_qk(hc[step])
                        cnt += 1
                        if cnt % KRG == 0:
                            next(rg, None)
                for _ in rg:
                    pass
                P.emit_all()

        def phase3a(l, xsrc):
            with ExitStack() as es:
                def sb(name, shape, dt):
                    return es.enter_context(nc.sbuf_tensor("L%d_" % l + name, shape, dt))
                xtoks = [sb("p3_xtok%d" % i, [128, 4, D], F32) for i in range(2)]
                mixTs = [sb("p3_mixT%d" % i, [128, 12, 512], BF16) for i in range(2)]
                xss = [sb("p3_xs%d" % i, [128, D], F32) for i in range(2)]
                grow = sb("p3_grow", [128, D], F32)
                hst = [sb("p3_hst%d" % i, [128, 16, 512], BF16) for i in range(2)]
                wres = sb("p3_wo", [128, 12, D], BF16)
                stat = sb("p3_stat", [128, 4, 4], F32)
                P.dma('sp', 'grow', lambda e: e.dma_start(out=grow[:], in_=g2row[l]), writes=['grow'])
                pb = Rot([0, 1, 2, 3, 4, 5])
                tb = Rot([6, 7])
                evr = Rot(['act', 'dve'])
                wo_l = w_out[l].rearrange("(k p) c -> p k c", p=128)
                smx = smix.rearrange("(c p) t -> p c t", p=128)
                shn_v = shn.rearrange("(k p) t -> p k t", p=128)

                def load_in(i):
                    sl = i % 2
                    for s in range(4):
                        r0 = (i * 4 + s) * 128
                        P.dma('sp', ('xtok', sl, s), lambda e, s=s, r0=r0, sl=sl: e.dma_start(out=xtoks[sl][:, s, :], in_=xsrc[r0:r0 + 128, :]), writes=[('xtok', sl, s)])
                    P.dma('sp', ('mixT', sl), lambda e, i=i, sl=sl: e.dma_start(out=mixTs[sl][:], in_=smx[:, :, i * 512:(i + 1) * 512]), writes=[('mixT', sl)])

                def load_x(i, s):
                    sl = i % 2
                    r0 = (i * 4 + s) * 128
                    P.dma('sp', ('xtok', sl, s), lambda e, s=s, r0=r0, sl=sl: e.dma_start(out=xtoks[sl][:, s, :], in_=xsrc[r0:r0 + 128, :]), writes=[('xtok', sl, s)])

                def load_mix(i):
                    sl = i % 2
                    P.dma('sp', ('mixT', sl), lambda e, i=i, sl=sl: e.dma_start(out=mixTs[sl][:], in_=smx[:, :, i * 512:(i + 1) * 512]), writes=[('mixT', sl)])

                def wout(i, n):
                    sl = i % 2
                    xtok = xtoks[sl]
                    mixT = mixTs[sl]
                    for s in range(4):
                        b = pb.next()
                        fns = [(lambda e, b=b, k=k, s=s, n=n: e.matmul(ps[:, b, :], lhsT=mixT[:, k, s * 128:(s + 1) * 128], rhs=wres[:, k, n * 512:(n + 1) * 512], start=(k == 0), stop=(k == 11))) for k in range(12)]
                        P.group('pe', fns, reads=[('mixT', sl), 'wres'], writes=[PSK(b)])
                        P.op('dve', lambda e, b=b, s=s, n=n: e.tensor_tensor(out=xtok[:, s, n * 512:(n + 1) * 512], in0=ps[:, b, :], in1=xtok[:, s, n * 512:(n + 1) * 512], op=ALU.add),
                             reads=[PSK(b), ('xtok', sl, s)], writes=[('xtok', sl, s)])

                def chain(i, s):
                    sl = i % 2
                    xtok = xtoks[sl]
                    xsl = xss[s % 2]
                    xsk = ('xs', s % 2)
                    r0 = (i * 4 + s) * 128
                    xk = ('xtok', sl, s)
                    P.dma('sp', ('x1o', sl, s), lambda e, s=s, r0=r0: e.dma_start(out=sx1[r0:r0 + 128, :], in_=xtok[:, s, :]), reads=[xk])
                    st = stat[:, s, :]
                    P.op('act', lambda e, s=s, st=st: e.activation(out=xsl[:], in_=xtok[:, s, :], func=AF.Square, accum_out=st[:, 0:1]),
                         reads=[xk], writes=[xsk, ('st', s, 0)])
                    P.op('dve', lambda e, st=st: e.tensor_scalar(out=st[:, 1:2], in0=st[:, 0:1], scalar1=1.0 / D, scalar2=EPS, op0=ALU.mult, op1=ALU.add),
                         reads=[('st', s, 0)], writes=[('st', s, 1)])
                    P.op('act', lambda e, st=st: e.activation(out=st[:, 2:3], in_=st[:, 1:2], func=AF.Sqrt), reads=[('st', s, 1)], writes=[('st', s, 2)])
                    P.op('dve', lambda e, st=st: e.reciprocal(out=st[:, 3:4], in_=st[:, 2:3]), reads=[('st', s, 2)], writes=[('st', s, 3)])
                    P.op('dve', lambda e, s=s, st=st: e.scalar_tensor_tensor(out=xsl[:], in0=xtok[:, s, :], scalar=st[:, 3:4], in1=grow[:], op0=ALU.mult, op1=ALU.mult),
                         reads=[xk, ('st', s, 3), 'grow'], writes=[xsk])

                def transp(i, s):
                    sl = i % 2
                    xsl = xss[s % 2]
                    xsk = ('xs', s % 2)
                    for kg in range(4):
                        b = tb.next()
                        fns = [(lambda e, b=b, kk=kk, kg=kg: e.transpose(out=ps[:, b, kk * 128:(kk + 1) * 128], in_=xsl[:, (kg * 4 + kk) * 128:(kg * 4 + kk + 1) * 128], identity=ident[:])) for kk in range(4)]
                        P.group('pe', fns, reads=[xsk, 'ident'], writes=[PSK(b)])
                        out_ap = hst[sl][:, kg * 4:(kg + 1) * 4, s * 128:(s + 1) * 128]
                        in_ap = ps[:, b, :].rearrange("p (a c) -> p a c", a=4)
                        if evr.next() == 'act':
                            P.op('act', lambda e, out_ap=out_ap, in_ap=in_ap: e.activation(out=out_ap, in_=in_ap, func=AF.Copy), reads=[PSK(b)], writes=[('hst', sl)])
                        else:
                            P.op('dve', lambda e, out_ap=out_ap, in_ap=in_ap: e.tensor_copy(out=out_ap, in_=in_ap), reads=[PSK(b)], writes=[('hst', sl)])

                for n in range(4):
                    P.dma('pool', 'wres', lambda e, n=n: e.dma_start(out=wres[:, :, n * 512:(n + 1) * 512], in_=wo_l[:, :, n * 512:(n + 1) * 512]), writes=['wres'])
                for i0 in range(2):
                    for s in range(4):
                        load_x(i0, s)
                    load_mix(i0)
                for n in range(4):
                    wout(0, n)
                for i in range(NT):
                    for s in range(4):
                        chain(i, s)
                        if i + 2 < NT:
                            load_x(i + 2, s)
                            if s == 0:
                                load_mix(i + 2)
                        if i + 1 < NT:
                            wout(i + 1, s)
                        transp(i, s)
                    P.dma('sp', ('hst', i % 2), lambda e, i=i: e.dma_start(out=shn_v[:, :, i * 512:(i + 1) * 512], in_=hst[i % 2][:]), reads=[('hst', i % 2)])
                P.emit_all()

        def phase3b(l):
            with ExitStack() as es:
                def sb(name, shape, dt):
                    return es.enter_context(nc.sbuf_tensor("L%d_" % l + name, shape, dt))
                hnT = sb("f_hnT", [128, 16, 1024], BF16)
                hT = sb("f_hT", [128, 44, 1024], BF16)
                WR = 3
                wring = [sb("f_w%d" % i, [128, 5632], BF16) for i in range(WR)]
                sg = [sb("f_sg%d" % i, [128, 512], F32) for i in range(2)]
                yTs = [sb("f_yT%d" % i, [128, 4, 1024], F32) for i in range(2)]
                xp = [sb("f_xp%d" % i, [128, 512], F32) for i in range(4)]
                wr = Rot(range(WR))
                pb = Rot([0, 1, 2, 3, 4, 5, 6, 7])
                pbo = Rot([0, 1, 2, 3, 4, 5])
                tb = Rot([6, 7])
                sgr = Rot([0, 1])
                xpr = Rot(range(4))
                w1_l = w_f1[l].rearrange("(k p) c -> p k c", p=128)
                w2_l = w_f2[l].rearrange("(f p) c -> p f c", p=128)
                shn_v = shn.rearrange("(k p) t -> p k t", p=128)
                def load_hn(j):
                    P.dma('sp', 'hnT', lambda e, j=j: e.dma_start(out=hnT[:], in_=shn_v[:, :, j * 1024:(j + 1) * 1024]), writes=['hnT'])
                load_hn(0)
                for j in range(4):
                    for f in range(44):
                        ws = wr.next()
                        wt = wring[ws][:, 0:4096].rearrange("p (k g c) -> p k g c", k=16, g=2)
                        P.dma('pool', ('w', ws), lambda e, wt=wt, f=f: e.dma_start(out=wt[:, :, 0, :], in_=w1_l[:, :, f * 128:(f + 1) * 128]), writes=[('w', ws)])
                        P.dma('pool', ('w', ws), lambda e, wt=wt, f=f: e.dma_start(out=wt[:, :, 1, :], in_=w1_l[:, :, DFF + f * 128:DFF + (f + 1) * 128]), writes=[('w', ws)])
                        for hf in range(2):
                            bg = pb.next()
                            bu = pb.next()
                            fns = [(lambda e, bg=bg, k=k, wt=wt, hf=hf: e.matmul(ps[:, bg, :], lhsT=wt[:, k, 0, :], rhs=hnT[:, k, hf * 512:(hf + 1) * 512], start=(k == 0), stop=(k == 15))) for k in range(16)]
                            P.group('pe', fns, reads=['hnT', ('w', ws)], writes=[PSK(bg)])
                            fns = [(lambda e, bu=bu, k=k, wt=wt, hf=hf: e.matmul(ps[:, bu, :], lhsT=wt[:, k, 1, :], rhs=hnT[:, k, hf * 512:(hf + 1) * 512], start=(k == 0), stop=(k == 15))) for k in range(16)]
                            P.group('pe', fns, reads=['hnT', ('w', ws)], writes=[PSK(bu)])
                            sgs = sgr.next()
                            P.op('act', lambda e, bg=bg, sgs=sgs: e.activation(out=sg[sgs][:], in_=ps[:, bg, :], func=AF.Silu), reads=[PSK(bg)], writes=[('sg', sgs)])
                            P.op('dve', lambda e, bu=bu, sgs=sgs, f=f, hf=hf: e.tensor_tensor(out=hT[:, f, hf * 512:(hf + 1) * 512], in0=sg[sgs][:], in1=ps[:, bu, :], op=ALU.mult),
                                 reads=[PSK(bu), ('sg', sgs)], writes=[('hT', f)])
                    HTK = [('hT', f) for f in range(44)]
                    if j + 1 < 4:
                        load_hn(j + 1)

                    def ffn_out_c(cg, cc):
                        c = cg * 4 + cc
                        yT = yTs[cg % 2]
                        ws = wr.next()
                        wt = wring[ws][:, 0:5632].rearrange("p (f c) -> p f c", f=44)
                        P.dma('pool', ('w', ws), lambda e, wt=wt, c=c: e.dma_start(out=wt, in_=w2_l[:, :, c * 128:(c + 1) * 128]), writes=[('w', ws)])
                        for hf in range(2):
                            b = pbo.next()
                            fns = [(lambda e, b=b, f=f, wt=wt, hf=hf: e.matmul(ps[:, b, :], lhsT=wt[:, f, :], rhs=hT[:, f, hf * 512:(hf + 1) * 512], start=(f == 0), stop=(f == 43))) for f in range(44)]
                            P.group('pe', fns, reads=HTK + [('w', ws)], writes=[PSK(b)])
                            P.op('act', lambda e, b=b, cc=cc, hf=hf, yT=yT: e.activation(out=yT[:, cc, hf * 512:(hf + 1) * 512], in_=ps[:, b, :], func=AF.Copy), reads=[PSK(b)], writes=[('yT', cg % 2, cc, hf)])

                    def tail(cg, j=j):
                        yT = yTs[cg % 2]

                        def xload(s):
                            r0 = (j * 8 + s) * 128
                            xs_ = s % 4
                            P.dma('sp', ('xp', xs_), lambda e, xs_=xs_, r0=r0, cg=cg: e.dma_start(out=xp[xs_][:], in_=sx1[r0:r0 + 128, cg * 512:(cg + 1) * 512]), writes=[('xp', xs_)])
                        for s in range(3):
                            xload(s)
                        for s in range(8):
                            r0 = (j * 8 + s) * 128
                            xs_ = s % 4
                            if s + 3 < 8:
                                xload(s + 3)
                            b = tb.next()
                            fns = [(lambda e, b=b, cc=cc, s=s, yT=yT: e.transpose(out=ps[:, b, cc * 128:(cc + 1) * 128], in_=yT[:, cc, s * 128:(s + 1) * 128], identity=ident[:])) for cc in range(4)]
                            P.group('pe', fns, reads=[('yT', cg % 2, cc, s // 4) for cc in range(4)] + ['ident'], writes=[PSK(b)])
                            P.op('dve', lambda e, b=b, xs_=xs_: e.tensor_tensor(out=xp[xs_][:], in0=ps[:, b, :], in1=xp[xs_][:], op=ALU.add),
                                 reads=[PSK(b), ('xp', xs_)], writes=[('xp', xs_)])
                            P.dma('sp', ('xp', xs_), lambda e, xs_=xs_, r0=r0, cg=cg: e.dma_start(out=sx[r0:r0 + 128, cg * 512:(cg + 1) * 512], in_=xp[xs_][:]), reads=[('xp', xs_)])

                    for cg in range(4):
                        for cc in range(4):
                            ffn_out_c(cg, cc)
                            if cc == 0 and cg > 0:
                                tail(cg - 1)
                    tail(3)
                P.emit_all()

        def phase3c():
            with ExitStack() as es:
                def sb(name, shape, dt):
                    return es.enter_context(nc.sbuf_tensor("fin_" + name, shape, dt))
                xb_ = [sb("x%d" % i, [128, D], F32) for i in range(8)]
                gf = sb("gf", [128, D], F32)
                junk = sb("junk", [128, D], BF16)
                stat = sb("stat", [128, 8, 4], F32)
                P.dma('sp', 'gf', lambda e: e.dma_start(out=gf[:], in_=gfrow), writes=['gf'])
                def fload(t):
                    sl = t % 8
                    r0 = t * 128
                    P.dma('sp', ('fx', sl), lambda e, sl=sl, r0=r0: e.dma_start(out=xb_[sl][:], in_=sx[r0:r0 + 128, :]), writes=[('fx', sl)])
                for t in range(6):
                    fload(t)
                for t in range(32):
                    sl = t % 8
                    r0 = t * 128
                    xk = ('fx', sl)
                    st = stat[:, sl, :]
                    if t + 6 < 32:
                        fload(t + 6)
                    P.op('act', lambda e, sl=sl, st=st: e.activation(out=junk[:], in_=xb_[sl][:], func=AF.Square, accum_out=st[:, 0:1]), reads=[xk], writes=['junk', ('st', sl, 0)])
                    P.op('dve', lambda e, st=st: e.tensor_scalar(out=st[:, 1:2], in0=st[:, 0:1], scalar1=1.0 / D, scalar2=EPS, op0=ALU.mult, op1=ALU.add), reads=[('st', sl, 0)], writes=[('st', sl, 1)])
                    P.op('act', lambda e, st=st: e.activation(out=st[:, 2:3], in_=st[:, 1:2], func=AF.Sqrt), reads=[('st', sl, 1)], writes=[('st', sl, 2)])
                    P.op('dve', lambda e, st=st: e.reciprocal(out=st[:, 3:4], in_=st[:, 2:3]), reads=[('st', sl, 2)], writes=[('st', sl, 3)])
                    P.op('dve', lambda e, sl=sl, st=st: e.scalar_tensor_tensor(out=xb_[sl][:], in0=xb_[sl][:], scalar=st[:, 3:4], in1=gf[:], op0=ALU.mult, op1=ALU.mult),
                         reads=[xk, ('st', sl, 3), 'gf'], writes=[xk])
                    P.dma('sp', xk, lambda e, sl=sl, r0=r0: e.dma_start(out=yout[r0:r0 + 128, :], in_=xb_[sl][:]), reads=[xk])
                P.emit_all()

        for l in range(NLAYERS):
            xsrc = xin if l == 0 else sx
            phase1(l, xsrc)
            phase2(l)
            phase3a(l, xsrc)
            phase3b(l)
        phase3c()
    return nc


def _bf16(a):
    return np.asarray(a, dtype=np.float32).astype(ml_dtypes.bfloat16)


def _const_tables():
    slopes = 2.0 ** (-8.0 * np.arange(1, 13, dtype=np.float64) / 12.0)
    p = np.arange(128)[:, None]
    q = np.arange(128)[None, :]
    eba = np.zeros((4, 128, 25, 128), np.float32)
    for j in range(4):
        base = 0
        for g, (d, rad) in enumerate(((1, 1), (4, 2), (16, 8))):
            h = 4 * g + j
            for dl in range(-rad, rad + 1):
                delta = 128 * dl + p - q
                ok = (delta % d == 0) & (np.abs(delta) <= 64 * d)
                val = np.exp(-slopes[h] * np.abs(delta))
                eba[j, :, base + dl + rad, :] = np.where(ok, val, 0.0)
            base += 2 * rad + 1
    rows = np.arange(TOK) // 64
    kaug = (rows[None, :] == np.arange(64)[:, None]).astype(np.float32)
    return eba, kaug


def _q_aug(is_sample):
    a = np.arange(64)[:, None]
    r = (np.arange(TOK) // 64)[None, :]
    if is_sample:
        rs = np.clip(r - 4, 0, 56)
        qaa = np.zeros((64, TOK), np.float32)
    else:
        base = (r // 32) * 32
        rs = base + np.clip(r % 32 - 4, 0, 24)
        qaa = np.where((a // 32) == (r // 32), 0.0, NEG).astype(np.float32)
    qac = np.where((a >= rs) & (a < rs + 8), 0.0, NEG).astype(np.float32)
    return qaa, qac


def _rawc(na_rpb):
    p = np.arange(128)
    kr2, kc = (p // 64)[:, None], (p % 64)[:, None]
    qr2, qc = (p // 64)[None, :], (p % 64)[None, :]
    cs = np.clip(qc - 8, 0, 48)
    colok = (kc >= cs) & (kc < cs + 16)
    dc = np.clip(kc - qc + 15, 0, 30)
    out = np.full((L, 12, 128, 7, 128), NEG, np.float32)
    for dl in range(-3, 4):
        dr = 2 * dl + kr2 - qr2 + 7
        ok = colok & (dr >= 0) & (dr < 15)
        drc = np.clip(dr, 0, 14)
        g = na_rpb[:, :, drc, dc]
        out[:, :, :, dl + 3, :] = np.where(ok[None, None], g, NEG)
    return out


_NC_CACHE = {}


def kernel(x_prompt, x_sample, norm1_g, w_in, conv_w, conv_b, rg_wa, rg_ba, rg_wx, rg_bx,
           rg_lam, na_rpb, w_out, norm2_g, w_ffn_in, w_ffn_out, final_g):
    f32 = lambda a: np.ascontiguousarray(np.asarray(a, dtype=np.float32))
    x_prompt, x_sample = f32(x_prompt), f32(x_sample)
    slots = [(x_sample[0], True), (x_sample[1], True)]
    for i in range(4):
        slots.append((x_prompt[2 * i:2 * i + 2].reshape(TOK, D), False))
    slots.append(slots[-1])
    slots.append(slots[-1])

    eba, kaug = _const_tables()
    rawc = _rawc(f32(na_rpb))
    qa = {True: _q_aug(True), False: _q_aug(False)}
    rgw = np.zeros((L, 2, 2, 4, 128, 128), np.float32)
    for kind, w in enumerate((f32(rg_wa), f32(rg_wx))):
        for c in range(4):
            for half in range(2):
                rgw[:, :, kind, c, half * 64:(half + 1) * 64, half * 64:(half + 1) * 64] = w[:, :, 2 * c + half]
    rgw = rgw.reshape(L, 16, 128, 128)
    rgv = np.zeros((128, L, 4, 11), np.float32)

    def chan(v):
        return v.reshape(L, 4, 128).transpose(2, 0, 1)
    cw = f32(conv_w)
    for j in range(4):
        rgv[:, :, :, j] = chan(cw[:, j])
    rgv[:, :, :, 4] = chan(f32(conv_b))
    for d in range(2):
        rgv[:, :, :, 5 + d] = chan(f32(rg_ba)[:, d])
        rgv[:, :, :, 7 + d] = chan(f32(rg_bx)[:, d])
        rgv[:, :, :, 9 + d] = chan(f32(rg_lam)[:, d])
    bc = lambda v: np.ascontiguousarray(np.broadcast_to(f32(v)[..., None, :], v.shape[:-1] + (128, D)))
    common = {
        "w_in": f32(w_in), "w_out": f32(w_out), "w_ffn_in": f32(w_ffn_in), "w_ffn_out": f32(w_ffn_out),
        "g1row": bc(norm1_g), "g2row": bc(norm2_g), "gfrow": bc(final_g),
        "rgw": rgw, "rgv": rgv, "eba": eba, "rawc": rawc, "kaug": _bf16(kaug),
        "ident": np.eye(128, dtype=np.float32),
    }
    in_maps = []
    for xs, is_s in slots:
        m = dict(common)
        m["xin"] = np.ascontiguousarray(xs)
        m["flag"] = np.full((128, 1), 1.0 if is_s else 0.0, np.float32)
        m["qaa"] = _bf16(qa[is_s][0])
        m["qac"] = _bf16(qa[is_s][1])
        in_maps.append(m)
    if "nc" not in _NC_CACHE:
        _NC_CACHE["nc"] = build_program()
    nc = _NC_CACHE["nc"]
    res = run_bass_kernel_spmd(nc, in_maps, core_ids=list(range(8)))
    outs = [np.asarray(r["yout"], dtype=np.float32) for r in res.results]
    if DEBUG:
        kernel.debug = res.results
    y_sample = np.stack([outs[0], outs[1]], axis=0)
    y_prompt = np.concatenate([outs[2 + i].reshape(2, 2048, D) for i in range(4)], axis=0)
    return (y_prompt, y_sample)
```

```python
import os
from contextlib import ExitStack
import numpy as np
import ml_dtypes
import concourse.bass as bass
import concourse.mybir as mybir
from concourse.bass_utils import run_bass_kernel_spmd

F32 = mybir.dt.float32
BF16 = mybir.dt.bfloat16
AF = mybir.ActivationFunctionType
ALU = mybir.AluOpType

L = 2
D = 2048
DIN = 5632
DFF = 5632
TOK = 4096
NT = 8
QA0, KA0, VA0, GB0, XB0, QC0, KC0, VC0 = 0, 768, 1536, 2304, 2816, 3328, 4096, 4864
NEG = -30000.0
EPS = 1e-6
NLAYERS = int(os.environ.get("MK_LAYERS", "2"))
DEBUG = int(os.environ.get("MK_DEBUG", "0"))


class Prog:
    ENG = ('pe', 'act', 'dve', 'pool', 'sp')
    CE = ('pe', 'act', 'dve', 'pool')

    def __init__(self, nc, es):
        self.nc = nc
        self.es = es
        self.streams = {e: [] for e in self.ENG}
        self.cnt = {e: 0 for e in self.ENG}
        self.sem = {e: es.enter_context(nc.semaphore("c_" + e)) for e in self.CE}
        self.dsem = {}
        self.dcnt = {}
        self.waited = {e: {} for e in self.ENG}
        self.lastw = {}
        self.readers = {}
        self.sim = {e: [] for e in self.ENG}
        self.simval = {}

    def _need(self, eng, tok, kind):
        src = tok[0]
        if src == eng and src == 'pe':
            return False
        return True

    def _deps(self, eng, reads, writes, dmakey=None):
        deps = []
        for b in reads:
            t = self.lastw.get(b)
            if t is not None and self._need(eng, t, 'raw'):
                deps.append(t)
        for b in writes:
            t = self.lastw.get(b)
            if t is not None:
                if not (dmakey is not None and t[1] == ('d', dmakey)) and self._need(eng, t, 'waw'):
                    deps.append(t)
            for t in self.readers.get(b, ()):
                if self._need(eng, t, 'war'):
                    deps.append(t)
        best = {}
        for (src, sk, val) in deps:
            if val > best.get(sk, 0):
                best[sk] = val
        out = []
        w = self.waited[eng]
        for sk, val in best.items():
            if w.get(sk, 0) >= val:
                continue
            w[sk] = val
            out.append((sk, val))
        return out

    def _semh(self, sk):
        return self.sem[sk[1]] if sk[0] == 'c' else self.dsem[sk[1]]

    def _record(self, tok, reads, writes):
        for b in reads:
            self.readers.setdefault(b, []).append(tok)
        for b in writes:
            self.lastw[b] = tok
            self.readers[b] = []

    def group(self, eng, fns, reads=(), writes=()):
        dl = self._deps(eng, reads, writes)
        self.sim[eng].append((dl, ('c', eng), 1))
        waits = [(self._semh(sk), v) for sk, v in dl]
        self.cnt[eng] += 1
        tok = (eng, ('c', eng), self.cnt[eng])
        sem = self.sem[eng]

        def emit(e, waits=waits, fns=fns, sem=sem):
            for s, v in waits:
                e.wait_ge(s, v)
            for f in fns[:-1]:
                f(e)
            fns[-1](e).then_inc(sem, 1)
        self.streams[eng].append(emit)
        self._record(tok, reads, writes)

    def op(self, eng, fn, reads=(), writes=()):
        self.group(eng, [fn], reads, writes)

    def dma(self, q, key, fn, reads=(), writes=()):
        if key not in self.dsem:
            self.dsem[key] = self.es.enter_context(self.nc.semaphore("d_%d" % len(self.dsem)))
            self.dcnt[key] = 0
        dl = self._deps(q, reads, writes, dmakey=key)
        self.sim[q].append((dl, ('d', key), 16))
        waits = [(self._semh(sk), v) for sk, v in dl]
        self.dcnt[key] += 16
        tok = ('dma', ('d', key), self.dcnt[key])
        sem = self.dsem[key]

        def emit(e, waits=waits, fn=fn, sem=sem):
            for s, v in waits:
                e.wait_ge(s, v)
            fn(e).then_inc(sem, 16)
        self.streams[q].append(emit)
        self._record(tok, reads, writes)

    def barrier(self):
        allw = [(('c', e), self.cnt[e]) for e in self.CE if self.cnt[e] > 0]
        allw += [(('d', k), v) for k, v in self.dcnt.items() if v > 0]
        for eng in self.ENG:
            w = self.waited[eng]
            ws = []
            dl = []
            for sk, v in allw:
                if w.get(sk, 0) >= v:
                    continue
                w[sk] = v
                ws.append((self._semh(sk), v))
                dl.append((sk, v))
            self.sim[eng].append((dl, None, 0))

            def emit(e, ws=ws):
                for s, v in ws:
                    e.wait_ge(s, v)
            self.streams[eng].append(emit)
        self.lastw = {}
        self.readers = {}

    def check_deadlock(self):
        pos = {e: 0 for e in self.ENG}
        val = self.simval
        prog = True
        while prog:
            prog = False
            for e in self.ENG:
                q = self.sim[e]
                while pos[e] < len(q):
                    dl, sk, inc = q[pos[e]]
                    if all(val.get(k, 0) >= v for k, v in dl):
                        if sk is not None:
                            val[sk] = val.get(sk, 0) + inc
                        pos[e] += 1
                        prog = True
                    else:
                        break
        stuck = {e: (pos[e], len(self.sim[e])) for e in self.ENG if pos[e] < len(self.sim[e])}
        if stuck:
            msg = []
            for e in stuck:
                dl, sk, inc = self.sim[e][pos[e]]
                msg.append((e, pos[e], [(k, v, val.get(k, 0)) for k, v in dl if val.get(k, 0) < v]))
            raise RuntimeError("DEADLOCK in program order: %r" % (msg,))
        self.sim = {e: [] for e in self.ENG}

    def emit_all(self):
        self.barrier()
        self.check_deadlock()
        nc = self.nc
        st = self.streams
        with nc.Block() as block:
            @block.tensor
            def _(e):
                for f in st['pe']:
                    f(e)

            @block.scalar
            def _(e):
                for f in st['act']:
                    f(e)

            @block.vector
            def _(e):
                for f in st['dve']:
                    f(e)

            @block.gpsimd
            def _(e):
                for f in st['pool']:
                    f(e)

            @block.sync
            def _(e):
                for f in st['sp']:
                    f(e)
        self.streams = {e: [] for e in self.ENG}


class Rot:
    def __init__(self, items):
        self.items = list(items)
        self.i = 0

    def next(self):
        v = self.items[self.i % len(self.items)]
        self.i += 1
        return v


def chunk_kind(cc):
    col = cc * 128
    if col < VA0:
        return ('fb', col)
    if col < GB0:
        return ('v', col - VA0)
    if col < QC0:
        return ('ff', col - GB0)
    if col < VC0:
        return ('fb', col)
    return ('v', 768 + col - VC0)


def build_program():
    nc = bass.Bass("TRN2", target_bir_lowering=False)

    def din(name, shape, dt=F32):
        return nc.dram_tensor(name, shape, dt, kind="ExternalInput").ap()

    def dscr(name, shape, dt):
        kind = "ExternalOutput" if DEBUG else "Internal"
        return nc.dram_tensor(name, shape, dt, kind=kind).ap()

    xin = din("xin", [TOK, D])
    w_in = din("w_in", [L, D, DIN])
    w_out = din("w_out", [L, 1536, D])
    w_f1 = din("w_ffn_in", [L, D, 2 * DFF])
    w_f2 = din("w_ffn_out", [L, DFF, D])
    g1row = din("g1row", [L, 128, D])
    g2row = din("g2row", [L, 128, D])
    gfrow = din("gfrow", [128, D])
    rgw = din("rgw", [L, 16, 128, 128])
    rgv = din("rgv", [128, L, 4, 11])
    flag = din("flag", [128, 1])
    eba = din("eba", [4, 128, 25, 128])
    rawc = din("rawc", [L, 12, 128, 7, 128])
    qaa = din("qaa", [64, TOK], BF16)
    qac = din("qac", [64, TOK], BF16)
    kaug = din("kaug", [64, TOK], BF16)
    identd = din("ident", [128, 128])
    yout = nc.dram_tensor("yout", [TOK, D], F32, kind="ExternalOutput").ap()

    sf = dscr("sf", [DIN, TOK], BF16)
    sgx = dscr("sgx", [1024, TOK], F32)
    sv = dscr("sv", [TOK, 1536], BF16)
    smix = dscr("smix", [1536, TOK], BF16)
    sx = dscr("sx", [TOK, D], F32)
    sx1 = dscr("sx1", [TOK, D], F32)
    shn = dscr("shn", [D, TOK], BF16)

    with ExitStack() as es0:
        P = Prog(nc, es0)
        ps = es0.enter_context(nc.psum_tensor("ps", [128, 8, 512], F32))
        ident = es0.enter_context(nc.sbuf_tensor("ident_sb", [128, 128], F32))
        flg = es0.enter_context(nc.sbuf_tensor("flg_sb", [128, 1], F32))
        P.dma('sp', 'ident', lambda e: e.dma_start(out=ident[:], in_=identd), writes=['ident'])
        P.dma('sp', 'flg', lambda e: e.dma_start(out=flg[:], in_=flag), writes=['flg'])
        P.emit_all()

        def PSK(b):
            return ('ps', b)

        def phase1(l, xsrc):
            with ExitStack() as es:
                def sb(name, shape, dt):
                    return es.enter_context(nc.sbuf_tensor("L%d_" % l + name, shape, dt))
                xtok = sb("p1_xtok", [128, 4, D], F32)
                xs = [sb("p1_xs%d" % i, [128, D], F32) for i in range(2)]
                grow = sb("p1_grow", [128, D], F32)
                junk = sb("p1_junk", [128, D], BF16)
                xnT = [sb("p1_xnT%d" % i, [128, 16, 1024], BF16) for i in range(2)]
                wb = [sb("p1_wb%d" % i, [128, 16, 512], BF16) for i in range(3)]
                stb = [sb("p1_stb%d" % i, [128, 512], BF16) for i in range(4)]
                stf = [sb("p1_stf%d" % i, [128, 512], F32) for i in range(2)]
                stat = sb("p1_stat", [128, 4, 4], F32)
                P.dma('sp', 'grow', lambda e: e.dma_start(out=grow[:], in_=g1row[l]), writes=['grow'])
                tb = Rot([0, 1])
                pb = Rot([2, 3, 4, 5, 6, 7])
                wr = Rot([0, 1, 2])
                sbr = Rot([0, 1, 2, 3])
                sfr = Rot([0, 1])
                evr = Rot(['act', 'dve'])
                w_l = w_in[l].rearrange("(k p) c -> p k c", p=128)

                def load_x(i, hf):
                    for s in range(4):
                        r0 = (i * 8 + hf * 4 + s) * 128
                        P.dma('sp', ('xtok', s), lambda e, s=s, r0=r0: e.dma_start(out=xtok[:, s, :], in_=xsrc[r0:r0 + 128, :]),
                              writes=[('xtok', s)])

                def evac(eng, out_ap, in_ap, reads, writes):
                    if eng == 'act':
                        P.op('act', lambda e: e.activation(out=out_ap, in_=in_ap, func=AF.Copy), reads, writes)
                    else:
                        P.op('dve', lambda e: e.tensor_copy(out=out_ap, in_=in_ap), reads, writes)

                def norm_T(i, hf):
                    slot = i % 2
                    for s in range(4):
                        xsl = xs[s % 2]
                        xk = ('xs', s % 2)
                        st = stat[:, s, :]
                        s8 = hf * 4 + s
                        P.op('act', lambda e, s=s, st=st: e.activation(out=junk[:], in_=xtok[:, s, :], func=AF.Square, accum_out=st[:, 0:1]),
                             reads=[('xtok', s)], writes=['junk', ('st', s, 0)])
                        P.op('dve', lambda e, st=st: e.tensor_scalar(out=st[:, 1:2], in0=st[:, 0:1], scalar1=1.0 / D, scalar2=EPS, op0=ALU.mult, op1=ALU.add),
                             reads=[('st', s, 0)], writes=[('st', s, 1)])
                        P.op('act', lambda e, st=st: e.activation(out=st[:, 2:3], in_=st[:, 1:2], func=AF.Sqrt),
                             reads=[('st', s, 1)], writes=[('st', s, 2)])
                        P.op('dve', lambda e, st=st: e.reciprocal(out=st[:, 3:4], in_=st[:, 2:3]),
                             reads=[('st', s, 2)], writes=[('st', s, 3)])
                        P.op('dve', lambda e, s=s, st=st, xsl=xsl: e.scalar_tensor_tensor(out=xsl[:], in0=xtok[:, s, :], scalar=st[:, 3:4], in1=grow[:], op0=ALU.mult, op1=ALU.mult),
                             reads=[('xtok', s), ('st', s, 3), 'grow'], writes=[xk])
                        for kg in range(4):
                            b = tb.next()
                            fns = []
                            for kk in range(4):
                                k = kg * 4 + kk
                                fns.append(lambda e, b=b, kk=kk, k=k, xsl=xsl: e.transpose(out=ps[:, b, kk * 128:(kk + 1) * 128], in_=xsl[:, k * 128:(k + 1) * 128], identity=ident[:]))
                            P.group('pe', fns, reads=[xk, 'ident'], writes=[PSK(b)])
                            out_ap = xnT[slot][:, kg * 4:(kg + 1) * 4, s8 * 128:(s8 + 1) * 128]
                            in_ap = ps[:, b, :].rearrange("p (a c) -> p a c", a=4)
                            evac(evr.next(), out_ap, in_ap, [PSK(b)], [('xnT', slot, s8)])

                def proj(i):
                    slot = i % 2
                    xk = [('xnT', slot, s) for s in range(8)]
                    for g in range(11):
                        if i + 1 < 4 and g == 3:
                            norm_T(i + 1, 0)
                            load_x(i + 1, 1)
                        if i + 1 < 4 and g == 7:
                            norm_T(i + 1, 1)
                            if i + 2 < 4:
                                load_x(i + 2, 0)
                        ws = wr.next()
                        wt = wb[ws]
                        P.dma('pool', ('wb', ws), lambda e, wt=wt, g=g: e.dma_start(out=wt[:], in_=w_l[:, :, g * 512:(g + 1) * 512]),
                              writes=[('wb', ws)])
                        kinds = [chunk_kind(g * 4 + c) for c in range(4)]
                        c = 0
                        while c < 4:
                            kd, off = kinds[c]
                            if kd == 'v':
                                n = 1
                                while c + n < 4 and kinds[c + n][0] == 'v':
                                    n += 1
                                ncol = n * 128
                                for s in range(8):
                                    b = pb.next()
                                    fns = [(lambda e, b=b, k=k, s=s, c=c, ncol=ncol, wt=wt: e.matmul(ps[:, b, 0:ncol], lhsT=xnT[slot][:, k, s * 128:(s + 1) * 128], rhs=wt[:, k, c * 128:c * 128 + ncol], start=(k == 0), stop=(k == 15))) for k in range(16)]
                                    P.group('pe', fns, reads=[xk[s], ('wb', ws)], writes=[PSK(b)])
                                    ss = sbr.next()
                                    evac(evr.next(), stb[ss][:, 0:ncol], ps[:, b, 0:ncol], [PSK(b)], [('stb', ss)])
                                    r0 = (i * 8 + s) * 128
                                    P.dma('sp', ('stb', ss), lambda e, ss=ss, r0=r0, off=off, ncol=ncol: e.dma_start(out=sv[r0:r0 + 128, off:off + ncol], in_=stb[ss][:, 0:ncol]),
                                          reads=[('stb', ss)])
                                c += n
                            else:
                                for hf in range(2):
                                    b = pb.next()
                                    t0_ = (i * 2 + hf) * 512
                                    fns = [(lambda e, b=b, k=k, c=c, wt=wt, hf=hf: e.matmul(ps[:, b, :], lhsT=wt[:, k, c * 128:(c + 1) * 128], rhs=xnT[slot][:, k, hf * 512:(hf + 1) * 512], start=(k == 0), stop=(k == 15))) for k in range(16)]
                                    P.group('pe', fns, reads=xk[hf * 4:hf * 4 + 4] + [('wb', ws)], writes=[PSK(b)])
                                    if kd == 'fb':
                                        ss = sbr.next()
                                        evac(evr.next(), stb[ss][:], ps[:, b, :], [PSK(b)], [('stb', ss)])
                                        P.dma('sp', ('stb', ss), lambda e, ss=ss, off=off, t0_=t0_: e.dma_start(out=sf[off:off + 128, t0_:t0_ + 512], in_=stb[ss][:]),
                                              reads=[('stb', ss)])
                                    else:
                                        ss = sfr.next()
                                        evac(evr.next(), stf[ss][:], ps[:, b, :], [PSK(b)], [('stf', ss)])
                                        P.dma('sp', ('stf', ss), lambda e, ss=ss, off=off, t0_=t0_: e.dma_start(out=sgx[off:off + 128, t0_:t0_ + 512], in_=stf[ss][:]),
                                              reads=[('stf', ss)])
                                c += 1

                load_x(0, 0)
                norm_T(0, 0)
                load_x(0, 1)
                norm_T(0, 1)
                load_x(1, 0)
                for i in range(4):
                    proj(i)
                P.emit_all()

        def c_tiles(R):
            lo, hi = 99, -99
            for kind in ('s', 'p'):
                for r in (2 * R, 2 * R + 1):
                    if kind == 's':
                        rs = min(max(r - 4, 0), 56)
                    else:
                        base = (r // 32) * 32
                        rs = base + min(max(r % 32 - 4, 0), 24)
                    lo = min(lo, rs // 2 - R)
                    hi = max(hi, (rs + 7) // 2 - R)
            return lo, hi

        def phase2(l):
            with ExitStack() as es:
                def sb(name, shape, dt):
                    return es.enter_context(nc.sbuf_tensor("L%d_" % l + name, shape, dt))
                wg = sb("rg_w", [128, 16, 128], BF16)
                vec = sb("rg_vec", [128, 4, 11], F32)
                dv = sb("rg_dv", [128, 4, 12], F32)
                qtr = sb("rg_qtr", [128, 1], F32)
                xbs = [sb("rg_xb%d" % i, [128, 2052], F32) for i in range(2)]
                xc = sb("rg_xc", [128, TOK], F32)
                xcb = sb("rg_xcb", [128, TOK], BF16)
                hf = sb("rg_hf", [128, TOK], F32)
                hbt = [sb("rg_hb%d" % i, [128, 512], F32) for i in range(2)]
                gbt = [sb("rg_gb%d" % i, [128, 512], F32) for i in range(2)]
                T = [sb("rg_t%d" % i, [128, 512], F32) for i in range(6)]
                C = [sb("rg_c%d" % i, [128, 512], F32) for i in range(3)]
                ob = [sb("rg_ob%d" % i, [128, 512], BF16) for i in range(2)]

                def rglru_gen():
                    P.dma('pool', 'rg_w', lambda e: e.dma_start(out=wg[:], in_=rgw[l].rearrange("m p n -> p m n")), writes=['rg_w'])
                    P.dma('sp', 'rg_vec', lambda e: e.dma_start(out=vec[:], in_=rgv[:, l, :, :]), writes=['rg_vec'])
                    P.op('pool', lambda e: e.tensor_scalar(out=dv[:, :, 0:4], in0=vec[:, :, 5:9], scalar1=0.5, scalar2=None, op0=ALU.mult), reads=['rg_vec'], writes=['dv_a'])
                    P.op('act', lambda e: e.activation(out=dv[:, :, 8:10], in_=vec[:, :, 9:11], func=AF.Exp, scale=-1.0), reads=['rg_vec'], writes=['dv_t'])
                    P.op('act', lambda e: e.activation(out=dv[:, :, 10:12], in_=dv[:, :, 8:10], func=AF.Ln, bias=1.0), reads=['dv_t'], writes=['dv_s'])
                    P.op('pool', lambda e: e.tensor_scalar(out=dv[:, :, 4:6], in0=dv[:, :, 10:12], scalar1=-4.0, scalar2=None, op0=ALU.mult), reads=['dv_s'], writes=['dv_c'])
                    P.op('pool', lambda e: e.tensor_scalar(out=dv[:, :, 6:8], in0=dv[:, :, 10:12], scalar1=-8.0, scalar2=None, op0=ALU.mult), reads=['dv_s'], writes=['dv_c2'])
                    P.op('pool', lambda e: e.memset(qtr[:], 0.25), writes=['half'])
                    DVK = ['dv_a', 'dv_c', 'dv_c2', 'rg_vec']
                    TK = lambda n: ('rgT', n)
                    CK = lambda n: ('rgC', n)
                    rgb = Rot([7])
                    obr = Rot([0, 1])
                    yield
                    for c in range(4):
                        for sgi in range(2):
                            P.op('pool', lambda e, sgi=sgi: e.memset(xbs[sgi][:, 0:2], 0.0), writes=[('xb', sgi)])
                            P.op('pool', lambda e, sgi=sgi: e.memset(xbs[sgi][:, 2050:2052], 0.0), writes=[('xb', sgi)])
                            P.dma('sp', ('xb', sgi), lambda e, sgi=sgi, c=c: e.dma_start(out=xbs[sgi][:, 2:2050], in_=sgx[512 + c * 128:512 + (c + 1) * 128, sgi * 2048:(sgi + 1) * 2048]),
                                  writes=[('xb', sgi)])
                        yield
                        P.op('pool', lambda e: e.tensor_scalar(out=xbs[0][:, 2050:2051], in0=xbs[1][:, 2:3], scalar1=flg[:, 0:1], scalar2=None, op0=ALU.mult),
                             reads=[('xb', 1), 'flg'], writes=[('xb', 0)])
                        P.op('pool', lambda e: e.tensor_scalar(out=xbs[1][:, 0:2], in0=xbs[0][:, 2048:2050], scalar1=flg[:, 0:1], scalar2=None, op0=ALU.mult),
                             reads=[('xb', 0), 'flg'], writes=[('xb', 1)])
                        for sgi in range(2):
                            xo = xc[:, sgi * 2048:(sgi + 1) * 2048]
                            P.op('pool', lambda e, sgi=sgi, xo=xo, c=c: e.tensor_scalar(out=xo, in0=xbs[sgi][:, 0:2048], scalar1=vec[:, c, 0:1], scalar2=vec[:, c, 4:5], op0=ALU.mult, op1=ALU.add),
                                 reads=[('xb', sgi), 'rg_vec'], writes=[('xc', sgi)])
                            for j in range(1, 4):
                                P.op('dve', lambda e, sgi=sgi, xo=xo, c=c, j=j: e.scalar_tensor_tensor(out=xo, in0=xbs[sgi][:, j:j + 2048], scalar=vec[:, c, j:j + 1], in1=xo, op0=ALU.mult, op1=ALU.add),
                                     reads=[('xb', sgi), 'rg_vec', ('xc', sgi)], writes=[('xc', sgi)])
                                yield
                            P.op('pool', lambda e, sgi=sgi, xo=xo: e.tensor_copy(out=xcb[:, sgi * 2048:(sgi + 1) * 2048], in_=xo),
                                 reads=[('xc', sgi)], writes=[('xcb', sgi)])
                            yield
                        for d in range(2):
                            for step in range(8):
                                t = step if d == 0 else 7 - step
                                sgi = t // 4
                                cols = slice(t * 512, (t + 1) * 512)
                                gs = step % 2
                                if d == 1:
                                    P.dma('sp', ('gbt', gs), lambda e, gs=gs, c=c, cols=cols: e.dma_start(out=gbt[gs][:], in_=sgx[c * 128:(c + 1) * 128, cols]), writes=[('gbt', gs)])
                                br = rgb.next()
                                bi = rgb.next()
                                P.op('pe', lambda e, br=br, d=d, c=c, cols=cols: e.matmul(ps[:, br, :], lhsT=wg[:, d * 8 + 0 * 4 + c, :], rhs=xcb[:, cols], start=True, stop=True),
                                     reads=['rg_w', ('xcb', sgi)], writes=[PSK(br)])
                                yield
                                P.op('act', lambda e, br=br, d=d, c=c: e.activation(out=T[0][:], in_=ps[:, br, :], func=AF.Tanh, scale=0.5, bias=dv[:, c, d:d + 1]),
                                     reads=[PSK(br)] + DVK, writes=[TK(0)])
                                yield
                                P.op('pe', lambda e, bi=bi, d=d, c=c, cols=cols: e.matmul(ps[:, bi, :], lhsT=wg[:, d * 8 + 1 * 4 + c, :], rhs=xcb[:, cols], start=True, stop=True),
                                     reads=['rg_w', ('xcb', sgi)], writes=[PSK(bi)])
                                yield
                                P.op('act', lambda e, bi=bi, d=d, c=c: e.activation(out=T[1][:], in_=ps[:, bi, :], func=AF.Tanh, scale=0.5, bias=dv[:, c, 2 + d:3 + d]),
                                     reads=[PSK(bi)] + DVK, writes=[TK(1)])
                                P.op('act', lambda e, d=d, c=c: e.activation(out=T[2][:], in_=T[0][:], func=AF.Exp, scale=dv[:, c, 4 + d:5 + d], bias=dv[:, c, 4 + d:5 + d]),
                                     reads=[TK(0)] + DVK, writes=[TK(2)])
                                P.op('act', lambda e, d=d, c=c: e.activation(out=T[3][:], in_=T[0][:], func=AF.Exp, scale=dv[:, c, 6 + d:7 + d], bias=dv[:, c, 6 + d:7 + d]),
                                     reads=[TK(0)] + DVK, writes=[TK(3)])
                                yield
                                P.op('dve', lambda e: e.tensor_scalar(out=T[3][:], in0=T[3][:], scalar1=-1.0, scalar2=-0.99999988, op0=ALU.mult, op1=ALU.max), reads=[TK(3)], writes=[TK(3)])
                                P.op('dve', lambda e, cols=cols: e.scalar_tensor_tensor(out=T[4][:], in0=T[1][:], scalar=1.0, in1=xc[:, cols], op0=ALU.add, op1=ALU.mult),
                                     reads=[TK(1), ('xc', sgi)], writes=[TK(4)])
                                yield
                                P.op('act', lambda e: e.activation(out=T[5][:], in_=T[3][:], func=AF.Sqrt, scale=0.25, bias=qtr[:, 0:1]), reads=[TK(3), 'half'], writes=[TK(5)])
                                P.op('pool', lambda e: e.tensor_tensor(out=T[4][:], in0=T[4][:], in1=T[5][:], op=ALU.mult), reads=[TK(4), TK(5)], writes=[TK(4)])
                                if d == 0 and t == 4:
                                    P.op('pool', lambda e: e.tensor_scalar(out=T[2][:, 0:1], in0=T[2][:, 0:1], scalar1=flg[:, 0:1], scalar2=None, op0=ALU.mult), reads=[TK(2), 'flg'], writes=[TK(2)])
                                if d == 1 and t == 3:
                                    P.op('pool', lambda e: e.tensor_scalar(out=T[2][:, 511:512], in0=T[2][:, 511:512], scalar1=flg[:, 0:1], scalar2=None, op0=ALU.mult), reads=[TK(2), 'flg'], writes=[TK(2)])
                                yield
                                if d == 0:
                                    init = 0.0 if t == 0 else hf[:, t * 512 - 1:t * 512]
                                    P.op('dve', lambda e, cols=cols, init=init: e.tensor_tensor_scan(out=hf[:, cols], data0=T[2][:], data1=T[4][:], initial=init, op0=ALU.mult, op1=ALU.add),
                                         reads=[TK(2), TK(4), 'hf'], writes=['hf'])
                                    yield
                                    continue
                                hs = step % 2
                                init = 0.0 if t == 7 else hbt[1 - hs][:, 0:1]
                                P.op('dve', lambda e, hs=hs, init=init: e.tensor_tensor_scan(out=hbt[hs][:, ::-1], data0=T[2][:, ::-1], data1=T[4][:, ::-1], initial=init, op0=ALU.mult, op1=ALU.add),
                                     reads=[TK(2), TK(4), ('hbt', 1 - hs)], writes=[('hbt', hs)])
                                yield
                                P.op('act', lambda e, gs=gs: e.activation(out=C[0][:], in_=gbt[gs][:], func=AF.Square), reads=[('gbt', gs)], writes=[CK(0)])
                                P.op('pool', lambda e: e.tensor_scalar(out=C[0][:], in0=C[0][:], scalar1=0.044715, scalar2=1.0, op0=ALU.mult, op1=ALU.add), reads=[CK(0)], writes=[CK(0)])
                                P.op('pool', lambda e, gs=gs: e.tensor_tensor(out=C[0][:], in0=C[0][:], in1=gbt[gs][:], op=ALU.mult), reads=[CK(0), ('gbt', gs)], writes=[CK(0)])
                                P.op('act', lambda e: e.activation(out=C[1][:], in_=C[0][:], func=AF.Tanh, scale=0.7978845608028654), reads=[CK(0)], writes=[CK(1)])
                                yield
                                P.op('pool', lambda e, hs=hs, cols=cols: e.tensor_tensor(out=C[2][:], in0=hf[:, cols], in1=hbt[hs][:], op=ALU.add), reads=['hf', ('hbt', hs)], writes=[CK(2)])
                                P.op('pool', lambda e: e.tensor_scalar(out=C[1][:], in0=C[1][:], scalar1=1.0, scalar2=0.5, op0=ALU.add, op1=ALU.mult), reads=[CK(1)], writes=[CK(1)])
                                P.op('pool', lambda e, gs=gs: e.tensor_tensor(out=C[1][:], in0=C[1][:], in1=gbt[gs][:], op=ALU.mult), reads=[CK(1), ('gbt', gs)], writes=[CK(1)])
                                oslot = obr.next()
                                P.op('pool', lambda e, oslot=oslot: e.tensor_tensor(out=ob[oslot][:], in0=C[1][:], in1=C[2][:], op=ALU.mult), reads=[CK(1), CK(2)], writes=[('ob', oslot)])
                                P.dma('sp', ('ob', oslot), lambda e, oslot=oslot, c=c, cols=cols: e.dma_start(out=smix[256 + c * 128:256 + (c + 1) * 128, cols], in_=ob[oslot][:]),
                                      reads=[('ob', oslot)])
                                yield

                NQK = 6
                qk = [sb("at_qk%d" % i, [128, TOK], BF16) for i in range(NQK)]
                vsl = [sb("at_v%d" % i, [128, 32, 65], BF16) for i in range(4)]
                ebA = [sb("at_ebA%d" % i, [128, 25, 128], BF16) for i in range(2)]
                ebraw = [sb("at_ebr%d" % i, [128, 7, 128], F32) for i in range(2)]
                ebC = [sb("at_ebC%d" % i, [128, 7, 128], BF16) for i in range(2)]
                E = [sb("at_E%d" % i, [128, 4, 128], BF16) for i in range(6)]
                PT = [sb("at_PT%d" % i, [128, 4, 128], BF16) for i in range(6)]
                otok = [sb("at_o%d" % i, [128, 64], F32) for i in range(3)]
                rec = [sb("at_r%d" % i, [128, 1], F32) for i in range(3)]
                mst = [sb("at_m%d" % i, [64, 512], BF16) for i in range(3)]
                for i in range(4):
                    P.op('pool', lambda e, i=i: e.memset(vsl[i][:, :, 64:65], 1.0), writes=[('v', i)])
                qkr = Rot(range(NQK))
                vr = Rot(range(4))
                sbank = Rot([0, 1, 2, 6])
                obank = Rot([3, 4])
                tbank = Rot([5])
                er = Rot(range(6))
                pr = Rot(range(6))
                orr = Rot(range(3))
                mr = Rot(range(3))
                ebAr = Rot([0, 1])
                ebCr = Rot([0, 1])
                svt = sv.rearrange("(t p) c -> p t c", p=128)

                def load_entry(qrow, krow, vcol, qaug):
                    qs = qkr.next()
                    ks = qkr.next()
                    vs = vr.next()
                    P.dma('sp', ('qk', qs), lambda e: e.dma_start(out=qk[qs][0:64, :], in_=sf[qrow:qrow + 64, :]), writes=[('qk', qs)])
                    P.dma('sp', ('qk', qs), lambda e: e.dma_start(out=qk[qs][64:128, :], in_=qaug), writes=[('qk', qs)])
                    P.dma('sp', ('qk', ks), lambda e: e.dma_start(out=qk[ks][0:64, :], in_=sf[krow:krow + 64, :]), writes=[('qk', ks)])
                    P.dma('sp', ('qk', ks), lambda e: e.dma_start(out=qk[ks][64:128, :], in_=kaug), writes=[('qk', ks)])
                    P.dma('sp', ('v', vs), lambda e: e.dma_start(out=vsl[vs][:, :, 0:64], in_=svt[:, :, vcol:vcol + 64]), writes=[('v', vs)])
                    return qs, ks, vs

                heads = [('A', j) for j in range(4)] + [('C', h) for h in range(12)]
                loaded = {}
                pending = {}

                def load_head(hd):
                    kind, idx = hd
                    if kind == 'A':
                        ents = []
                        for g in range(3):
                            h = 4 * g + idx
                            ents.append(load_entry(QA0 + h * 64, KA0 + h * 64, h * 64, qaa))
                        es_ = ebAr.next()
                        P.dma('pool', ('ebA', es_), lambda e: e.dma_start(out=ebA[es_][:], in_=eba[idx]), writes=[('ebA', es_)])
                        loaded[hd] = (ents, ebA[es_], ('ebA', es_))
                    else:
                        h = idx
                        es_ = ebCr.next()
                        P.dma('sp', ('ebr', es_), lambda e: e.dma_start(out=ebraw[es_][:], in_=rawc[l, h]), writes=[('ebr', es_)])
                        ents = [load_entry(QC0 + h * 64, KC0 + h * 64, 768 + h * 64, qac)]
                        pending[hd] = lambda: P.op('act', lambda e: e.activation(out=ebC[es_][:], in_=ebraw[es_][:], func=AF.Exp), reads=[('ebr', es_)], writes=[('ebC', es_)])
                        loaded[hd] = (ents, ebC[es_], ('ebC', es_))

                def head_tiles(hd, B):
                    kind, idx = hd
                    res = []
                    if kind == 'A':
                        base = 0
                        for g, rad in enumerate((1, 2, 8)):
                            for dl in range(-rad, rad + 1):
                                kt = B + dl
                                if 0 <= kt < 32:
                                    res.append((base + dl + rad, g, kt))
                            base += 2 * rad + 1
                    else:
                        lo, hi = c_tiles(B)
                        for dl in range(lo, hi + 1):
                            kt = B + dl
                            if 0 <= kt < 32:
                                res.append((dl + 3, 0, kt))
                    return res

                chunks = []
                for hi_, hd in enumerate(heads):
                    kind, idx = hd
                    mixrow = idx * 64 if kind == 'A' else 768 + idx * 64
                    for B in range(32):
                        tl = head_tiles(hd, B)
                        runs = []
                        cur = [tl[0]]
                        for tt in tl[1:]:
                            if tt[0] == cur[-1][0] + 1 and len(cur) < 4:
                                cur.append(tt)
                            else:
                                runs.append(cur)
                                cur = [tt]
                        runs.append(cur)
                        for ri, run in enumerate(runs):
                            chunks.append(dict(hd=hd, hi=hi_, B=B, run=run, first=(ri == 0), last=(ri == len(runs) - 1),
                                               mixrow=mixrow, headstart=(B == 0 and ri == 0)))

                state = {}

                def emit_qk(ch):
                    hd = ch['hd']
                    if ch['headstart']:
                        if hd not in loaded:
                            load_head(hd)
                        if hd in pending:
                            pending.pop(hd)()
                        nxt = ch['hi'] + 1
                        if hd[0] == 'C' and nxt < len(heads) and heads[nxt] not in loaded:
                            load_head(heads[nxt])
                    ents, ebt, ebk = loaded[hd]
                    b = sbank.next()
                    ch['sb'] = b
                    B = ch['B']
                    fns = []
                    rd = set()
                    for ti, (ebi, en, kt) in enumerate(ch['run']):
                        qs, ks, vs = ents[en]
                        rd.add(('qk', qs))
                        rd.add(('qk', ks))
                        fns.append(lambda e, b=b, ti=ti, qs=qs, ks=ks, kt=kt, B=B: e.matmul(ps[:, b, ti * 128:(ti + 1) * 128], lhsT=qk[ks][:, kt * 128:(kt + 1) * 128], rhs=qk[qs][:, B * 128:(B + 1) * 128], start=True, stop=True))
                    P.group('pe', fns, reads=list(rd), writes=[PSK(b)])

                binfo = {}

                def emit_exp(ch):
                    b = ch['sb']
                    n = len(ch['run'])
                    es_ = er.next()
                    ch['es'] = es_
                    P.op('act', lambda e: e.activation(out=E[es_][:, 0:n, :], in_=ps[:, b, 0:n * 128].rearrange("p (a c) -> p a c", a=n), func=AF.Exp, scale=0.125, bias=-8.0),
                         reads=[PSK(b)], writes=[('E', es_)])

                def emit_mul(ch):
                    ents, ebt, ebk = loaded[ch['hd']]
                    n = len(ch['run'])
                    eb0 = ch['run'][0][0]
                    es_ = ch['es']
                    ps_ = pr.next()
                    ch['pt'] = ps_
                    state['mulc'] = state.get('mulc', 0) + 1
                    meng = 'dve'
                    P.op(meng, lambda e: e.tensor_tensor(out=PT[ps_][:, 0:n, :], in0=E[es_][:, 0:n, :], in1=ebt[:, eb0:eb0 + n, :], op=ALU.mult),
                         reads=[('E', es_), ebk], writes=[('PT', ps_)])

                def emit_pv(ch):
                    ents, ebt, ebk = loaded[ch['hd']]
                    n = len(ch['run'])
                    ps_ = ch['pt']
                    key = (ch['hi'], ch['B'])
                    if ch['first']:
                        binfo[key] = {'ob': obank.next()}
                    ob_ = binfo[key]['ob']
                    fns = []
                    rd = {('PT', ps_)}
                    for ti, (ebi, en, kt) in enumerate(ch['run']):
                        qs, ks, vs = ents[en]
                        rd.add(('v', vs))
                        fns.append(lambda e, ti=ti, vs=vs, kt=kt, st=(ch['first'] and ti == 0), sp=(ch['last'] and ti == n - 1): e.matmul(ps[:, ob_, 0:65], lhsT=PT[ps_][:, ti, :], rhs=vsl[vs][:, kt, :], start=st, stop=sp))
                    P.group('pe', fns, reads=list(rd), writes=[PSK(ob_)])

                def emit_fin(ch):
                    bi_ = binfo[(ch['hi'], ch['B'])]
                    ob_ = bi_['ob']
                    os_ = orr.next()
                    bi_['os'] = os_
                    P.op('dve', lambda e: e.reciprocal(out=rec[os_][:], in_=ps[:, ob_, 64:65]), reads=[PSK(ob_)], writes=[('rec', os_)])
                    P.op('dve', lambda e: e.tensor_scalar(out=otok[os_][:], in0=ps[:, ob_, 0:64], scalar1=rec[os_][:, 0:1], scalar2=None, op0=ALU.mult),
                         reads=[PSK(ob_), ('rec', os_)], writes=[('otok', os_)])

                def emit_tr(ch):
                    bi_ = binfo[(ch['hi'], ch['B'])]
                    os_ = bi_['os']
                    B = ch['B']
                    if B % 4 == 0:
                        state['tb'] = tbank.next()
                    tb_ = state['tb']
                    bi_['tb'] = tb_
                    P.op('pe', lambda e: e.transpose(out=ps[0:64, tb_, (B % 4) * 128:(B % 4 + 1) * 128], in_=otok[os_][:], identity=ident[:]),
                         reads=[('otok', os_), 'ident'], writes=[PSK(tb_)])

                def emit_ev(ch):
                    bi_ = binfo[(ch['hi'], ch['B'])]
                    tb_ = bi_['tb']
                    B = ch['B']
                    ms_ = mr.next()
                    mixrow = ch['mixrow']
                    P.op('act', lambda e: e.activation(out=mst[ms_][:], in_=ps[0:64, tb_, :], func=AF.Copy), reads=[PSK(tb_)], writes=[('mst', ms_)])
                    P.dma('sp', ('mst', ms_), lambda e: e.dma_start(out=smix[mixrow:mixrow + 64, (B - 3) * 128:(B + 1) * 128], in_=mst[ms_][:]),
                          reads=[('mst', ms_)])

                rg = rglru_gen()
                next(rg)
                KRG = 3
                cnt = 0
                for hi_ in range(len(heads)):
                    hc = [c_ for c_ in chunks if c_['hi'] == hi_]
                    n_ = len(hc)
                    for step in range(n_ + 8):
                        if 0 <= step - 7 < n_ and hc[step - 7]['last'] and hc[step - 7]['B'] % 4 == 3:
                            emit_ev(hc[step - 7])
                        if 0 <= step - 6 < n_ and hc[step - 6]['last']:
                            emit_tr(hc[step - 6])
                        if 0 <= step - 5 < n_ and hc[step - 5]['last']:
                            emit_fin(hc[step - 5])
                        if 0 <= step - 3 < n_:
                            emit_pv(hc[step - 3])
                        if 0 <= step - 2 < n_:
                            emit_mul(hc[step - 2])
                        if 0 <= step - 1 < n_:
                            emit_exp(hc[step - 1])
                        if step < n_:
                            emit_qk(hc[step])
                        cnt += 1
                        if cnt % KRG == 0:
                            next(rg, None)
                for _ in rg:
                    pass
                P.emit_all()

        def phase3a(l, xsrc):
            with ExitStack() as es:
                def sb(name, shape, dt):
                    return es.enter_context(nc.sbuf_tensor("L%d_" % l + name, shape, dt))
                xtoks = [sb("p3_xtok%d" % i, [128, 4, D], F32) for i in range(2)]
                mixTs = [sb("p3_mixT%d" % i, [128, 12, 512], BF16) for i in range(2)]
                xss = [sb("p3_xs%d" % i, [128, D], F32) for i in range(2)]
                grow = sb("p3_grow", [128, D], F32)
                hst = [sb("p3_hst%d" % i, [128, 16, 512], BF16) for i in range(2)]
                wres = sb("p3_wo", [128, 12, D], BF16)
                stat = sb("p3_stat", [128, 4, 4], F32)
                P.dma('sp', 'grow', lambda e: e.dma_start(out=grow[:], in_=g2row[l]), writes=['grow'])
                pb = Rot([0, 1, 2, 3, 4, 5])
                tb = Rot([6, 7])
                evr = Rot(['act', 'dve'])
                wo_l = w_out[l].rearrange("(k p) c -> p k c", p=128)
                smx = smix.rearrange("(c p) t -> p c t", p=128)
                shn_v = shn.rearrange("(k p) t -> p k t", p=128)

                def load_in(i):
                    sl = i % 2
                    for s in range(4):
                        r0 = (i * 4 + s) * 128
                        P.dma('sp', ('xtok', sl, s), lambda e, s=s, r0=r0, sl=sl: e.dma_start(out=xtoks[sl][:, s, :], in_=xsrc[r0:r0 + 128, :]), writes=[('xtok', sl, s)])
                    P.dma('sp', ('mixT', sl), lambda e, i=i, sl=sl: e.dma_start(out=mixTs[sl][:], in_=smx[:, :, i * 512:(i + 1) * 512]), writes=[('mixT', sl)])

                def load_x(i, s):
                    sl = i % 2
                    r0 = (i * 4 + s) * 128
                    P.dma('sp', ('xtok', sl, s), lambda e, s=s, r0=r0, sl=sl: e.dma_start(out=xtoks[sl][:, s, :], in_=xsrc[r0:r0 + 128, :]), writes=[('xtok', sl, s)])

                def load_mix(i):
                    sl = i % 2
                    P.dma('sp', ('mixT', sl), lambda e, i=i, sl=sl: e.dma_start(out=mixTs[sl][:], in_=smx[:, :, i * 512:(i + 1) * 512]), writes=[('mixT', sl)])

                def wout(i, n):
                    sl = i % 2
                    xtok = xtoks[sl]
                    mixT = mixTs[sl]
                    for s in range(4):
                        b = pb.next()
                        fns = [(lambda e, b=b, k=k, s=s, n=n: e.matmul(ps[:, b, :], lhsT=mixT[:, k, s * 128:(s + 1) * 128], rhs=wres[:, k, n * 512:(n + 1) * 512], start=(k == 0), stop=(k == 11))) for k in range(12)]
                        P.group('pe', fns, reads=[('mixT', sl), 'wres'], writes=[PSK(b)])
                        P.op('dve', lambda e, b=b, s=s, n=n: e.tensor_tensor(out=xtok[:, s, n * 512:(n + 1) * 512], in0=ps[:, b, :], in1=xtok[:, s, n * 512:(n + 1) * 512], op=ALU.add),
                             reads=[PSK(b), ('xtok', sl, s)], writes=[('xtok', sl, s)])

                def chain(i, s):
                    sl = i % 2
                    xtok = xtoks[sl]
                    xsl = xss[s % 2]
                    xsk = ('xs', s % 2)
                    r0 = (i * 4 + s) * 128
                    xk = ('xtok', sl, s)
                    P.dma('sp', ('x1o', sl, s), lambda e, s=s, r0=r0: e.dma_start(out=sx1[r0:r0 + 128, :], in_=xtok[:, s, :]), reads=[xk])
                    st = stat[:, s, :]
                    P.op('act', lambda e, s=s, st=st: e.activation(out=xsl[:], in_=xtok[:, s, :], func=AF.Square, accum_out=st[:, 0:1]),
                         reads=[xk], writes=[xsk, ('st', s, 0)])
                    P.op('dve', lambda e, st=st: e.tensor_scalar(out=st[:, 1:2], in0=st[:, 0:1], scalar1=1.0 / D, scalar2=EPS, op0=ALU.mult, op1=ALU.add),
                         reads=[('st', s, 0)], writes=[('st', s, 1)])
                    P.op('act', lambda e, st=st: e.activation(out=st[:, 2:3], in_=st[:, 1:2], func=AF.Sqrt), reads=[('st', s, 1)], writes=[('st', s, 2)])
                    P.op('dve', lambda e, st=st: e.reciprocal(out=st[:, 3:4], in_=st[:, 2:3]), reads=[('st', s, 2)], writes=[('st', s, 3)])
                    P.op('dve', lambda e, s=s, st=st: e.scalar_tensor_tensor(out=xsl[:], in0=xtok[:, s, :], scalar=st[:, 3:4], in1=grow[:], op0=ALU.mult, op1=ALU.mult),
                         reads=[xk, ('st', s, 3), 'grow'], writes=[xsk])

                def transp(i, s):
                    sl = i % 2
                    xsl = xss[s % 2]
                    xsk = ('xs', s % 2)
                    for kg in range(4):
                        b = tb.next()
                        fns = [(lambda e, b=b, kk=kk, kg=kg: e.transpose(out=ps[:, b, kk * 128:(kk + 1) * 128], in_=xsl[:, (kg * 4 + kk) * 128:(kg * 4 + kk + 1) * 128], identity=ident[:])) for kk in range(4)]
                        P.group('pe', fns, reads=[xsk, 'ident'], writes=[PSK(b)])
                        out_ap = hst[sl][:, kg * 4:(kg + 1) * 4, s * 128:(s + 1) * 128]
                        in_ap = ps[:, b, :].rearrange("p (a c) -> p a c", a=4)
                        if evr.next() == 'act':
                            P.op('act', lambda e, out_ap=out_ap, in_ap=in_ap: e.activation(out=out_ap, in_=in_ap, func=AF.Copy), reads=[PSK(b)], writes=[('hst', sl)])
                        else:
                            P.op('dve', lambda e, out_ap=out_ap, in_ap=in_ap: e.tensor_copy(out=out_ap, in_=in_ap), reads=[PSK(b)], writes=[('hst', sl)])

                for n in range(4):
                    P.dma('pool', 'wres', lambda e, n=n: e.dma_start(out=wres[:, :, n * 512:(n + 1) * 512], in_=wo_l[:, :, n * 512:(n + 1) * 512]), writes=['wres'])
                for i0 in range(2):
                    for s in range(4):
                        load_x(i0, s)
                    load_mix(i0)
                for n in range(4):
                    wout(0, n)
                for i in range(NT):
                    for s in range(4):
                        chain(i, s)
                        if i + 2 < NT:
                            load_x(i + 2, s)
                            if s == 0:
                                load_mix(i + 2)
                        if i + 1 < NT:
                            wout(i + 1, s)
                        transp(i, s)
                    P.dma('sp', ('hst', i % 2), lambda e, i=i: e.dma_start(out=shn_v[:, :, i * 512:(i + 1) * 512], in_=hst[i % 2][:]), reads=[('hst', i % 2)])
                P.emit_all()

        def phase3b(l):
            with ExitStack() as es:
                def sb(name, shape, dt):
                    return es.enter_context(nc.sbuf_tensor("L%d_" % l + name, shape, dt))
                hnT = sb("f_hnT", [128, 16, 1024], BF16)
                hT = sb("f_hT", [128, 44, 1024], BF16)
                WR = 3
                wring = [sb("f_w%d" % i, [128, 5632], BF16) for i in range(WR)]
                sg = [sb("f_sg%d" % i, [128, 512], F32) for i in range(2)]
                yTs = [sb("f_yT%d" % i, [128, 4, 1024], F32) for i in range(2)]
                xp = [sb("f_xp%d" % i, [128, 512], F32) for i in range(4)]
                wr = Rot(range(WR))
                pb = Rot([0, 1, 2, 3, 4, 5, 6, 7])
                pbo = Rot([0, 1, 2, 3, 4, 5])
                tb = Rot([6, 7])
                sgr = Rot([0, 1])
                xpr = Rot(range(4))
                w1_l = w_f1[l].rearrange("(k p) c -> p k c", p=128)
                w2_l = w_f2[l].rearrange("(f p) c -> p f c", p=128)
                shn_v = shn.rearrange("(k p) t -> p k t", p=128)
                def load_hn(j):
                    P.dma('sp', 'hnT', lambda e, j=j: e.dma_start(out=hnT[:], in_=shn_v[:, :, j * 1024:(j + 1) * 1024]), writes=['hnT'])
                load_hn(0)
                for j in range(4):
                    for f in range(44):
                        ws = wr.next()
                        wt = wring[ws][:, 0:4096].rearrange("p (k g c) -> p k g c", k=16, g=2)
                        P.dma('pool', ('w', ws), lambda e, wt=wt, f=f: e.dma_start(out=wt[:, :, 0, :], in_=w1_l[:, :, f * 128:(f + 1) * 128]), writes=[('w', ws)])
                        P.dma('pool', ('w', ws), lambda e, wt=wt, f=f: e.dma_start(out=wt[:, :, 1, :], in_=w1_l[:, :, DFF + f * 128:DFF + (f + 1) * 128]), writes=[('w', ws)])
                        for hf in range(2):
                            bg = pb.next()
                            bu = pb.next()
                            fns = [(lambda e, bg=bg, k=k, wt=wt, hf=hf: e.matmul(ps[:, bg, :], lhsT=wt[:, k, 0, :], rhs=hnT[:, k, hf * 512:(hf + 1) * 512], start=(k == 0), stop=(k == 15))) for k in range(16)]
                            P.group('pe', fns, reads=['hnT', ('w', ws)], writes=[PSK(bg)])
                            fns = [(lambda e, bu=bu, k=k, wt=wt, hf=hf: e.matmul(ps[:, bu, :], lhsT=wt[:, k, 1, :], rhs=hnT[:, k, hf * 512:(hf + 1) * 512], start=(k == 0), stop=(k == 15))) for k in range(16)]
                            P.group('pe', fns, reads=['hnT', ('w', ws)], writes=[PSK(bu)])
                            sgs = sgr.next()
                            P.op('act', lambda e, bg=bg, sgs=sgs: e.activation(out=sg[sgs][:], in_=ps[:, bg, :], func=AF.Silu), reads=[PSK(bg)], writes=[('sg', sgs)])
                            P.op('dve', lambda e, bu=bu, sgs=sgs, f=f, hf=hf: e.tensor_tensor(out=hT[:, f, hf * 512:(hf + 1) * 512], in0=sg[sgs][:], in1=ps[:, bu, :], op=ALU.mult),
                                 reads=[PSK(bu), ('sg', sgs)], writes=[('hT', f)])
                    HTK = [('hT', f) for f in range(44)]
                    if j + 1 < 4:
                        load_hn(j + 1)

                    def ffn_out_c(cg, cc):
                        c = cg * 4 + cc
                        yT = yTs[cg % 2]
                        ws = wr.next()
                        wt = wring[ws][:, 0:5632].rearrange("p (f c) -> p f c", f=44)
                        P.dma('pool', ('w', ws), lambda e, wt=wt, c=c: e.dma_start(out=wt, in_=w2_l[:, :, c * 128:(c + 1) * 128]), writes=[('w', ws)])
                        for hf in range(2):
                            b = pbo.next()
                            fns = [(lambda e, b=b, f=f, wt=wt, hf=hf: e.matmul(ps[:, b, :], lhsT=wt[:, f, :], rhs=hT[:, f, hf * 512:(hf + 1) * 512], start=(f == 0), stop=(f == 43))) for f in range(44)]
                            P.group('pe', fns, reads=HTK + [('w', ws)], writes=[PSK(b)])
                            P.op('act', lambda e, b=b, cc=cc, hf=hf, yT=yT: e.activation(out=yT[:, cc, hf * 512:(hf + 1) * 512], in_=ps[:, b, :], func=AF.Copy), reads=[PSK(b)], writes=[('yT', cg % 2, cc, hf)])

                    def tail(cg, j=j):
                        yT = yTs[cg % 2]

                        def xload(s):
                            r0 = (j * 8 + s) * 128
                            xs_ = s % 4
                            P.dma('sp', ('xp', xs_), lambda e, xs_=xs_, r0=r0, cg=cg: e.dma_start(out=xp[xs_][:], in_=sx1[r0:r0 + 128, cg * 512:(cg + 1) * 512]), writes=[('xp', xs_)])
                        for s in range(3):
                            xload(s)
                        for s in range(8):
                            r0 = (j * 8 + s) * 128
                            xs_ = s % 4
                            if s + 3 < 8:
                                xload(s + 3)
                            b = tb.next()
                            fns = [(lambda e, b=b, cc=cc, s=s, yT=yT: e.transpose(out=ps[:, b, cc * 128:(cc + 1) * 128], in_=yT[:, cc, s * 128:(s + 1) * 128], identity=ident[:])) for cc in range(4)]
                            P.group('pe', fns, reads=[('yT', cg % 2, cc, s // 4) for cc in range(4)] + ['ident'], writes=[PSK(b)])
                            P.op('dve', lambda e, b=b, xs_=xs_: e.tensor_tensor(out=xp[xs_][:], in0=ps[:, b, :], in1=xp[xs_][:], op=ALU.add),
                                 reads=[PSK(b), ('xp', xs_)], writes=[('xp', xs_)])
                            P.dma('sp', ('xp', xs_), lambda e, xs_=xs_, r0=r0, cg=cg: e.dma_start(out=sx[r0:r0 + 128, cg * 512:(cg + 1) * 512], in_=xp[xs_][:]), reads=[('xp', xs_)])
                            if s % 2 == 1:
                                yield

                    tg = None
                    for cg in range(4):
                        for cc in range(4):
                            ffn_out_c(cg, cc)
                            if cc == 0 and cg > 0:
                                tg = tail(cg - 1)
                            if tg is not None:
                                next(tg, None)
                    for _ in tail(3):
                        pass
                P.emit_all()

        def phase3c():
            with ExitStack() as es:
                def sb(name, shape, dt):
                    return es.enter_context(nc.sbuf_tensor("fin_" + name, shape, dt))
                xb_ = [sb("x%d" % i, [128, D], F32) for i in range(8)]
                gf = sb("gf", [128, D], F32)
                junk = sb("junk", [128, D], BF16)
                stat = sb("stat", [128, 8, 4], F32)
                P.dma('sp', 'gf', lambda e: e.dma_start(out=gf[:], in_=gfrow), writes=['gf'])
                def fload(t):
                    sl = t % 8
                    r0 = t * 128
                    P.dma('sp', ('fx', sl), lambda e, sl=sl, r0=r0: e.dma_start(out=xb_[sl][:], in_=sx[r0:r0 + 128, :]), writes=[('fx', sl)])
                for t in range(6):
                    fload(t)
                for t in range(32):
                    sl = t % 8
                    r0 = t * 128
                    xk = ('fx', sl)
                    st = stat[:, sl, :]
                    if t + 6 < 32:
                        fload(t + 6)
                    P.op('act', lambda e, sl=sl, st=st: e.activation(out=junk[:], in_=xb_[sl][:], func=AF.Square, accum_out=st[:, 0:1]), reads=[xk], writes=['junk', ('st', sl, 0)])
                    P.op('dve', lambda e, st=st: e.tensor_scalar(out=st[:, 1:2], in0=st[:, 0:1], scalar1=1.0 / D, scalar2=EPS, op0=ALU.mult, op1=ALU.add), reads=[('st', sl, 0)], writes=[('st', sl, 1)])
                    P.op('act', lambda e, st=st: e.activation(out=st[:, 2:3], in_=st[:, 1:2], func=AF.Sqrt), reads=[('st', sl, 1)], writes=[('st', sl, 2)])
                    P.op('dve', lambda e, st=st: e.reciprocal(out=st[:, 3:4], in_=st[:, 2:3]), reads=[('st', sl, 2)], writes=[('st', sl, 3)])
                    P.op('dve', lambda e, sl=sl, st=st: e.scalar_tensor_tensor(out=xb_[sl][:], in0=xb_[sl][:], scalar=st[:, 3:4], in1=gf[:], op0=ALU.mult, op1=ALU.mult),
                         reads=[xk, ('st', sl, 3), 'gf'], writes=[xk])
                    P.dma('sp', xk, lambda e, sl=sl, r0=r0: e.dma_start(out=yout[r0:r0 + 128, :], in_=xb_[sl][:]), reads=[xk])
                P.emit_all()

        for l in range(NLAYERS):
            xsrc = xin if l == 0 else sx
            phase1(l, xsrc)
            phase2(l)
            phase3a(l, xsrc)
            phase3b(l)
        phase3c()
    return nc


def _bf16(a):
    return np.asarray(a, dtype=np.float32).astype(ml_dtypes.bfloat16)


def _const_tables():
    slopes = 2.0 ** (-8.0 * np.arange(1, 13, dtype=np.float64) / 12.0)
    p = np.arange(128)[:, None]
    q = np.arange(128)[None, :]
    eba = np.zeros((4, 128, 25, 128), np.float32)
    for j in range(4):
        base = 0
        for g, (d, rad) in enumerate(((1, 1), (4, 2), (16, 8))):
            h = 4 * g + j
            for dl in range(-rad, rad + 1):
                delta = 128 * dl + p - q
                ok = (delta % d == 0) & (np.abs(delta) <= 64 * d)
                val = np.exp(-slopes[h] * np.abs(delta))
                eba[j, :, base + dl + rad, :] = np.where(ok, val, 0.0)
            base += 2 * rad + 1
    rows = np.arange(TOK) // 64
    kaug = (rows[None, :] == np.arange(64)[:, None]).astype(np.float32)
    return eba, kaug


def _q_aug(is_sample):
    a = np.arange(64)[:, None]
    r = (np.arange(TOK) // 64)[None, :]
    if is_sample:
        rs = np.clip(r - 4, 0, 56)
        qaa = np.zeros((64, TOK), np.float32)
    else:
        base = (r // 32) * 32
        rs = base + np.clip(r % 32 - 4, 0, 24)
        qaa = np.where((a // 32) == (r // 32), 0.0, NEG).astype(np.float32)
    qac = np.where((a >= rs) & (a < rs + 8), 0.0, NEG).astype(np.float32)
    return qaa, qac


def _rawc(na_rpb):
    p = np.arange(128)
    kr2, kc = (p // 64)[:, None], (p % 64)[:, None]
    qr2, qc = (p // 64)[None, :], (p % 64)[None, :]
    cs = np.clip(qc - 8, 0, 48)
    colok = (kc >= cs) & (kc < cs + 16)
    dc = np.clip(kc - qc + 15, 0, 30)
    out = np.full((L, 12, 128, 7, 128), NEG, np.float32)
    for dl in range(-3, 4):
        dr = 2 * dl + kr2 - qr2 + 7
        ok = colok & (dr >= 0) & (dr < 15)
        drc = np.clip(dr, 0, 14)
        g = na_rpb[:, :, drc, dc]
        out[:, :, :, dl + 3, :] = np.where(ok[None, None], g, NEG)
    return out


_NC_CACHE = {}


def kernel(x_prompt, x_sample, norm1_g, w_in, conv_w, conv_b, rg_wa, rg_ba, rg_wx, rg_bx,
           rg_lam, na_rpb, w_out, norm2_g, w_ffn_in, w_ffn_out, final_g):
    f32 = lambda a: np.ascontiguousarray(np.asarray(a, dtype=np.float32))
    x_prompt, x_sample = f32(x_prompt), f32(x_sample)
    slots = [(x_sample[0], True), (x_sample[1], True)]
    for i in range(4):
        slots.append((x_prompt[2 * i:2 * i + 2].reshape(TOK, D), False))
    slots.append(slots[-1])
    slots.append(slots[-1])

    eba, kaug = _const_tables()
    rawc = _rawc(f32(na_rpb))
    qa = {True: _q_aug(True), False: _q_aug(False)}
    rgw = np.zeros((L, 2, 2, 4, 128, 128), np.float32)
    for kind, w in enumerate((f32(rg_wa), f32(rg_wx))):
        for c in range(4):
            for half in range(2):
                rgw[:, :, kind, c, half * 64:(half + 1) * 64, half * 64:(half + 1) * 64] = w[:, :, 2 * c + half]
    rgw = rgw.reshape(L, 16, 128, 128)
    rgv = np.zeros((128, L, 4, 11), np.float32)

    def chan(v):
        return v.reshape(L, 4, 128).transpose(2, 0, 1)
    cw = f32(conv_w)
    for j in range(4):
        rgv[:, :, :, j] = chan(cw[:, j])
    rgv[:, :, :, 4] = chan(f32(conv_b))
    for d in range(2):
        rgv[:, :, :, 5 + d] = chan(f32(rg_ba)[:, d])
        rgv[:, :, :, 7 + d] = chan(f32(rg_bx)[:, d])
        rgv[:, :, :, 9 + d] = chan(f32(rg_lam)[:, d])
    bc = lambda v: np.ascontiguousarray(np.broadcast_to(f32(v)[..., None, :], v.shape[:-1] + (128, D)))
    common = {
        "w_in": f32(w_in), "w_out": f32(w_out), "w_ffn_in": f32(w_ffn_in), "w_ffn_out": f32(w_ffn_out),
        "g1row": bc(norm1_g), "g2row": bc(norm2_g), "gfrow": bc(final_g),
        "rgw": rgw, "rgv": rgv, "eba": eba, "rawc": rawc, "kaug": _bf16(kaug),
        "ident": np.eye(128, dtype=np.float32),
    }
    in_maps = []
    for xs, is_s in slots:
        m = dict(common)
        m["xin"] = np.ascontiguousarray(xs)
        m["flag"] = np.full((128, 1), 1.0 if is_s else 0.0, np.float32)
        m["qaa"] = _bf16(qa[is_s][0])
        m["qac"] = _bf16(qa[is_s][1])
        in_maps.append(m)
    if "nc" not in _NC_CACHE:
        _NC_CACHE["nc"] = build_program()
    nc = _NC_CACHE["nc"]
    res = run_bass_kernel_spmd(nc, in_maps, core_ids=list(range(8)))
    outs = [np.asarray(r["yout"], dtype=np.float32) for r in res.results]
    if DEBUG:
        kernel.debug = res.results
    y_sample = np.stack([outs[0], outs[1]], axis=0)
    y_prompt = np.concatenate([outs[2 + i].reshape(2, 2048, D) for i in range(4)], axis=0)
    return (y_prompt, y_sample)
```

```python
import os
from contextlib import ExitStack
import numpy as np
import ml_dtypes
import concourse.bass as bass
import concourse.mybir as mybir
from concourse.bass_utils import run_bass_kernel_spmd

F32 = mybir.dt.float32
BF16 = mybir.dt.bfloat16
AF = mybir.ActivationFunctionType
ALU = mybir.AluOpType

L = 2
D = 2048
DIN = 5632
DFF = 5632
TOK = 4096
NT = 8
QA0, KA0, VA0, GB0, XB0, QC0, KC0, VC0 = 0, 768, 1536, 2304, 2816, 3328, 4096, 4864
NEG = -30000.0
EPS = 1e-6
NLAYERS = int(os.environ.get("MK_LAYERS", "2"))
DEBUG = int(os.environ.get("MK_DEBUG", "0"))


class Prog:
    ENG = ('pe', 'act', 'dve', 'pool', 'sp')
    CE = ('pe', 'act', 'dve', 'pool')

    def __init__(self, nc, es):
        self.nc = nc
        self.es = es
        self.streams = {e: [] for e in self.ENG}
        self.cnt = {e: 0 for e in self.ENG}
        self.sem = {e: es.enter_context(nc.semaphore("c_" + e)) for e in self.CE}
        self.dsem = {}
        self.dcnt = {}
        self.waited = {e: {} for e in self.ENG}
        self.lastw = {}
        self.readers = {}
        self.sim = {e: [] for e in self.ENG}
        self.simval = {}

    def _need(self, eng, tok, kind):
        src = tok[0]
        if src == eng and src == 'pe':
            return False
        return True

    def _deps(self, eng, reads, writes, dmakey=None):
        deps = []
        for b in reads:
            t = self.lastw.get(b)
            if t is not None and self._need(eng, t, 'raw'):
                deps.append(t)
        for b in writes:
            t = self.lastw.get(b)
            if t is not None:
                if not (dmakey is not None and t[1] == ('d', dmakey)) and self._need(eng, t, 'waw'):
                    deps.append(t)
            for t in self.readers.get(b, ()):
                if self._need(eng, t, 'war'):
                    deps.append(t)
        best = {}
        for (src, sk, val) in deps:
            if val > best.get(sk, 0):
                best[sk] = val
        out = []
        w = self.waited[eng]
        for sk, val in best.items():
            if w.get(sk, 0) >= val:
                continue
            w[sk] = val
            out.append((sk, val))
        return out

    def _semh(self, sk):
        return self.sem[sk[1]] if sk[0] == 'c' else self.dsem[sk[1]]

    def _record(self, tok, reads, writes):
        for b in reads:
            self.readers.setdefault(b, []).append(tok)
        for b in writes:
            self.lastw[b] = tok
            self.readers[b] = []

    def group(self, eng, fns, reads=(), writes=()):
        dl = self._deps(eng, reads, writes)
        self.sim[eng].append((dl, ('c', eng), 1))
        waits = [(self._semh(sk), v) for sk, v in dl]
        self.cnt[eng] += 1
        tok = (eng, ('c', eng), self.cnt[eng])
        sem = self.sem[eng]

        def emit(e, waits=waits, fns=fns, sem=sem):
            for s, v in waits:
                e.wait_ge(s, v)
            for f in fns[:-1]:
                f(e)
            fns[-1](e).then_inc(sem, 1)
        self.streams[eng].append(emit)
        self._record(tok, reads, writes)

    def op(self, eng, fn, reads=(), writes=()):
        self.group(eng, [fn], reads, writes)

    def dma(self, q, key, fn, reads=(), writes=()):
        if key not in self.dsem:
            self.dsem[key] = self.es.enter_context(self.nc.semaphore("d_%d" % len(self.dsem)))
            self.dcnt[key] = 0
        dl = self._deps(q, reads, writes, dmakey=key)
        self.sim[q].append((dl, ('d', key), 16))
        waits = [(self._semh(sk), v) for sk, v in dl]
        self.dcnt[key] += 16
        tok = ('dma', ('d', key), self.dcnt[key])
        sem = self.dsem[key]

        def emit(e, waits=waits, fn=fn, sem=sem):
            for s, v in waits:
                e.wait_ge(s, v)
            fn(e).then_inc(sem, 16)
        self.streams[q].append(emit)
        self._record(tok, reads, writes)

    def barrier(self):
        allw = [(('c', e), self.cnt[e]) for e in self.CE if self.cnt[e] > 0]
        allw += [(('d', k), v) for k, v in self.dcnt.items() if v > 0]
        for eng in self.ENG:
            w = self.waited[eng]
            ws = []
            dl = []
            for sk, v in allw:
                if w.get(sk, 0) >= v:
                    continue
                w[sk] = v
                ws.append((self._semh(sk), v))
                dl.append((sk, v))
            self.sim[eng].append((dl, None, 0))

            def emit(e, ws=ws):
                for s, v in ws:
                    e.wait_ge(s, v)
            self.streams[eng].append(emit)
        self.lastw = {}
        self.readers = {}

    def check_deadlock(self):
        pos = {e: 0 for e in self.ENG}
        val = self.simval
        prog = True
        while prog:
            prog = False
            for e in self.ENG:
                q = self.sim[e]
                while pos[e] < len(q):
                    dl, sk, inc = q[pos[e]]
                    if all(val.get(k, 0) >= v for k, v in dl):
                        if sk is not None:
                            val[sk] = val.get(sk, 0) + inc
                        pos[e] += 1
                        prog = True
                    else:
                        break
        stuck = {e: (pos[e], len(self.sim[e])) for e in self.ENG if pos[e] < len(self.sim[e])}
        if stuck:
            msg = []
            for e in stuck:
                dl, sk, inc = self.sim[e][pos[e]]
                msg.append((e, pos[e], [(k, v, val.get(k, 0)) for k, v in dl if val.get(k, 0) < v]))
            raise RuntimeError("DEADLOCK in program order: %r" % (msg,))
        self.sim = {e: [] for e in self.ENG}

    def emit_all(self):
        self.barrier()
        self.check_deadlock()
        nc = self.nc
        st = self.streams
        with nc.Block() as block:
            @block.tensor
            def _(e):
                for f in st['pe']:
                    f(e)

            @block.scalar
            def _(e):
                for f in st['act']:
                    f(e)

            @block.vector
            def _(e):
                for f in st['dve']:
                    f(e)

            @block.gpsimd
            def _(e):
                for f in st['pool']:
                    f(e)

            @block.sync
            def _(e):
                for f in st['sp']:
                    f(e)
        self.streams = {e: [] for e in self.ENG}


class Rot:
    def __init__(self, items):
        self.items = list(items)
        self.i = 0

    def next(self):
        v = self.items[self.i % len(self.items)]
        self.i += 1
        return v


def chunk_kind(cc):
    col = cc * 128
    if col < VA0:
        return ('fb', col)
    if col < GB0:
        return ('v', col - VA0)
    if col < QC0:
        return ('ff', col - GB0)
    if col < VC0:
        return ('fb', col)
    return ('v', 768 + col - VC0)


def build_program():
    nc = bass.Bass("TRN2", target_bir_lowering=False)

    def din(name, shape, dt=F32):
        return nc.dram_tensor(name, shape, dt, kind="ExternalInput").ap()

    def dscr(name, shape, dt):
        kind = "ExternalOutput" if DEBUG else "Internal"
        return nc.dram_tensor(name, shape, dt, kind=kind).ap()

    xin = din("xin", [TOK, D])
    w_in = din("w_in", [L, D, DIN])
    w_out = din("w_out", [L, 1536, D])
    w_f1 = din("w_ffn_in", [L, D, 2 * DFF])
    w_f2 = din("w_ffn_out", [L, DFF, D])
    g1row = din("g1row", [L, 128, D])
    g2row = din("g2row", [L, 128, D])
    gfrow = din("gfrow", [128, D])
    rgw = din("rgw", [L, 16, 128, 128])
    rgv = din("rgv", [128, L, 4, 11])
    flag = din("flag", [128, 1])
    eba = din("eba", [4, 128, 25, 128])
    rawc = din("rawc", [L, 12, 128, 7, 128])
    qaa = din("qaa", [64, TOK], BF16)
    qac = din("qac", [64, TOK], BF16)
    kaug = din("kaug", [64, TOK], BF16)
    identd = din("ident", [128, 128])
    yout = nc.dram_tensor("yout", [TOK, D], F32, kind="ExternalOutput").ap()

    sf = dscr("sf", [DIN, TOK], BF16)
    sgx = dscr("sgx", [1024, TOK], F32)
    sv = dscr("sv", [TOK, 1536], BF16)
    smix = dscr("smix", [1536, TOK], BF16)
    sx = dscr("sx", [TOK, D], F32)
    sx1 = dscr("sx1", [TOK, D], F32)
    shn = dscr("shn", [D, TOK], BF16)

    with ExitStack() as es0:
        P = Prog(nc, es0)
        ps = es0.enter_context(nc.psum_tensor("ps", [128, 8, 512], F32))
        ident = es0.enter_context(nc.sbuf_tensor("ident_sb", [128, 128], F32))
        flg = es0.enter_context(nc.sbuf_tensor("flg_sb", [128, 1], F32))
        P.dma('sp', 'ident', lambda e: e.dma_start(out=ident[:], in_=identd), writes=['ident'])
        P.dma('sp', 'flg', lambda e: e.dma_start(out=flg[:], in_=flag), writes=['flg'])
        P.emit_all()

        def PSK(b):
            return ('ps', b)

        def phase1(l, xsrc):
            with ExitStack() as es:
                def sb(name, shape, dt):
                    return es.enter_context(nc.sbuf_tensor("L%d_" % l + name, shape, dt))
                xtok = sb("p1_xtok", [128, 4, D], F32)
                xs = [sb("p1_xs%d" % i, [128, D], F32) for i in range(2)]
                grow = sb("p1_grow", [128, D], F32)
                junk = sb("p1_junk", [128, D], BF16)
                xnT = [sb("p1_xnT%d" % i, [128, 16, 1024], BF16) for i in range(2)]
                wb = [sb("p1_wb%d" % i, [128, 16, 512], BF16) for i in range(3)]
                stb = [sb("p1_stb%d" % i, [128, 512], BF16) for i in range(4)]
                stf = [sb("p1_stf%d" % i, [128, 512], F32) for i in range(2)]
                stat = sb("p1_stat", [128, 4, 4], F32)
                P.dma('sp', 'grow', lambda e: e.dma_start(out=grow[:], in_=g1row[l]), writes=['grow'])
                tb = Rot([0, 1])
                pb = Rot([2, 3, 4, 5, 6, 7])
                wr = Rot([0, 1, 2])
                sbr = Rot([0, 1, 2, 3])
                sfr = Rot([0, 1])
                evr = Rot(['act', 'dve'])
                w_l = w_in[l].rearrange("(k p) c -> p k c", p=128)

                def load_x(i, hf):
                    for s in range(4):
                        r0 = (i * 8 + hf * 4 + s) * 128
                        P.dma('sp', ('xtok', s), lambda e, s=s, r0=r0: e.dma_start(out=xtok[:, s, :], in_=xsrc[r0:r0 + 128, :]),
                              writes=[('xtok', s)])

                def evac(eng, out_ap, in_ap, reads, writes):
                    if eng == 'act':
                        P.op('act', lambda e: e.activation(out=out_ap, in_=in_ap, func=AF.Copy), reads, writes)
                    else:
                        P.op('dve', lambda e: e.tensor_copy(out=out_ap, in_=in_ap), reads, writes)

                def norm_T(i, hf):
                    slot = i % 2
                    for s in range(4):
                        xsl = xs[s % 2]
                        xk = ('xs', s % 2)
                        st = stat[:, s, :]
                        s8 = hf * 4 + s
                        P.op('act', lambda e, s=s, st=st: e.activation(out=junk[:], in_=xtok[:, s, :], func=AF.Square, accum_out=st[:, 0:1]),
                             reads=[('xtok', s)], writes=['junk', ('st', s, 0)])
                        P.op('dve', lambda e, st=st: e.tensor_scalar(out=st[:, 1:2], in0=st[:, 0:1], scalar1=1.0 / D, scalar2=EPS, op0=ALU.mult, op1=ALU.add),
                             reads=[('st', s, 0)], writes=[('st', s, 1)])
                        P.op('act', lambda e, st=st: e.activation(out=st[:, 2:3], in_=st[:, 1:2], func=AF.Sqrt),
                             reads=[('st', s, 1)], writes=[('st', s, 2)])
                        P.op('dve', lambda e, st=st: e.reciprocal(out=st[:, 3:4], in_=st[:, 2:3]),
                             reads=[('st', s, 2)], writes=[('st', s, 3)])
                        P.op('dve', lambda e, s=s, st=st, xsl=xsl: e.scalar_tensor_tensor(out=xsl[:], in0=xtok[:, s, :], scalar=st[:, 3:4], in1=grow[:], op0=ALU.mult, op1=ALU.mult),
                             reads=[('xtok', s), ('st', s, 3), 'grow'], writes=[xk])
                        for kg in range(4):
                            b = tb.next()
                            fns = []
                            for kk in range(4):
                                k = kg * 4 + kk
                                fns.append(lambda e, b=b, kk=kk, k=k, xsl=xsl: e.transpose(out=ps[:, b, kk * 128:(kk + 1) * 128], in_=xsl[:, k * 128:(k + 1) * 128], identity=ident[:]))
                            P.group('pe', fns, reads=[xk, 'ident'], writes=[PSK(b)])
                            out_ap = xnT[slot][:, kg * 4:(kg + 1) * 4, s8 * 128:(s8 + 1) * 128]
                            in_ap = ps[:, b, :].rearrange("p (a c) -> p a c", a=4)
                            evac(evr.next(), out_ap, in_ap, [PSK(b)], [('xnT', slot, s8)])

                def proj(i):
                    slot = i % 2
                    xk = [('xnT', slot, s) for s in range(8)]
                    for g in range(11):
                        if i + 1 < 4 and g == 3:
                            norm_T(i + 1, 0)
                            load_x(i + 1, 1)
                        if i + 1 < 4 and g == 7:
                            norm_T(i + 1, 1)
                            if i + 2 < 4:
                                load_x(i + 2, 0)
                        ws = wr.next()
                        wt = wb[ws]
                        P.dma('pool', ('wb', ws), lambda e, wt=wt, g=g: e.dma_start(out=wt[:], in_=w_l[:, :, g * 512:(g + 1) * 512]),
                              writes=[('wb', ws)])
                        kinds = [chunk_kind(g * 4 + c) for c in range(4)]
                        c = 0
                        while c < 4:
                            kd, off = kinds[c]
                            if kd == 'v':
                                n = 1
                                while c + n < 4 and kinds[c + n][0] == 'v':
                                    n += 1
                                ncol = n * 128
                                for s in range(8):
                                    b = pb.next()
                                    fns = [(lambda e, b=b, k=k, s=s, c=c, ncol=ncol, wt=wt: e.matmul(ps[:, b, 0:ncol], lhsT=xnT[slot][:, k, s * 128:(s + 1) * 128], rhs=wt[:, k, c * 128:c * 128 + ncol], start=(k == 0), stop=(k == 15))) for k in range(16)]
                                    P.group('pe', fns, reads=[xk[s], ('wb', ws)], writes=[PSK(b)])
                                    ss = sbr.next()
                                    evac(evr.next(), stb[ss][:, 0:ncol], ps[:, b, 0:ncol], [PSK(b)], [('stb', ss)])
                                    r0 = (i * 8 + s) * 128
                                    P.dma('sp', ('stb', ss), lambda e, ss=ss, r0=r0, off=off, ncol=ncol: e.dma_start(out=sv[r0:r0 + 128, off:off + ncol], in_=stb[ss][:, 0:ncol]),
                                          reads=[('stb', ss)])
                                c += n
                            else:
                                for hf in range(2):
                                    b = pb.next()
                                    t0_ = (i * 2 + hf) * 512
                                    fns = [(lambda e, b=b, k=k, c=c, wt=wt, hf=hf: e.matmul(ps[:, b, :], lhsT=wt[:, k, c * 128:(c + 1) * 128], rhs=xnT[slot][:, k, hf * 512:(hf + 1) * 512], start=(k == 0), stop=(k == 15))) for k in range(16)]
                                    P.group('pe', fns, reads=xk[hf * 4:hf * 4 + 4] + [('wb', ws)], writes=[PSK(b)])
                                    if kd == 'fb':
                                        ss = sbr.next()
                                        evac(evr.next(), stb[ss][:], ps[:, b, :], [PSK(b)], [('stb', ss)])
                                        P.dma('sp', ('stb', ss), lambda e, ss=ss, off=off, t0_=t0_: e.dma_start(out=sf[off:off + 128, t0_:t0_ + 512], in_=stb[ss][:]),
                                              reads=[('stb', ss)])
                                    else:
                                        ss = sfr.next()
                                        evac(evr.next(), stf[ss][:], ps[:, b, :], [PSK(b)], [('stf', ss)])
                                        P.dma('sp', ('stf', ss), lambda e, ss=ss, off=off, t0_=t0_: e.dma_start(out=sgx[off:off + 128, t0_:t0_ + 512], in_=stf[ss][:]),
                                              reads=[('stf', ss)])
                                c += 1

                load_x(0, 0)
                norm_T(0, 0)
                load_x(0, 1)
                norm_T(0, 1)
                load_x(1, 0)
                for i in range(4):
                    proj(i)
                P.emit_all()

        def c_tiles(R):
            lo, hi = 99, -99
            for kind in ('s', 'p'):
                for r in (2 * R, 2 * R + 1):
                    if kind == 's':
                        rs = min(max(r - 4, 0), 56)
                    else:
                        base = (r // 32) * 32
                        rs = base + min(max(r % 32 - 4, 0), 24)
                    lo = min(lo, rs // 2 - R)
                    hi = max(hi, (rs + 7) // 2 - R)
            return lo, hi

        def phase2(l):
            with ExitStack() as es:
                def sb(name, shape, dt):
                    return es.enter_context(nc.sbuf_tensor("L%d_" % l + name, shape, dt))
                wg = sb("rg_w", [128, 16, 128], BF16)
                vec = sb("rg_vec", [128, 4, 11], F32)
                dv = sb("rg_dv", [128, 4, 12], F32)
                qtr = sb("rg_qtr", [128, 1], F32)
                xbs = [sb("rg_xb%d" % i, [128, 2052], F32) for i in range(2)]
                xc = sb("rg_xc", [128, TOK], F32)
                xcb = sb("rg_xcb", [128, TOK], BF16)
                hf = sb("rg_hf", [128, TOK], F32)
                hbt = [sb("rg_hb%d" % i, [128, 512], F32) for i in range(2)]
                gbt = [sb("rg_gb%d" % i, [128, 512], F32) for i in range(2)]
                T = [sb("rg_t%d" % i, [128, 512], F32) for i in range(6)]
                C = [sb("rg_c%d" % i, [128, 512], F32) for i in range(3)]
                ob = [sb("rg_ob%d" % i, [128, 512], BF16) for i in range(2)]

                def rglru_gen():
                    P.dma('pool', 'rg_w', lambda e: e.dma_start(out=wg[:], in_=rgw[l].rearrange("m p n -> p m n")), writes=['rg_w'])
                    P.dma('sp', 'rg_vec', lambda e: e.dma_start(out=vec[:], in_=rgv[:, l, :, :]), writes=['rg_vec'])
                    P.op('pool', lambda e: e.tensor_scalar(out=dv[:, :, 0:4], in0=vec[:, :, 5:9], scalar1=0.5, scalar2=None, op0=ALU.mult), reads=['rg_vec'], writes=['dv_a'])
                    P.op('act', lambda e: e.activation(out=dv[:, :, 8:10], in_=vec[:, :, 9:11], func=AF.Exp, scale=-1.0), reads=['rg_vec'], writes=['dv_t'])
                    P.op('act', lambda e: e.activation(out=dv[:, :, 10:12], in_=dv[:, :, 8:10], func=AF.Ln, bias=1.0), reads=['dv_t'], writes=['dv_s'])
                    P.op('pool', lambda e: e.tensor_scalar(out=dv[:, :, 4:6], in0=dv[:, :, 10:12], scalar1=-4.0, scalar2=None, op0=ALU.mult), reads=['dv_s'], writes=['dv_c'])
                    P.op('pool', lambda e: e.tensor_scalar(out=dv[:, :, 6:8], in0=dv[:, :, 10:12], scalar1=-8.0, scalar2=None, op0=ALU.mult), reads=['dv_s'], writes=['dv_c2'])
                    P.op('pool', lambda e: e.memset(qtr[:], 0.25), writes=['half'])
                    DVK = ['dv_a', 'dv_c', 'dv_c2', 'rg_vec']
                    TK = lambda n: ('rgT', n)
                    CK = lambda n: ('rgC', n)
                    rgb = Rot([7])
                    obr = Rot([0, 1])
                    yield
                    for c in range(4):
                        for sgi in range(2):
                            P.op('pool', lambda e, sgi=sgi: e.memset(xbs[sgi][:, 0:2], 0.0), writes=[('xb', sgi)])
                            P.op('pool', lambda e, sgi=sgi: e.memset(xbs[sgi][:, 2050:2052], 0.0), writes=[('xb', sgi)])
                            P.dma('sp', ('xb', sgi), lambda e, sgi=sgi, c=c: e.dma_start(out=xbs[sgi][:, 2:2050], in_=sgx[512 + c * 128:512 + (c + 1) * 128, sgi * 2048:(sgi + 1) * 2048]),
                                  writes=[('xb', sgi)])
                        yield
                        P.op('pool', lambda e: e.tensor_scalar(out=xbs[0][:, 2050:2051], in0=xbs[1][:, 2:3], scalar1=flg[:, 0:1], scalar2=None, op0=ALU.mult),
                             reads=[('xb', 1), 'flg'], writes=[('xb', 0)])
                        P.op('pool', lambda e: e.tensor_scalar(out=xbs[1][:, 0:2], in0=xbs[0][:, 2048:2050], scalar1=flg[:, 0:1], scalar2=None, op0=ALU.mult),
                             reads=[('xb', 0), 'flg'], writes=[('xb', 1)])
                        for sgi in range(2):
                            xo = xc[:, sgi * 2048:(sgi + 1) * 2048]
                            P.op('pool', lambda e, sgi=sgi, xo=xo, c=c: e.tensor_scalar(out=xo, in0=xbs[sgi][:, 0:2048], scalar1=vec[:, c, 0:1], scalar2=vec[:, c, 4:5], op0=ALU.mult, op1=ALU.add),
                                 reads=[('xb', sgi), 'rg_vec'], writes=[('xc', sgi)])
                            for j in range(1, 4):
                                P.op('dve', lambda e, sgi=sgi, xo=xo, c=c, j=j: e.scalar_tensor_tensor(out=xo, in0=xbs[sgi][:, j:j + 2048], scalar=vec[:, c, j:j + 1], in1=xo, op0=ALU.mult, op1=ALU.add),
                                     reads=[('xb', sgi), 'rg_vec', ('xc', sgi)], writes=[('xc', sgi)])
                                yield
                            P.op('pool', lambda e, sgi=sgi, xo=xo: e.tensor_copy(out=xcb[:, sgi * 2048:(sgi + 1) * 2048], in_=xo),
                                 reads=[('xc', sgi)], writes=[('xcb', sgi)])
                            yield
                        for d in range(2):
                            for step in range(8):
                                t = step if d == 0 else 7 - step
                                sgi = t // 4
                                cols = slice(t * 512, (t + 1) * 512)
                                gs = step % 2
                                if d == 1:
                                    P.dma('sp', ('gbt', gs), lambda e, gs=gs, c=c, cols=cols: e.dma_start(out=gbt[gs][:], in_=sgx[c * 128:(c + 1) * 128, cols]), writes=[('gbt', gs)])
                                br = rgb.next()
                                bi = rgb.next()
                                P.op('pe', lambda e, br=br, d=d, c=c, cols=cols: e.matmul(ps[:, br, :], lhsT=wg[:, d * 8 + 0 * 4 + c, :], rhs=xcb[:, cols], start=True, stop=True),
                                     reads=['rg_w', ('xcb', sgi)], writes=[PSK(br)])
                                yield
                                P.op('act', lambda e, br=br, d=d, c=c: e.activation(out=T[0][:], in_=ps[:, br, :], func=AF.Tanh, scale=0.5, bias=dv[:, c, d:d + 1]),
                                     reads=[PSK(br)] + DVK, writes=[TK(0)])
                                yield
                                P.op('pe', lambda e, bi=bi, d=d, c=c, cols=cols: e.matmul(ps[:, bi, :], lhsT=wg[:, d * 8 + 1 * 4 + c, :], rhs=xcb[:, cols], start=True, stop=True),
                                     reads=['rg_w', ('xcb', sgi)], writes=[PSK(bi)])
                                yield
                                P.op('act', lambda e, bi=bi, d=d, c=c: e.activation(out=T[1][:], in_=ps[:, bi, :], func=AF.Tanh, scale=0.5, bias=dv[:, c, 2 + d:3 + d]),
                                     reads=[PSK(bi)] + DVK, writes=[TK(1)])
                                P.op('act', lambda e, d=d, c=c: e.activation(out=T[2][:], in_=T[0][:], func=AF.Exp, scale=dv[:, c, 4 + d:5 + d], bias=dv[:, c, 4 + d:5 + d]),
                                     reads=[TK(0)] + DVK, writes=[TK(2)])
                                P.op('act', lambda e, d=d, c=c: e.activation(out=T[3][:], in_=T[0][:], func=AF.Exp, scale=dv[:, c, 6 + d:7 + d], bias=dv[:, c, 6 + d:7 + d]),
                                     reads=[TK(0)] + DVK, writes=[TK(3)])
                                yield
                                P.op('dve', lambda e: e.tensor_scalar(out=T[3][:], in0=T[3][:], scalar1=-1.0, scalar2=-0.99999988, op0=ALU.mult, op1=ALU.max), reads=[TK(3)], writes=[TK(3)])
                                P.op('dve', lambda e, cols=cols: e.scalar_tensor_tensor(out=T[4][:], in0=T[1][:], scalar=1.0, in1=xc[:, cols], op0=ALU.add, op1=ALU.mult),
                                     reads=[TK(1), ('xc', sgi)], writes=[TK(4)])
                                yield
                                P.op('act', lambda e: e.activation(out=T[5][:], in_=T[3][:], func=AF.Sqrt, scale=0.25, bias=qtr[:, 0:1]), reads=[TK(3), 'half'], writes=[TK(5)])
                                P.op('pool', lambda e: e.tensor_tensor(out=T[4][:], in0=T[4][:], in1=T[5][:], op=ALU.mult), reads=[TK(4), TK(5)], writes=[TK(4)])
                                if d == 0 and t == 4:
                                    P.op('pool', lambda e: e.tensor_scalar(out=T[2][:, 0:1], in0=T[2][:, 0:1], scalar1=flg[:, 0:1], scalar2=None, op0=ALU.mult), reads=[TK(2), 'flg'], writes=[TK(2)])
                                if d == 1 and t == 3:
                                    P.op('pool', lambda e: e.tensor_scalar(out=T[2][:, 511:512], in0=T[2][:, 511:512], scalar1=flg[:, 0:1], scalar2=None, op0=ALU.mult), reads=[TK(2), 'flg'], writes=[TK(2)])
                                yield
                                if d == 0:
                                    init = 0.0 if t == 0 else hf[:, t * 512 - 1:t * 512]
                                    P.op('dve', lambda e, cols=cols, init=init: e.tensor_tensor_scan(out=hf[:, cols], data0=T[2][:], data1=T[4][:], initial=init, op0=ALU.mult, op1=ALU.add),
                                         reads=[TK(2), TK(4), 'hf'], writes=['hf'])
                                    yield
                                    continue
                                hs = step % 2
                                init = 0.0 if t == 7 else hbt[1 - hs][:, 0:1]
                                P.op('dve', lambda e, hs=hs, init=init: e.tensor_tensor_scan(out=hbt[hs][:, ::-1], data0=T[2][:, ::-1], data1=T[4][:, ::-1], initial=init, op0=ALU.mult, op1=ALU.add),
                                     reads=[TK(2), TK(4), ('hbt', 1 - hs)], writes=[('hbt', hs)])
                                yield
                                P.op('act', lambda e, gs=gs: e.activation(out=C[0][:], in_=gbt[gs][:], func=AF.Square), reads=[('gbt', gs)], writes=[CK(0)])
                                P.op('pool', lambda e: e.tensor_scalar(out=C[0][:], in0=C[0][:], scalar1=0.044715, scalar2=1.0, op0=ALU.mult, op1=ALU.add), reads=[CK(0)], writes=[CK(0)])
                                P.op('pool', lambda e, gs=gs: e.tensor_tensor(out=C[0][:], in0=C[0][:], in1=gbt[gs][:], op=ALU.mult), reads=[CK(0), ('gbt', gs)], writes=[CK(0)])
                                P.op('act', lambda e: e.activation(out=C[1][:], in_=C[0][:], func=AF.Tanh, scale=0.7978845608028654), reads=[CK(0)], writes=[CK(1)])
                                yield
                                P.op('pool', lambda e, hs=hs, cols=cols: e.tensor_tensor(out=C[2][:], in0=hf[:, cols], in1=hbt[hs][:], op=ALU.add), reads=['hf', ('hbt', hs)], writes=[CK(2)])
                                P.op('pool', lambda e: e.tensor_scalar(out=C[1][:], in0=C[1][:], scalar1=1.0, scalar2=0.5, op0=ALU.add, op1=ALU.mult), reads=[CK(1)], writes=[CK(1)])
                                P.op('pool', lambda e, gs=gs: e.tensor_tensor(out=C[1][:], in0=C[1][:], in1=gbt[gs][:], op=ALU.mult), reads=[CK(1), ('gbt', gs)], writes=[CK(1)])
                                oslot = obr.next()
                                P.op('pool', lambda e, oslot=oslot: e.tensor_tensor(out=ob[oslot][:], in0=C[1][:], in1=C[2][:], op=ALU.mult), reads=[CK(1), CK(2)], writes=[('ob', oslot)])
                                P.dma('sp', ('ob', oslot), lambda e, oslot=oslot, c=c, cols=cols: e.dma_start(out=smix[256 + c * 128:256 + (c + 1) * 128, cols], in_=ob[oslot][:]),
                                      reads=[('ob', oslot)])
                                yield

                NQK = 6
                qk = [sb("at_qk%d" % i, [128, TOK], BF16) for i in range(NQK)]
                vsl = [sb("at_v%d" % i, [128, 32, 65], BF16) for i in range(4)]
                ebA = [sb("at_ebA%d" % i, [128, 25, 128], BF16) for i in range(2)]
                ebraw = [sb("at_ebr%d" % i, [128, 7, 128], F32) for i in range(2)]
                ebC = [sb("at_ebC%d" % i, [128, 7, 128], BF16) for i in range(2)]
                E = [sb("at_E%d" % i, [128, 4, 128], BF16) for i in range(6)]
                PT = [sb("at_PT%d" % i, [128, 4, 128], BF16) for i in range(6)]
                otok = [sb("at_o%d" % i, [128, 64], F32) for i in range(3)]
                rec = [sb("at_r%d" % i, [128, 1], F32) for i in range(3)]
                mst = [sb("at_m%d" % i, [64, 512], BF16) for i in range(3)]
                for i in range(4):
                    P.op('pool', lambda e, i=i: e.memset(vsl[i][:, :, 64:65], 1.0), writes=[('v', i)])
                qkr = Rot(range(NQK))
                vr = Rot(range(4))
                sbank = Rot([0, 1, 2, 6])
                obank = Rot([3, 4])
                tbank = Rot([5])
                er = Rot(range(6))
                pr = Rot(range(6))
                orr = Rot(range(3))
                mr = Rot(range(3))
                ebAr = Rot([0, 1])
                ebCr = Rot([0, 1])
                svt = sv.rearrange("(t p) c -> p t c", p=128)

                def load_entry(qrow, krow, vcol, qaug):
                    qs = qkr.next()
                    ks = qkr.next()
                    vs = vr.next()
                    P.dma('sp', ('qk', qs), lambda e: e.dma_start(out=qk[qs][0:64, :], in_=sf[qrow:qrow + 64, :]), writes=[('qk', qs)])
                    P.dma('sp', ('qk', qs), lambda e: e.dma_start(out=qk[qs][64:128, :], in_=qaug), writes=[('qk', qs)])
                    P.dma('sp', ('qk', ks), lambda e: e.dma_start(out=qk[ks][0:64, :], in_=sf[krow:krow + 64, :]), writes=[('qk', ks)])
                    P.dma('sp', ('qk', ks), lambda e: e.dma_start(out=qk[ks][64:128, :], in_=kaug), writes=[('qk', ks)])
                    P.dma('sp', ('v', vs), lambda e: e.dma_start(out=vsl[vs][:, :, 0:64], in_=svt[:, :, vcol:vcol + 64]), writes=[('v', vs)])
                    return qs, ks, vs

                heads = [('A', j) for j in range(4)] + [('C', h) for h in range(12)]
                loaded = {}
                pending = {}

                def load_head(hd):
                    kind, idx = hd
                    if kind == 'A':
                        ents = []
                        for g in range(3):
                            h = 4 * g + idx
                            ents.append(load_entry(QA0 + h * 64, KA0 + h * 64, h * 64, qaa))
                        es_ = ebAr.next()
                        P.dma('pool', ('ebA', es_), lambda e: e.dma_start(out=ebA[es_][:], in_=eba[idx]), writes=[('ebA', es_)])
                        loaded[hd] = (ents, ebA[es_], ('ebA', es_))
                    else:
                        h = idx
                        es_ = ebCr.next()
                        P.dma('sp', ('ebr', es_), lambda e: e.dma_start(out=ebraw[es_][:], in_=rawc[l, h]), writes=[('ebr', es_)])
                        ents = [load_entry(QC0 + h * 64, KC0 + h * 64, 768 + h * 64, qac)]
                        pending[hd] = lambda: P.op('act', lambda e: e.activation(out=ebC[es_][:], in_=ebraw[es_][:], func=AF.Exp), reads=[('ebr', es_)], writes=[('ebC', es_)])
                        loaded[hd] = (ents, ebC[es_], ('ebC', es_))

                def head_tiles(hd, B):
                    kind, idx = hd
                    res = []
                    if kind == 'A':
                        base = 0
                        for g, rad in enumerate((1, 2, 8)):
                            for dl in range(-rad, rad + 1):
                                kt = B + dl
                                if 0 <= kt < 32:
                                    res.append((base + dl + rad, g, kt))
                            base += 2 * rad + 1
                    else:
                        lo, hi = c_tiles(B)
                        for dl in range(lo, hi + 1):
                            kt = B + dl
                            if 0 <= kt < 32:
                                res.append((dl + 3, 0, kt))
                    return res

                chunks = []
                for hi_, hd in enumerate(heads):
                    kind, idx = hd
                    mixrow = idx * 64 if kind == 'A' else 768 + idx * 64
                    for B in range(32):
                        tl = head_tiles(hd, B)
                        runs = []
                        cur = [tl[0]]
                        for tt in tl[1:]:
                            if tt[0] == cur[-1][0] + 1 and len(cur) < 4:
                                cur.append(tt)
                            else:
                                runs.append(cur)
                                cur = [tt]
                        runs.append(cur)
                        for ri, run in enumerate(runs):
                            chunks.append(dict(hd=hd, hi=hi_, B=B, run=run, first=(ri == 0), last=(ri == len(runs) - 1),
                                               mixrow=mixrow, headstart=(B == 0 and ri == 0)))

                state = {}

                def emit_qk(ch):
                    hd = ch['hd']
                    if ch['headstart']:
                        if hd not in loaded:
                            load_head(hd)
                        if hd in pending:
                            pending.pop(hd)()
                        nxt = ch['hi'] + 1
                        if hd[0] == 'C' and nxt < len(heads) and heads[nxt] not in loaded:
                            load_head(heads[nxt])
                    ents, ebt, ebk = loaded[hd]
                    b = sbank.next()
                    ch['sb'] = b
                    B = ch['B']
                    fns = []
                    rd = set()
                    for ti, (ebi, en, kt) in enumerate(ch['run']):
                        qs, ks, vs = ents[en]
                        rd.add(('qk', qs))
                        rd.add(('qk', ks))
                        fns.append(lambda e, b=b, ti=ti, qs=qs, ks=ks, kt=kt, B=B: e.matmul(ps[:, b, ti * 128:(ti + 1) * 128], lhsT=qk[ks][:, kt * 128:(kt + 1) * 128], rhs=qk[qs][:, B * 128:(B + 1) * 128], start=True, stop=True))
                    P.group('pe', fns, reads=list(rd), writes=[PSK(b)])

                binfo = {}

                def emit_exp(ch):
                    b = ch['sb']
                    n = len(ch['run'])
                    es_ = er.next()
                    ch['es'] = es_
                    P.op('act', lambda e: e.activation(out=E[es_][:, 0:n, :], in_=ps[:, b, 0:n * 128].rearrange("p (a c) -> p a c", a=n), func=AF.Exp, scale=0.125, bias=-8.0),
                         reads=[PSK(b)], writes=[('E', es_)])

                def emit_mul(ch):
                    ents, ebt, ebk = loaded[ch['hd']]
                    n = len(ch['run'])
                    eb0 = ch['run'][0][0]
                    es_ = ch['es']
                    ps_ = pr.next()
                    ch['pt'] = ps_
                    state['mulc'] = state.get('mulc', 0) + 1
                    meng = 'dve'
                    P.op(meng, lambda e: e.tensor_tensor(out=PT[ps_][:, 0:n, :], in0=E[es_][:, 0:n, :], in1=ebt[:, eb0:eb0 + n, :], op=ALU.mult),
                         reads=[('E', es_), ebk], writes=[('PT', ps_)])

                def emit_pv(ch):
                    ents, ebt, ebk = loaded[ch['hd']]
                    n = len(ch['run'])
                    ps_ = ch['pt']
                    key = (ch['hi'], ch['B'])
                    if ch['first']:
                        binfo[key] = {'ob': obank.next()}
                    ob_ = binfo[key]['ob']
                    fns = []
                    rd = {('PT', ps_)}
                    for ti, (ebi, en, kt) in enumerate(ch['run']):
                        qs, ks, vs = ents[en]
                        rd.add(('v', vs))
                        fns.append(lambda e, ti=ti, vs=vs, kt=kt, st=(ch['first'] and ti == 0), sp=(ch['last'] and ti == n - 1): e.matmul(ps[:, ob_, 0:65], lhsT=PT[ps_][:, ti, :], rhs=vsl[vs][:, kt, :], start=st, stop=sp))
                    P.group('pe', fns, reads=list(rd), writes=[PSK(ob_)])

                def emit_fin(ch):
                    bi_ = binfo[(ch['hi'], ch['B'])]
                    ob_ = bi_['ob']
                    os_ = orr.next()
                    bi_['os'] = os_
                    P.op('dve', lambda e: e.reciprocal(out=rec[os_][:], in_=ps[:, ob_, 64:65]), reads=[PSK(ob_)], writes=[('rec', os_)])
                    P.op('dve', lambda e: e.tensor_scalar(out=otok[os_][:], in0=ps[:, ob_, 0:64], scalar1=rec[os_][:, 0:1], scalar2=None, op0=ALU.mult),
                         reads=[PSK(ob_), ('rec', os_)], writes=[('otok', os_)])

                def emit_tr(ch):
                    bi_ = binfo[(ch['hi'], ch['B'])]
                    os_ = bi_['os']
                    B = ch['B']
                    if B % 4 == 0:
                        state['tb'] = tbank.next()
                    tb_ = state['tb']
                    bi_['tb'] = tb_
                    P.op('pe', lambda e: e.transpose(out=ps[0:64, tb_, (B % 4) * 128:(B % 4 + 1) * 128], in_=otok[os_][:], identity=ident[:]),
                         reads=[('otok', os_), 'ident'], writes=[PSK(tb_)])

                def emit_ev(ch):
                    bi_ = binfo[(ch['hi'], ch['B'])]
                    tb_ = bi_['tb']
                    B = ch['B']
                    ms_ = mr.next()
                    mixrow = ch['mixrow']
                    P.op('act', lambda e: e.activation(out=mst[ms_][:], in_=ps[0:64, tb_, :], func=AF.Copy), reads=[PSK(tb_)], writes=[('mst', ms_)])
                    P.dma('sp', ('mst', ms_), lambda e: e.dma_start(out=smix[mixrow:mixrow + 64, (B - 3) * 128:(B + 1) * 128], in_=mst[ms_][:]),
                          reads=[('mst', ms_)])

                rg = rglru_gen()
                next(rg)
                KRG = 3
                cnt = 0
                for hi_ in range(len(heads)):
                    hc = [c_ for c_ in chunks if c_['hi'] == hi_]
                    n_ = len(hc)
                    for step in range(n_ + 8):
                        if 0 <= step - 7 < n_ and hc[step - 7]['last'] and hc[step - 7]['B'] % 4 == 3:
                            emit_ev(hc[step - 7])
                        if 0 <= step - 6 < n_ and hc[step - 6]['last']:
                            emit_tr(hc[step - 6])
                        if 0 <= step - 5 < n_ and hc[step - 5]['last']:
                            emit_fin(hc[step - 5])
                        if 0 <= step - 3 < n_:
                            emit_pv(hc[step - 3])
                        if 0 <= step - 2 < n_:
                            emit_mul(hc[step - 2])
                        if 0 <= step - 1 < n_:
                            emit_exp(hc[step - 1])
                        if step < n_:
                            emit_qk(hc[step])
                        cnt += 1
                        if cnt % KRG == 0:
                            next(rg, None)
                for _ in rg:
                    pass
                P.emit_all()

        def phase3a(l, xsrc):
            with ExitStack() as es:
                def sb(name, shape, dt):
                    return es.enter_context(nc.sbuf_tensor("L%d_" % l + name, shape, dt))
                xtoks = [sb("p3_xtok%d" % i, [128, 4, D], F32) for i in range(2)]
                mixTs = [sb("p3_mixT%d" % i, [128, 12, 512], BF16) for i in range(2)]
                xss = [sb("p3_xs%d" % i, [128, D], F32) for i in range(2)]
                grow = sb("p3_grow", [128, D], F32)
                hst = [sb("p3_hst%d" % i, [128, 16, 512], BF16) for i in range(2)]
                wres = sb("p3_wo", [128, 12, D], BF16)
                stat = sb("p3_stat", [128, 4, 4], F32)
                P.dma('sp', 'grow', lambda e: e.dma_start(out=grow[:], in_=g2row[l]), writes=['grow'])
                pb = Rot([0, 1, 2, 3, 4, 5])
                tb = Rot([6, 7])
                evr = Rot(['act', 'dve'])
                wo_l = w_out[l].rearrange("(k p) c -> p k c", p=128)
                smx = smix.rearrange("(c p) t -> p c t", p=128)
                shn_v = shn.rearrange("(k p) t -> p k t", p=128)

                def load_in(i):
                    sl = i % 2
                    for s in range(4):
                        r0 = (i * 4 + s) * 128
                        P.dma('sp', ('xtok', sl, s), lambda e, s=s, r0=r0, sl=sl: e.dma_start(out=xtoks[sl][:, s, :], in_=xsrc[r0:r0 + 128, :]), writes=[('xtok', sl, s)])
                    P.dma('sp', ('mixT', sl), lambda e, i=i, sl=sl: e.dma_start(out=mixTs[sl][:], in_=smx[:, :, i * 512:(i + 1) * 512]), writes=[('mixT', sl)])

                def load_x(i, s):
                    sl = i % 2
                    r0 = (i * 4 + s) * 128
                    P.dma('sp', ('xtok', sl, s), lambda e, s=s, r0=r0, sl=sl: e.dma_start(out=xtoks[sl][:, s, :], in_=xsrc[r0:r0 + 128, :]), writes=[('xtok', sl, s)])

                def load_mix(i):
                    sl = i % 2
                    P.dma('sp', ('mixT', sl), lambda e, i=i, sl=sl: e.dma_start(out=mixTs[sl][:], in_=smx[:, :, i * 512:(i + 1) * 512]), writes=[('mixT', sl)])

                def wout(i, n):
                    sl = i % 2
                    xtok = xtoks[sl]
                    mixT = mixTs[sl]
                    for s in range(4):
                        b = pb.next()
                        fns = [(lambda e, b=b, k=k, s=s, n=n: e.matmul(ps[:, b, :], lhsT=mixT[:, k, s * 128:(s + 1) * 128], rhs=wres[:, k, n * 512:(n + 1) * 512], start=(k == 0), stop=(k == 11))) for k in range(12)]
                        P.group('pe', fns, reads=[('mixT', sl), 'wres'], writes=[PSK(b)])
                        P.op('dve', lambda e, b=b, s=s, n=n: e.tensor_tensor(out=xtok[:, s, n * 512:(n + 1) * 512], in0=ps[:, b, :], in1=xtok[:, s, n * 512:(n + 1) * 512], op=ALU.add),
                             reads=[PSK(b), ('xtok', sl, s)], writes=[('xtok', sl, s)])

                def chain(i, s):
                    sl = i % 2
                    xtok = xtoks[sl]
                    xsl = xss[s % 2]
                    xsk = ('xs', s % 2)
                    r0 = (i * 4 + s) * 128
                    xk = ('xtok', sl, s)
                    P.dma('sp', ('x1o', sl, s), lambda e, s=s, r0=r0: e.dma_start(out=sx1[r0:r0 + 128, :], in_=xtok[:, s, :]), reads=[xk])
                    st = stat[:, s, :]
                    P.op('act', lambda e, s=s, st=st: e.activation(out=xsl[:], in_=xtok[:, s, :], func=AF.Square, accum_out=st[:, 0:1]),
                         reads=[xk], writes=[xsk, ('st', s, 0)])
                    P.op('dve', lambda e, st=st: e.tensor_scalar(out=st[:, 1:2], in0=st[:, 0:1], scalar1=1.0 / D, scalar2=EPS, op0=ALU.mult, op1=ALU.add),
                         reads=[('st', s, 0)], writes=[('st', s, 1)])
                    P.op('act', lambda e, st=st: e.activation(out=st[:, 2:3], in_=st[:, 1:2], func=AF.Sqrt), reads=[('st', s, 1)], writes=[('st', s, 2)])
                    P.op('dve', lambda e, st=st: e.reciprocal(out=st[:, 3:4], in_=st[:, 2:3]), reads=[('st', s, 2)], writes=[('st', s, 3)])
                    P.op('dve', lambda e, s=s, st=st: e.scalar_tensor_tensor(out=xsl[:], in0=xtok[:, s, :], scalar=st[:, 3:4], in1=grow[:], op0=ALU.mult, op1=ALU.mult),
                         reads=[xk, ('st', s, 3), 'grow'], writes=[xsk])

                def transp(i, s):
                    sl = i % 2
                    xsl = xss[s % 2]
                    xsk = ('xs', s % 2)
                    for kg in range(4):
                        b = tb.next()
                        fns = [(lambda e, b=b, kk=kk, kg=kg: e.transpose(out=ps[:, b, kk * 128:(kk + 1) * 128], in_=xsl[:, (kg * 4 + kk) * 128:(kg * 4 + kk + 1) * 128], identity=ident[:])) for kk in range(4)]
                        P.group('pe', fns, reads=[xsk, 'ident'], writes=[PSK(b)])
                        out_ap = hst[sl][:, kg * 4:(kg + 1) * 4, s * 128:(s + 1) * 128]
                        in_ap = ps[:, b, :].rearrange("p (a c) -> p a c", a=4)
                        if evr.next() == 'act':
                            P.op('act', lambda e, out_ap=out_ap, in_ap=in_ap: e.activation(out=out_ap, in_=in_ap, func=AF.Copy), reads=[PSK(b)], writes=[('hst', sl)])
                        else:
                            P.op('dve', lambda e, out_ap=out_ap, in_ap=in_ap: e.tensor_copy(out=out_ap, in_=in_ap), reads=[PSK(b)], writes=[('hst', sl)])

                for n in range(4):
                    P.dma('pool', 'wres', lambda e, n=n: e.dma_start(out=wres[:, :, n * 512:(n + 1) * 512], in_=wo_l[:, :, n * 512:(n + 1) * 512]), writes=['wres'])
                for i0 in range(2):
                    for s in range(4):
                        load_x(i0, s)
                    load_mix(i0)
                for n in range(4):
                    wout(0, n)
                for i in range(NT):
                    for s in range(4):
                        chain(i, s)
                        if i + 2 < NT:
                            load_x(i + 2, s)
                            if s == 0:
                                load_mix(i + 2)
                        if i + 1 < NT:
                            wout(i + 1, s)
                        transp(i, s)
                    P.dma('sp', ('hst', i % 2), lambda e, i=i: e.dma_start(out=shn_v[:, :, i * 512:(i + 1) * 512], in_=hst[i % 2][:]), reads=[('hst', i % 2)])
                P.emit_all()

        def phase3b(l):
            with ExitStack() as es:
                def sb(name, shape, dt):
                    return es.enter_context(nc.sbuf_tensor("L%d_" % l + name, shape, dt))
                hnT = sb("f_hnT", [128, 16, 1024], BF16)
                hT = sb("f_hT", [128, 44, 1024], BF16)
                WR = 3
                wring = [sb("f_w%d" % i, [128, 5632], BF16) for i in range(WR)]
                sg = [sb("f_sg%d" % i, [128, 512], F32) for i in range(2)]
                yTs = [sb("f_yT%d" % i, [128, 4, 1024], F32) for i in range(2)]
                xp = [sb("f_xp%d" % i, [128, 512], F32) for i in range(4)]
                wr = Rot(range(WR))
                pb = Rot([0, 1, 2, 3, 4, 5, 6, 7])
                pbo = Rot([0, 1, 2, 3, 4, 5])
                tb = Rot([6, 7])
                sgr = Rot([0, 1])
                xpr = Rot(range(4))
                w1_l = w_f1[l].rearrange("(k p) c -> p k c", p=128)
                w2_l = w_f2[l].rearrange("(f p) c -> p f c", p=128)
                shn_v = shn.rearrange("(k p) t -> p k t", p=128)
                def load_hn(j):
                    P.dma('sp', 'hnT', lambda e, j=j: e.dma_start(out=hnT[:], in_=shn_v[:, :, j * 1024:(j + 1) * 1024]), writes=['hnT'])
                load_hn(0)
                ptail = [None]
                for j in range(4):
                    for f in range(44):
                        if ptail[0] is not None and f >= 2 and f % 2 == 0:
                            if next(ptail[0], 'done') == 'done':
                                ptail[0] = None
                        ws = wr.next()
                        wt = wring[ws][:, 0:4096].rearrange("p (k g c) -> p k g c", k=16, g=2)
                        P.dma('pool', ('w', ws), lambda e, wt=wt, f=f: e.dma_start(out=wt[:, :, 0, :], in_=w1_l[:, :, f * 128:(f + 1) * 128]), writes=[('w', ws)])
                        P.dma('pool', ('w', ws), lambda e, wt=wt, f=f: e.dma_start(out=wt[:, :, 1, :], in_=w1_l[:, :, DFF + f * 128:DFF + (f + 1) * 128]), writes=[('w', ws)])
                        for hf in range(2):
                            bg = pb.next()
                            bu = pb.next()
                            fns = [(lambda e, bg=bg, k=k, wt=wt, hf=hf: e.matmul(ps[:, bg, :], lhsT=wt[:, k, 0, :], rhs=hnT[:, k, hf * 512:(hf + 1) * 512], start=(k == 0), stop=(k == 15))) for k in range(16)]
                            P.group('pe', fns, reads=['hnT', ('w', ws)], writes=[PSK(bg)])
                            fns = [(lambda e, bu=bu, k=k, wt=wt, hf=hf: e.matmul(ps[:, bu, :], lhsT=wt[:, k, 1, :], rhs=hnT[:, k, hf * 512:(hf + 1) * 512], start=(k == 0), stop=(k == 15))) for k in range(16)]
                            P.group('pe', fns, reads=['hnT', ('w', ws)], writes=[PSK(bu)])
                            sgs = sgr.next()
                            P.op('act', lambda e, bg=bg, sgs=sgs: e.activation(out=sg[sgs][:], in_=ps[:, bg, :], func=AF.Silu), reads=[PSK(bg)], writes=[('sg', sgs)])
                            P.op('dve', lambda e, bu=bu, sgs=sgs, f=f, hf=hf: e.tensor_tensor(out=hT[:, f, hf * 512:(hf + 1) * 512], in0=sg[sgs][:], in1=ps[:, bu, :], op=ALU.mult),
                                 reads=[PSK(bu), ('sg', sgs)], writes=[('hT', f)])
                    HTK = [('hT', f) for f in range(44)]
                    if j + 1 < 4:
                        load_hn(j + 1)

                    def ffn_out_c(cg, cc):
                        c = cg * 4 + cc
                        yT = yTs[cg % 2]
                        ws = wr.next()
                        wt = wring[ws][:, 0:5632].rearrange("p (f c) -> p f c", f=44)
                        P.dma('pool', ('w', ws), lambda e, wt=wt, c=c: e.dma_start(out=wt, in_=w2_l[:, :, c * 128:(c + 1) * 128]), writes=[('w', ws)])
                        for hf in range(2):
                            b = pbo.next()
                            fns = [(lambda e, b=b, f=f, wt=wt, hf=hf: e.matmul(ps[:, b, :], lhsT=wt[:, f, :], rhs=hT[:, f, hf * 512:(hf + 1) * 512], start=(f == 0), stop=(f == 43))) for f in range(44)]
                            P.group('pe', fns, reads=HTK + [('w', ws)], writes=[PSK(b)])
                            P.op('act', lambda e, b=b, cc=cc, hf=hf, yT=yT: e.activation(out=yT[:, cc, hf * 512:(hf + 1) * 512], in_=ps[:, b, :], func=AF.Copy), reads=[PSK(b)], writes=[('yT', cg % 2, cc, hf)])

                    def tail(cg, j=j):
                        yT = yTs[cg % 2]

                        def xload(s):
                            r0 = (j * 8 + s) * 128
                            xs_ = s % 4
                            P.dma('sp', ('xp', xs_), lambda e, xs_=xs_, r0=r0, cg=cg: e.dma_start(out=xp[xs_][:], in_=sx1[r0:r0 + 128, cg * 512:(cg + 1) * 512]), writes=[('xp', xs_)])
                        for s in range(3):
                            xload(s)
                        for s in range(8):
                            r0 = (j * 8 + s) * 128
                            xs_ = s % 4
                            if s + 3 < 8:
                                xload(s + 3)
                            b = tb.next()
                            fns = [(lambda e, b=b, cc=cc, s=s, yT=yT: e.transpose(out=ps[:, b, cc * 128:(cc + 1) * 128], in_=yT[:, cc, s * 128:(s + 1) * 128], identity=ident[:])) for cc in range(4)]
                            P.group('pe', fns, reads=[('yT', cg % 2, cc, s // 4) for cc in range(4)] + ['ident'], writes=[PSK(b)])
                            P.op('dve', lambda e, b=b, xs_=xs_: e.tensor_tensor(out=xp[xs_][:], in0=ps[:, b, :], in1=xp[xs_][:], op=ALU.add),
                                 reads=[PSK(b), ('xp', xs_)], writes=[('xp', xs_)])
                            P.dma('sp', ('xp', xs_), lambda e, xs_=xs_, r0=r0, cg=cg: e.dma_start(out=sx[r0:r0 + 128, cg * 512:(cg + 1) * 512], in_=xp[xs_][:]), reads=[('xp', xs_)])
                            if s % 2 == 1:
                                yield

                    tg = None
                    for cg in range(4):
                        for cc in range(4):
                            ffn_out_c(cg, cc)
                            if cc == 0 and cg > 0:
                                tg = tail(cg - 1)
                            if tg is not None:
                                next(tg, None)
                    if j + 1 < 4:
                        ptail[0] = tail(3)
                    else:
                        for _ in tail(3):
                            pass
                P.emit_all()

        def phase3c():
            with ExitStack() as es:
                def sb(name, shape, dt):
                    return es.enter_context(nc.sbuf_tensor("fin_" + name, shape, dt))
                xb_ = [sb("x%d" % i, [128, D], F32) for i in range(8)]
                gf = sb("gf", [128, D], F32)
                junk = sb("junk", [128, D], BF16)
                stat = sb("stat", [128, 8, 4], F32)
                P.dma('sp', 'gf', lambda e: e.dma_start(out=gf[:], in_=gfrow), writes=['gf'])
                def fload(t):
                    sl = t % 8
                    r0 = t * 128
                    P.dma('sp', ('fx', sl), lambda e, sl=sl, r0=r0: e.dma_start(out=xb_[sl][:], in_=sx[r0:r0 + 128, :]), writes=[('fx', sl)])
                for t in range(6):
                    fload(t)
                for t in range(32):
                    sl = t % 8
                    r0 = t * 128
                    xk = ('fx', sl)
                    st = stat[:, sl, :]
                    if t + 6 < 32:
                        fload(t + 6)
                    P.op('act', lambda e, sl=sl, st=st: e.activation(out=junk[:], in_=xb_[sl][:], func=AF.Square, accum_out=st[:, 0:1]), reads=[xk], writes=['junk', ('st', sl, 0)])
                    P.op('dve', lambda e, st=st: e.tensor_scalar(out=st[:, 1:2], in0=st[:, 0:1], scalar1=1.0 / D, scalar2=EPS, op0=ALU.mult, op1=ALU.add), reads=[('st', sl, 0)], writes=[('st', sl, 1)])
                    P.op('act', lambda e, st=st: e.activation(out=st[:, 2:3], in_=st[:, 1:2], func=AF.Sqrt), reads=[('st', sl, 1)], writes=[('st', sl, 2)])
                    P.op('dve', lambda e, st=st: e.reciprocal(out=st[:, 3:4], in_=st[:, 2:3]), reads=[('st', sl, 2)], writes=[('st', sl, 3)])
                    P.op('dve', lambda e, sl=sl, st=st: e.scalar_tensor_tensor(out=xb_[sl][:], in0=xb_[sl][:], scalar=st[:, 3:4], in1=gf[:], op0=ALU.mult, op1=ALU.mult),
                         reads=[xk, ('st', sl, 3), 'gf'], writes=[xk])
                    P.dma('sp', xk, lambda e, sl=sl, r0=r0: e.dma_start(out=yout[r0:r0 + 128, :], in_=xb_[sl][:]), reads=[xk])
                P.emit_all()

        for l in range(NLAYERS):
            xsrc = xin if l == 0 else sx
            phase1(l, xsrc)
            phase2(l)
            phase3a(l, xsrc)
            phase3b(l)
        phase3c()
    return nc


def _bf16(a):
    return np.asarray(a, dtype=np.float32).astype(ml_dtypes.bfloat16)


def _const_tables():
    slopes = 2.0 ** (-8.0 * np.arange(1, 13, dtype=np.float64) / 12.0)
    p = np.arange(128)[:, None]
    q = np.arange(128)[None, :]
    eba = np.zeros((4, 128, 25, 128), np.float32)
    for j in range(4):
        base = 0
        for g, (d, rad) in enumerate(((1, 1), (4, 2), (16, 8))):
            h = 4 * g + j
            for dl in range(-rad, rad + 1):
                delta = 128 * dl + p - q
                ok = (delta % d == 0) & (np.abs(delta) <= 64 * d)
                val = np.exp(-slopes[h] * np.abs(delta))
                eba[j, :, base + dl + rad, :] = np.where(ok, val, 0.0)
            base += 2 * rad + 1
    rows = np.arange(TOK) // 64
    kaug = (rows[None, :] == np.arange(64)[:, None]).astype(np.float32)
    return eba, kaug


def _q_aug(is_sample):
    a = np.arange(64)[:, None]
    r = (np.arange(TOK) // 64)[None, :]
    if is_sample:
        rs = np.clip(r - 4, 0, 56)
        qaa = np.zeros((64, TOK), np.float32)
    else:
        base = (r // 32) * 32
        rs = base + np.clip(r % 32 - 4, 0, 24)
        qaa = np.where((a // 32) == (r // 32), 0.0, NEG).astype(np.float32)
    qac = np.where((a >= rs) & (a < rs + 8), 0.0, NEG).astype(np.float32)
    return qaa, qac


def _rawc(na_rpb):
    p = np.arange(128)
    kr2, kc = (p // 64)[:, None], (p % 64)[:, None]
    qr2, qc = (p // 64)[None, :], (p % 64)[None, :]
    cs = np.clip(qc - 8, 0, 48)
    colok = (kc >= cs) & (kc < cs + 16)
    dc = np.clip(kc - qc + 15, 0, 30)
    out = np.full((L, 12, 128, 7, 128), NEG, np.float32)
    for dl in range(-3, 4):
        dr = 2 * dl + kr2 - qr2 + 7
        ok = colok & (dr >= 0) & (dr < 15)
        drc = np.clip(dr, 0, 14)
        g = na_rpb[:, :, drc, dc]
        out[:, :, :, dl + 3, :] = np.where(ok[None, None], g, NEG)
    return out


_NC_CACHE = {}


def kernel(x_prompt, x_sample, norm1_g, w_in, conv_w, conv_b, rg_wa, rg_ba, rg_wx, rg_bx,
           rg_lam, na_rpb, w_out, norm2_g, w_ffn_in, w_ffn_out, final_g):
    f32 = lambda a: np.ascontiguousarray(np.asarray(a, dtype=np.float32))
    x_prompt, x_sample = f32(x_prompt), f32(x_sample)
    slots = [(x_sample[0], True), (x_sample[1], True)]
    for i in range(4):
        slots.append((x_prompt[2 * i:2 * i + 2].reshape(TOK, D), False))
    slots.append(slots[-1])
    slots.append(slots[-1])

    eba, kaug = _const_tables()
    rawc = _rawc(f32(na_rpb))
    qa = {True: _q_aug(True), False: _q_aug(False)}
    rgw = np.zeros((L, 2, 2, 4, 128, 128), np.float32)
    for kind, w in enumerate((f32(rg_wa), f32(rg_wx))):
        for c in range(4):
            for half in range(2):
                rgw[:, :, kind, c, half * 64:(half + 1) * 64, half * 64:(half + 1) * 64] = w[:, :, 2 * c + half]
    rgw = rgw.reshape(L, 16, 128, 128)
    rgv = np.zeros((128, L, 4, 11), np.float32)

    def chan(v):
        return v.reshape(L, 4, 128).transpose(2, 0, 1)
    cw = f32(conv_w)
    for j in range(4):
        rgv[:, :, :, j] = chan(cw[:, j])
    rgv[:, :, :, 4] = chan(f32(conv_b))
    for d in range(2):
        rgv[:, :, :, 5 + d] = chan(f32(rg_ba)[:, d])
        rgv[:, :, :, 7 + d] = chan(f32(rg_bx)[:, d])
        rgv[:, :, :, 9 + d] = chan(f32(rg_lam)[:, d])
    bc = lambda v: np.ascontiguousarray(np.broadcast_to(f32(v)[..., None, :], v.shape[:-1] + (128, D)))
    common = {
        "w_in": f32(w_in), "w_out": f32(w_out), "w_ffn_in": f32(w_ffn_in), "w_ffn_out": f32(w_ffn_out),
        "g1row": bc(norm1_g), "g2row": bc(norm2_g), "gfrow": bc(final_g),
        "rgw": rgw, "rgv": rgv, "eba": eba, "rawc": rawc, "kaug": _bf16(kaug),
        "ident": np.eye(128, dtype=np.float32),
    }
    in_maps = []
    for xs, is_s in slots:
        m = dict(common)
        m["xin"] = np.ascontiguousarray(xs)
        m["flag"] = np.full((128, 1), 1.0 if is_s else 0.0, np.float32)
        m["qaa"] = _bf16(qa[is_s][0])
        m["qac"] = _bf16(qa[is_s][1])
        in_maps.append(m)
    if "nc" not in _NC_CACHE:
        _NC_CACHE["nc"] = build_program()
    nc = _NC_CACHE["nc"]
    res = run_bass_kernel_spmd(nc, in_maps, core_ids=list(range(8)))
    outs = [np.asarray(r["yout"], dtype=np.float32) for r in res.results]
    if DEBUG:
        kernel.debug = res.results
    y_sample = np.stack([outs[0], outs[1]], axis=0)
    y_prompt = np.concatenate([outs[2 + i].reshape(2, 2048, D) for i in range(4)], axis=0)
    return (y_prompt, y_sample)
```
